# Optimizing a Trainium2 kernel written in Bass

```python
import math
import jax, jax.numpy as jnp
from jax import lax
import numpy as np

D_MODEL = 1024
BATCH = 8
SEQ = 8192
DEPTH = 2

GRID_W = 64
CTX_LEN = 256
BRANCH_W = D_MODEL
GLA_HEADS = 4
GLA_DK = D_MODEL // (2 * GLA_HEADS)
GLA_DV = BRANCH_W // GLA_HEADS
GLA_RANK = 16
GLA_TAU = 16.0
GLA_CHUNK = 64
DIFF_HEADS = 8
DIFF_DH = BRANCH_W // (2 * DIFF_HEADS)
MLA_HEADS = 8
MLA_Q_RANK = 256
MLA_KV_RANK = 128
MLA_NOPE = 128
MLA_ROPE = 64
MLA_DV = BRANCH_W // MLA_HEADS
MLA_SCALE = (MLA_NOPE + MLA_ROPE) ** -0.5
ROPE_DIM = 64
ROPE_BASE = 10000.0
Q_BLOCK = 128
FFN_HIDDEN = ((8 * D_MODEL // 3 + 255) // 256) * 256
DEEPNORM_ALPHA = (2 * DEPTH) ** 0.25
DEEPNORM_BETA = (8 * DEPTH) ** -0.25
EPS = 1e-6
IN_SIZES = (GLA_HEADS * GLA_DK, GLA_HEADS * GLA_DK, BRANCH_W, BRANCH_W, 2 * GLA_RANK,
            2 * DIFF_HEADS * DIFF_DH, 2 * DIFF_HEADS * DIFF_DH, BRANCH_W,
            MLA_Q_RANK, MLA_KV_RANK, MLA_ROPE, 3 * D_MODEL)
N_IN = sum(IN_SIZES)

kernel_name = "hybrid_gla_diff_mla_dit_block"


def layer_norm(x, g=None, b=None):
    xf = x.astype(jnp.float32)
    mu = jnp.mean(xf, axis=-1, keepdims=True)
    var = jnp.mean(jnp.square(xf - mu), axis=-1, keepdims=True)
    y = (xf - mu) * lax.rsqrt(var + EPS)
    if g is not None:
        y = y * g + b
    return y.astype(x.dtype)


def rms_norm(x, g):
    xf = x.astype(jnp.float32)
    y = xf * lax.rsqrt(jnp.mean(jnp.square(xf), axis=-1, keepdims=True) + EPS) * g
    return y.astype(x.dtype)


def modulate(x, shift, scale):
    return layer_norm(x) * (1.0 + scale) + shift


def axial_rope_tables(L, dtype):
    rows = L // GRID_W
    pos_row = jnp.broadcast_to(jnp.arange(rows, dtype=jnp.float32)[:, None], (rows, GRID_W)).reshape(L)
    pos_col = jnp.broadcast_to(jnp.arange(GRID_W, dtype=jnp.float32)[None, :], (rows, GRID_W)).reshape(L)
    d_axis = ROPE_DIM // 2
    inv = ROPE_BASE ** (-jnp.arange(0, d_axis, 2, dtype=jnp.float32) / d_axis)
    ang = jnp.concatenate([pos_row[:, None] * inv, pos_col[:, None] * inv], axis=-1)
    return jnp.cos(ang).astype(dtype), jnp.sin(ang).astype(dtype)


def apply_rope(x, cos, sin):
    half = ROPE_DIM // 2
    x1, x2 = x[..., :half], x[..., half:]
    return jnp.concatenate([x1 * cos - x2 * sin, x1 * sin + x2 * cos], axis=-1)


def split_columns(p):
    offsets = []
    acc = 0
    for size in IN_SIZES[:-1]:
        acc += size
        offsets.append(acc)
    return jnp.split(p, offsets, axis=-1)


def gla_chunked(q, k, v, log_a, s0, inclusive):
    B, H, L, dk = q.shape
    dv = v.shape[-1]
    n = L // GLA_CHUNK

    def chunks(t):
        return jnp.moveaxis(t.reshape(B, H, n, GLA_CHUNK, t.shape[-1]), 2, 0)

    mask = jnp.tril(jnp.ones((GLA_CHUNK, GLA_CHUNK), dtype=bool), k=0 if inclusive else -1)

    def step(s, inp):
        qc, kc, vc, gc = inp
        b = jnp.cumsum(gc, axis=-2)
        b_last = b[:, :, -1:, :]
        o_inter = jnp.einsum('bhcd,bhde->bhce', qc * jnp.exp(b), s)
        rel = jnp.where(mask[:, :, None], b[:, :, :, None, :] - b[:, :, None, :, :], -jnp.inf)
        a = jnp.einsum('bhid,bhjd,bhijd->bhij', qc, kc, jnp.exp(rel))
        o_intra = jnp.einsum('bhij,bhje->bhie', a, vc)
        s_new = jnp.exp(b_last[:, :, 0, :])[..., None] * s + jnp.einsum(
            'bhcd,bhce->bhde', kc * jnp.exp(b_last - b), vc)
        return s_new, o_inter + o_intra

    s_fin, o = lax.scan(step, s0, (chunks(q), chunks(k), chunks(v), chunks(log_a)))
    o = jnp.moveaxis(o, 0, 2).reshape(B, H, L, dv)
    return o.astype(v.dtype), s_fin


def gla_bidir(q, k, v, la_f, la_b, s0_f, s0_b):
    o_f, s_f = gla_chunked(q, k, v, la_f, s0_f, True)
    flip = lambda t: jnp.flip(t, axis=2)
    o_b, s_b = gla_chunked(flip(q), flip(k), flip(v), flip(la_b), s0_b, False)
    return o_f + flip(o_b), s_f, s_b


def diff_core(q, k, v, lam):
    s = jnp.einsum('bhmqd,bhmkd->bhmqk', q, k)
    p = jax.nn.softmax(s.astype(jnp.float32), axis=-1)
    w = p[:, :, 0] - lam * p[:, :, 1]
    return jnp.einsum('bhqk,bhkd->bhqd', w.astype(v.dtype), v)


def mla_core(q_nope, q_rope, k_nope, k_rope, v):
    s = jnp.einsum('bhqd,bhkd->bhqk', q_nope, k_nope) + jnp.einsum('bhqd,bkd->bhqk', q_rope, k_rope)
    p = jax.nn.softmax(s.astype(jnp.float32), axis=-1)
    return jnp.einsum('bhqk,bhkd->bhqd', p.astype(v.dtype), v)


def to_blocks(t):
    nb = t.shape[-2] // Q_BLOCK
    t = t.reshape(t.shape[:-2] + (nb, Q_BLOCK, t.shape[-1]))
    return jnp.moveaxis(t, -3, 0)


def from_blocks(o):
    o = jnp.moveaxis(o, 0, -3)
    return o.reshape(o.shape[:-3] + (-1, o.shape[-1]))


def sweep_query_blocks(core, qs):
    return from_blocks(lax.map(lambda qb: core(*qb), tuple(to_blocks(q) for q in qs)))


def merge_heads(o):
    B, H, L, d = o.shape
    return o.transpose(0, 2, 1, 3).reshape(B, L, H * d)


def mixer_features(h, rope, w_in, gla_w_a2, gla_b_a, mla_q_norm_g, mla_kv_norm_g, mla_w_uq, mla_w_ukv):
    B, L, _ = h.shape
    gq, gk, gv, gr, ga, dq, dk, dv, mq, mkv, mkr, gates = split_columns(h @ w_in)
    heads = lambda t, n: t.reshape(B, L, n, -1).transpose(0, 2, 1, 3)
    ga = ga.reshape(B, L, 2, GLA_RANK).astype(jnp.float32)
    log_a = jax.nn.log_sigmoid(jnp.einsum('blnr,nre->nble', ga, gla_w_a2)
                               + gla_b_a[:, None, None, :]) / GLA_TAU
    g_q = heads(gq, GLA_HEADS) * GLA_DK ** -0.5
    g_k = heads(gk, GLA_HEADS)
    g_v = heads(gv, GLA_HEADS)
    la_f = heads(log_a[0], GLA_HEADS)
    la_b = heads(log_a[1], GLA_HEADS)
    d_q = dq.reshape(B, L, DIFF_HEADS, 2, DIFF_DH).transpose(0, 2, 3, 1, 4) * DIFF_DH ** -0.5
    d_k = dk.reshape(B, L, DIFF_HEADS, 2, DIFF_DH).transpose(0, 2, 3, 1, 4)
    d_v = heads(dv, DIFF_HEADS)
    q_full = heads(rms_norm(mq, mla_q_norm_g) @ mla_w_uq, MLA_HEADS)
    q_nope, q_rope = q_full[..., :MLA_NOPE], q_full[..., MLA_NOPE:]
    kv = heads(rms_norm(mkv, mla_kv_norm_g) @ mla_w_ukv, MLA_HEADS)
    k_nope, m_v = kv[..., :MLA_NOPE], kv[..., MLA_NOPE:]
    k_rope = mkr
    if rope is not None:
        cos, sin = rope
        d_q = apply_rope(d_q, cos, sin)
        d_k = apply_rope(d_k, cos, sin)
        q_rope = apply_rope(q_rope, cos, sin)
        k_rope = apply_rope(k_rope, cos, sin)
    gla = (g_q, g_k, g_v, la_f, la_b, gr)
    diff = (d_q, d_k, d_v)
    mla = (q_nope * MLA_SCALE, q_rope * MLA_SCALE, k_nope, k_rope, m_v)
    return gla, diff, mla, gates


def merge_branches(o_gla, r, o_diff, o_mla, gates, lam_init, gla_norm_g, diff_norm_g, w_branch, w_out):
    y_a = merge_heads(rms_norm(o_gla, gla_norm_g)) * jax.nn.silu(r)
    y_b = merge_heads(rms_norm(o_diff, diff_norm_g) * (1.0 - lam_init))
    y_c = merge_heads(o_mla)
    g_a, g_b, g_c = jnp.split(gates, 3, axis=-1)
    y = (jax.nn.sigmoid(g_a) * (y_a @ w_branch[0]) + jax.nn.sigmoid(g_b) * (y_b @ w_branch[1])
         + jax.nn.sigmoid(g_c) * (y_c @ w_branch[2]))
    return y @ w_out


def token_mixer(h, hc, rope, lam, lam_init, w_in, gla_w_a2, gla_b_a, gla_norm_g, diff_norm_g,
                mla_q_norm_g, mla_kv_norm_g, mla_w_uq, mla_w_ukv, w_branch, w_out, ctx_out):
    feat_args = (w_in, gla_w_a2, gla_b_a, mla_q_norm_g, mla_kv_norm_g, mla_w_uq, mla_w_ukv)
    lat_gla, lat_diff, lat_mla, lat_gates = mixer_features(h, rope, *feat_args)
    ctx_gla, ctx_diff, ctx_mla, ctx_gates = mixer_features(hc, None, *feat_args)
    B = h.shape[0]
    s0 = jnp.zeros((B, GLA_HEADS, GLA_DK, GLA_DV), jnp.float32)
    o_gc, s_f, s_b = gla_bidir(*ctx_gla[:5], s0, s0)
    o_g, _, _ = gla_bidir(*lat_gla[:5], s_f, s_b)
    dq, dk, dv = lat_diff
    cq, ck, cv = ctx_diff
    dk_all = jnp.concatenate([dk, ck], axis=3)
    dv_all = jnp.concatenate([dv, cv], axis=2)
    o_d = sweep_query_blocks(lambda q: diff_core(q, dk_all, dv_all, lam), (dq,))
    qn, qr, kn, kr, mv = lat_mla
    cqn, cqr, ckn, ckr, cmv = ctx_mla
    kn_all = jnp.concatenate([kn, ckn], axis=2)
    kr_all = jnp.concatenate([kr, ckr], axis=1)
    mv_all = jnp.concatenate([mv, cmv], axis=2)
    o_m = sweep_query_blocks(lambda a, b: mla_core(a, b, kn_all, kr_all, mv_all), (qn, qr))
    merge_args = (lam_init, gla_norm_g, diff_norm_g, w_branch, w_out)
    y = merge_branches(o_g, lat_gla[5], o_d, o_m, lat_gates, *merge_args)
    if not ctx_out:
        return y, None
    o_dc = diff_core(cq, ck, cv, lam)
    o_mc = mla_core(cqn, cqr, ckn, ckr, cmv)
    yc = merge_branches(o_gc, ctx_gla[5], o_dc, o_mc, ctx_gates, *merge_args)
    return y, yc


def swiglu(h, w_in, w_out):
    gate, up = jnp.split(h @ w_in, 2, axis=-1)
    return (jax.nn.silu(gate) * up) @ w_out


def setup_inputs(seed: int = 0) -> dict:
    key = jax.random.key(seed)
    ks = jax.random.split(key, 24)
    D, L = D_MODEL, DEPTH

    def nrm(k, shape, scale):
        return jax.random.normal(k, shape, jnp.float32) * scale

    def gain(k, shape):
        return 1.0 + nrm(k, shape, 0.02)

    return {
        "x": nrm(ks[0], (BATCH, SEQ, D), 1.0),
        "c": nrm(ks[1], (BATCH, D), 1.0),
        "ctx": nrm(ks[2], (BATCH, CTX_LEN, D), 1.0),
        "c_ctx": nrm(ks[3], (D,), 1.0),
        "w_mod": nrm(ks[4], (L, D, 6 * D), D ** -0.5),
        "b_mod": nrm(ks[5], (L, 6 * D), 0.02),
        "w_in": nrm(ks[6], (L, D, N_IN), D ** -0.5),
        "gla_w_a2": nrm(ks[7], (L, 2, GLA_RANK, GLA_HEADS * GLA_DK), GLA_RANK ** -0.5),
        "gla_b_a": nrm(ks[8], (L, 2, GLA_HEADS * GLA_DK), 0.02),
        "gla_norm_g": gain(ks[9], (L, GLA_DV)),
        "diff_lam": nrm(ks[10], (L, 4, DIFF_DH), 0.1),
        "diff_norm_g": gain(ks[11], (L, 2 * DIFF_DH)),
        "mla_q_norm_g": gain(ks[12], (L, MLA_Q_RANK)),
        "mla_kv_norm_g": gain(ks[13], (L, MLA_KV_RANK)),
        "mla_w_uq": nrm(ks[14], (L, MLA_Q_RANK, MLA_HEADS * (MLA_NOPE + MLA_ROPE)), MLA_Q_RANK ** -0.5),
        "mla_w_ukv": nrm(ks[15], (L, MLA_KV_RANK, MLA_HEADS * (MLA_NOPE + MLA_DV)), MLA_KV_RANK ** -0.5),
        "w_branch": nrm(ks[16], (L, 3, BRANCH_W, D), BRANCH_W ** -0.5),
        "w_out": nrm(ks[17], (L, D, D), D ** -0.5 * DEEPNORM_BETA),
        "ln1_g": gain(ks[18], (L, D)),
        "ln1_b": nrm(ks[19], (L, D), 0.02),
        "ffn_w_in": nrm(ks[20], (L, D, 2 * FFN_HIDDEN), D ** -0.5),
        "ffn_w_out": nrm(ks[21], (L, FFN_HIDDEN, D), FFN_HIDDEN ** -0.5 * DEEPNORM_BETA),
        "ln2_g": gain(ks[22], (L, D)),
        "ln2_b": nrm(ks[23], (L, D), 0.02),
    }


def reference(x, c, ctx, c_ctx, w_mod, b_mod, w_in, gla_w_a2, gla_b_a, gla_norm_g, diff_lam,
              diff_norm_g, mla_q_norm_g, mla_kv_norm_g, mla_w_uq, mla_w_ukv, w_branch, w_out,
              ln1_g, ln1_b, ffn_w_in, ffn_w_out, ln2_g, ln2_b):
    rope = axial_rope_tables(x.shape[1], x.dtype)
    c_act = jax.nn.silu(c)
    cc_act = jax.nn.silu(c_ctx)
    xc = ctx
    for l in range(DEPTH):
        last = l == DEPTH - 1
        sh1, sc1, g1, sh2, sc2, g2 = jnp.split((c_act @ w_mod[l] + b_mod[l])[:, None, :], 6, axis=-1)
        csh1, csc1, cg1, csh2, csc2, cg2 = jnp.split(cc_act @ w_mod[l] + b_mod[l], 6, axis=-1)
        lam_init = 0.8 - 0.6 * math.exp(-0.3 * l)
        dl = diff_lam[l].astype(jnp.float32)
        lam = jnp.exp(jnp.sum(dl[0] * dl[1])) - jnp.exp(jnp.sum(dl[2] * dl[3])) + lam_init
        y, yc = token_mixer(modulate(x, sh1, sc1), modulate(xc, csh1, csc1), rope, lam, lam_init,
                            w_in[l], gla_w_a2[l], gla_b_a[l], gla_norm_g[l], diff_norm_g[l],
                            mla_q_norm_g[l], mla_kv_norm_g[l], mla_w_uq[l], mla_w_ukv[l],
                            w_branch[l], w_out[l], not last)
        x = layer_norm(DEEPNORM_ALPHA * x + g1 * y, ln1_g[l], ln1_b[l])
        x = layer_norm(DEEPNORM_ALPHA * x + g2 * swiglu(modulate(x, sh2, sc2), ffn_w_in[l], ffn_w_out[l]),
                       ln2_g[l], ln2_b[l])
        if not last:
            xc = layer_norm(DEEPNORM_ALPHA * xc + cg1 * yc, ln1_g[l], ln1_b[l])
            xc = layer_norm(DEEPNORM_ALPHA * xc + cg2 * swiglu(modulate(xc, csh2, csc2), ffn_w_in[l], ffn_w_out[l]),
                            ln2_g[l], ln2_b[l])
    return x
```

```python
from contextlib import ExitStack
import math
import numpy as np
import ml_dtypes
import concourse.bass as bass
import concourse.mybir as mybir
from concourse.bass_utils import run_bass_kernel_spmd

F32 = mybir.dt.float32
BF16 = mybir.dt.bfloat16
AF = mybir.ActivationFunctionType
ALU = mybir.AluOpType
AX = mybir.AxisListType
P = 128

D = 1024
DEPTH = 2
CT = 256
NIN = 9696
EPS = 1e-6
ALPHA = (2 * DEPTH) ** 0.25
FH = 2816
MLA_SCALE = (128 + 64) ** -0.5

ENGS = ("pe", "act", "dve", "pool", "sp")
SEM_LIM = 30000
_SEM_POOL = {}


class Buf:
    __slots__ = ("w", "r")

    def __init__(self):
        self.w = None
        self.r = []


class Op:
    __slots__ = ("eng", "fn", "deps", "dma", "flag", "sem", "target")

    def __init__(self, eng, fn, deps, dma):
        self.eng = eng
        self.fn = fn
        self.deps = deps
        self.dma = dma
        self.flag = False
        self.sem = None
        self.target = 0


class Sched:
    N_DMA_SEMS = 10
    _uid = 0

    def __init__(self, nc, es):
        self.nc = nc
        self.es = es
        self.ops = {e: [] for e in ENGS}

    def sbuf(self, name, shape, dt):
        Sched._uid += 1
        return self.es.enter_context(self.nc.sbuf_tensor(f"{name}_{Sched._uid}", list(shape), dt))

    def psum(self, name, shape, dt):
        Sched._uid += 1
        return self.es.enter_context(self.nc.psum_tensor(f"{name}_{Sched._uid}", list(shape), dt))

    def op(self, eng, fn, rd=(), wr=(), dma=False, deps=()):
        dl = [d for d in deps if d is not None]
        for b in rd:
            if b.w is not None:
                dl.append(b.w)
        for b in wr:
            if b.w is not None:
                dl.append(b.w)
            dl.extend(b.r)
        o = Op(eng, fn, None, dma)
        seen = set()
        dd = []
        for d in dl:
            if d is o or id(d) in seen:
                continue
            if (not d.dma) and (not dma) and d.eng == "pe" and eng == "pe":
                continue
            seen.add(id(d))
            dd.append(d)
            if not d.dma:
                d.flag = True
        o.deps = dd
        self.ops[eng].append(o)
        for b in rd:
            b.r.append(o)
        for b in wr:
            b.w = o
            b.r = []
        return o

    def dma(self, eng, out, in_, rd=(), wr=(), deps=(), **kw):
        return self.op(eng, lambda e: e.dma_start(out=out, in_=in_, **kw), rd, wr, dma=True, deps=deps)

    def emit(self):
        nc = self.nc
        es = self.es
        Sched._uid += 1
        u = Sched._uid
        dq = ("sp", "act", "pool")
        handles = []
        pool = _SEM_POOL.setdefault(id(nc), {})

        def new_sem(name):
            if name not in pool:
                pool[name] = nc.alloc_semaphore(name=name)
            h = pool[name]
            handles.append(h)
            return h

        dsem = {e: [new_sem(f"sd_{e}{i}") for i in range(self.N_DMA_SEMS)] for e in dq}
        duse = {e: [0] * self.N_DMA_SEMS for e in dq}
        esems = {}
        efinal = []
        for e in ENGS:
            c = 0
            k = 0
            cur = None
            for o in self.ops[e]:
                if o.dma:
                    slot = k % self.N_DMA_SEMS
                    k += 1
                    duse[e][slot] += 1
                    o.sem = dsem[e][slot]
                    o.target = 16 * duse[e][slot]
                elif o.flag:
                    if c % SEM_LIM == 0:
                        cur = new_sem(f"se_{e}{c // SEM_LIM}")
                        esems.setdefault(e, []).append(cur)
                    c += 1
                    o.sem = cur
                    o.target = (c - 1) % SEM_LIM + 1
            if c:
                efinal.append((cur, (c - 1) % SEM_LIM + 1))
        finals = []
        for e in dq:
            for slot in range(self.N_DMA_SEMS):
                if duse[e][slot]:
                    finals.append((dsem[e][slot], 16 * duse[e][slot]))

        def run(e, eng):
            waited = {}
            for o in self.ops[e]:
                needs = [(d.sem, d.target) for d in o.deps]
                if o.dma and o.target > 16:
                    needs.append((o.sem, o.target - 16))
                for sem, tgt in needs:
                    key = id(sem)
                    if waited.get(key, 0) >= tgt:
                        continue
                    waited[key] = tgt
                    eng.wait_ge(sem, tgt)
                ins = o.fn(eng)
                if o.dma:
                    ins.then_inc(o.sem, 16)
                elif o.flag:
                    ins.then_inc(o.sem, 1)
            for sem, tgt in finals + efinal:
                if waited.get(id(sem), 0) < tgt:
                    eng.wait_ge(sem, tgt)

        nc.all_engine_barrier()
        for h in handles:
            nc.gpsimd.sem_clear(h)
        nc.all_engine_barrier()
        self._handles = handles

        with nc.Block() as block:
            @block.tensor
            def _(eng):
                run("pe", eng)

            @block.scalar
            def _(eng):
                run("act", eng)

            @block.vector
            def _(eng):
                run("dve", eng)

            @block.gpsimd
            def _(eng):
                run("pool", eng)

            @block.sync
            def _(eng):
                run("sp", eng)


import os
_STAGE_CNT = [0]


def stage(nc, build, *a, **k):
    with ExitStack() as es:
        S = Sched(nc, es)
        build(S, *a, **k)
        S.emit()


class Ring:
    def __init__(self, S, name, shape, dt, n, psum=False):
        mk = S.psum if psum else S.sbuf
        self.t = [mk(f"{name}{i}", shape, dt) for i in range(n)]
        self.b = [Buf() for _ in range(n)]
        self.i = -1

    def next(self):
        self.i = (self.i + 1) % len(self.t)
        return self.t[self.i], self.b[self.i]


CAST_W = 2048


def cast_ring(S):
    return Ring(S, "cst", [P, CAST_W], F32, 2)


def load_cast(S, ring, dst, src, rows, ncol, wb, shape3=None):
    for c0 in range(0, ncol, CAST_W):
        cl = min(CAST_W, ncol - c0)
        stg, bs = ring.next()
        S.dma("sp", stg[:rows, 0:cl], src[:, c0:c0 + cl], wr=[bs])
        S.op("pool", lambda e, stg=stg, c0=c0, cl=cl: e.tensor_copy(dst[:, c0:c0 + cl], stg[:rows, 0:cl]),
             rd=[bs], wr=[wb])


def st_mod(S, cin, w_mod, b_mod, modv):
    cs = S.sbuf("cs", [P, 16], F32)
    ca = S.sbuf("ca", [P, 16], F32)
    bcs, bca = Buf(), Buf()
    S.dma("sp", cs[:], cin, wr=[bcs])
    S.op("act", lambda e: e.activation(out=ca[:], in_=cs[:], func=AF.Silu), rd=[bcs], wr=[bca])
    wr_ = Ring(S, "wm", [P, 8, 512], F32, 2)
    pr_ = Ring(S, "pm", [2, 512], F32, 2, psum=True)
    for l in range(DEPTH):
        row = S.sbuf("row", [2, 6144], F32)
        brow = Buf()
        S.dma("sp", row[0:1, :], b_mod[l:l + 1, :], wr=[brow])
        S.dma("sp", row[1:2, :], b_mod[l:l + 1, :], wr=[brow])
        for n in range(12):
            wt, wb = wr_.next()
            S.dma("sp", wt[:],
                  w_mod[l, :, n * 512:(n + 1) * 512].rearrange("(kc p) n -> p kc n", p=P), wr=[wb])
            ps, pb = pr_.next()
            for kc in range(8):
                S.op("pe", lambda e, ps=ps, wt=wt, kc=kc: e.matmul(
                    ps[:], ca[:, kc * 2:kc * 2 + 2], wt[:, kc, :], start=(kc == 0), stop=(kc == 7)),
                    rd=[bca, wb], wr=[pb])
            add1 = 1.0 if n in (2, 3, 8, 9) else 0.0
            sl = row[:, n * 512:(n + 1) * 512]
            S.op("dve", lambda e, sl=sl, ps=ps, add1=add1: e.scalar_tensor_tensor(
                out=sl, in0=ps[:], scalar=add1, in1=sl, op0=ALU.add, op1=ALU.add),
                rd=[pb], wr=[brow])
        S.dma("sp", modv[l], row[:], rd=[brow])


def ln_stats(S, xt, bx, st, bst):
    S.op("dve", lambda e: e.bn_stats(st[:, 0:6], xt[:, 0:512]), rd=[bx], wr=[bst])
    S.op("dve", lambda e: e.bn_stats(st[:, 6:12], xt[:, 512:1024]), rd=[bx], wr=[bst])
    S.op("dve", lambda e: e.bn_aggr(st[:, 12:14], st[:, 0:12]), rd=[bst], wr=[bst])
    S.op("dve", lambda e: e.tensor_scalar(st[:, 14:15], st[:, 13:14], EPS, None, ALU.add),
         rd=[bst], wr=[bst])
    S.op("act", lambda e: e.activation(out=st[:, 14:15], in_=st[:, 14:15], func=AF.Sqrt), rd=[bst], wr=[bst])
    S.op("dve", lambda e: e.reciprocal(st[:, 14:15], st[:, 14:15]), rd=[bst], wr=[bst])
    S.op("dve", lambda e: e.scalar_tensor_tensor(
        out=st[:, 15:16], in0=st[:, 12:13], scalar=-1.0, in1=st[:, 14:15], op0=ALU.mult, op1=ALU.mult),
        rd=[bst], wr=[bst])


def load_bc(S, name, src_row, eng="sp"):
    n = src_row.shape[-1]
    t = S.sbuf(name, [P, n], F32)
    b = Buf()
    S.dma(eng, t[:], src_row.partition_broadcast(P), wr=[b])
    return t, b


def st_modulate(S, xs, modv_l, sc_off, sh_off, hT, ident, T, LT):
    idt = S.sbuf("idt", [P, P], BF16)
    bid = Buf()
    S.dma("sp", idt[:], ident, wr=[bid])
    bc = {}
    for j in range(2):
        bc[j] = (load_bc(S, f"sc{j}", modv_l[j:j + 1, sc_off:sc_off + D]),
                 load_bc(S, f"sh{j}", modv_l[j:j + 1, sh_off:sh_off + D]))
    xr = Ring(S, "xt", [P, D], F32, 2)
    sr = Ring(S, "st", [P, 16], F32, 2)
    nr = Ring(S, "xn", [P, D], F32, 2)
    hr = Ring(S, "hb", [P, D], BF16, 2)
    tr = Ring(S, "tp", [P, 8, P], BF16, 2, psum=True)
    outr = Ring(S, "ho", [P, 8, 512], BF16, 2)
    hTv = hT.rearrange("(kc p) t -> p kc t", p=P)
    for t0 in range(0, T, 512):
        tl = min(512, T - t0)
        ot, ob = outr.next()
        for s in range(tl // P):
            i = t0 // P + s
            j = 0 if i * P < LT else 1
            (sct, scb), (sht, shb) = bc[j]
            xt, bx = xr.next()
            S.dma("sp", xt[:], xs[i * P:(i + 1) * P, :], wr=[bx])
            st, bst = sr.next()
            ln_stats(S, xt, bx, st, bst)
            xn, bn = nr.next()
            S.op("act", lambda e, xn=xn, xt=xt, st=st: e.activation(
                out=xn[:], in_=xt[:], func=AF.Identity, scale=st[:, 14:15], bias=st[:, 15:16]),
                rd=[bx, bst], wr=[bn])
            S.op("dve", lambda e, xn=xn, sct=sct: e.tensor_tensor(xn[:], xn[:], sct[:], ALU.mult),
                 rd=[bn, scb], wr=[bn])
            hb, bh = hr.next()
            S.op("pool", lambda e, hb=hb, xn=xn, sht=sht: e.tensor_tensor(hb[:], xn[:], sht[:], ALU.add),
                 rd=[bn, shb], wr=[bh])
            tp, btp = tr.next()
            for kc in range(8):
                S.op("pe", lambda e, tp=tp, hb=hb, kc=kc: e.transpose(
                    tp[:, kc, :], hb[:, kc * P:(kc + 1) * P], idt[:]), rd=[bh, bid], wr=[btp])
            S.op("act", lambda e, ot=ot, tp=tp, s=s: e.copy(ot[:, :, s * P:(s + 1) * P], tp[:]),
                 rd=[btp], wr=[ob])
        S.dma("sp", hTv[:, :, t0:t0 + tl], ot[:, :, 0:tl], rd=[ob])


def load_w_bf16(S, name, W, KC, n, ring):
    wt = S.sbuf(name, [P, KC, n], BF16)
    wb = Buf()
    Wv = W.rearrange("(kc p) n -> p kc n", p=P)
    for kc in range(KC):
        load_cast(S, ring, wt[:, kc, :], Wv[:, kc, :], P, n, wb)
    return wt, wb


def rope_evac(S, ps, pb, m, scale, xr, pr2, t1r, perm, bperm, cs, sn, bcs, dst_ap, bdst):
    xb, bxb = xr.next()
    S.op("act", lambda e: e.activation(out=xb[:m, :], in_=ps[:m, :], func=AF.Copy, scale=scale),
         rd=[pb], wr=[bxb])
    p2, bp2 = pr2.next()
    S.op("pe", lambda e: e.matmul(p2[:m, :], perm[:m, :m], xb[:m, :], start=True, stop=True),
         rd=[bxb, bperm], wr=[bp2])
    t1, bt1 = t1r.next()
    S.op("pool", lambda e: e.tensor_tensor(t1[:m, :], xb[:m, :], cs[:m, :], ALU.mult),
         rd=[bxb, bcs], wr=[bt1])
    t2, bt2 = t1r.next()
    S.op("dve", lambda e: e.tensor_tensor(t2[:m, :], p2[:m, :], sn[:m, :], ALU.mult),
         rd=[bp2, bcs], wr=[bt2])
    S.op("pool", lambda e: e.tensor_tensor(dst_ap, t1[:m, :], t2[:m, :], ALU.add),
         rd=[bt1, bt2], wr=[bdst])


def st_proj(S, srcT, KC, T, jobs, cosT, sinT, permd):
    perm = S.sbuf("perm", [P, P], BF16)
    bperm = Buf()
    S.dma("sp", perm[:], permd, wr=[bperm])
    srcv = srcT.rearrange("(kc p) t -> p kc t", p=P)
    if True:
        if True:
            S2 = S
            cring = cast_ring(S)
            for j in jobs:
                j["wt"], j["wb"] = load_w_bf16(S2, "w", j["W"], KC, j["n"], cring)
            sr = Ring(S2, "src", [P, KC, 512], BF16, 2)
            pr = Ring(S2, "ps", [P, 512], F32, 4, psum=True)
            pr2 = Ring(S2, "ps2", [P, 512], F32, 2, psum=True)
            o16 = Ring(S2, "o16", [P, 8, 512], BF16, 3)
            o32 = Ring(S2, "o32", [P, 4, 512], F32, 2)
            xr = Ring(S2, "xb", [P, 512], BF16, 2)
            t1r = Ring(S2, "t1", [P, 512], F32, 4)
            csr = Ring(S2, "cs", [P, 2, 512], F32, 2)
            has_rope = any(j["kind"] == "rope" for j in jobs)
            def tile(t0, tl):
                st, sb = sr.next()
                S.dma("sp", st[:, :, 0:tl], srcv[:, :, t0:t0 + tl], wr=[sb])
                cst = bcs = None
                if has_rope:
                    cst, bcs = csr.next()
                    S.dma("sp", cst[:, 0, 0:tl], cosT[:, t0:t0 + tl], wr=[bcs])
                    S.dma("sp", cst[:, 1, 0:tl], sinT[:, t0:t0 + tl], wr=[bcs])
                for j in jobs:
                    wt, wb, n = j["wt"], j["wb"], j["n"]
                    if j["kind"] == "tm":
                        for s in range(tl // P):
                            ot, ob = o16.next()
                            otv = ot[:].rearrange("p a b -> p (a b)")
                            for hf in range(n // 512):
                                ps, pb = pr.next()
                                for kc in range(KC):
                                    S.op("pe", lambda e, ps=ps, st=st, wt=wt, kc=kc, s=s, hf=hf: e.matmul(
                                        ps[:], st[:, kc, s * P:(s + 1) * P], wt[:, kc, hf * 512:(hf + 1) * 512],
                                        start=(kc == 0), stop=(kc == KC - 1)), rd=[sb, wb], wr=[pb])
                                if hf % 2 == 0:
                                    S.op("act", lambda e, otv=otv, ps=ps, hf=hf: e.copy(
                                        otv[:, hf * 512:(hf + 1) * 512], ps[:]), rd=[pb], wr=[ob])
                                else:
                                    S.op("dve", lambda e, otv=otv, ps=ps, hf=hf: e.tensor_copy(
                                        otv[:, hf * 512:(hf + 1) * 512], ps[:]), rd=[pb], wr=[ob])
                            S.dma("sp", j["dst"][t0 + s * P:t0 + (s + 1) * P, :], otv[:, 0:n], rd=[ob])
                        continue
                    nch = (n + P - 1) // P
                    f32out = j.get("dt") == "f32"
                    grp = 4 if f32out else 8
                    for c0 in range(0, nch, grp):
                        ot, ob = (o32 if f32out else o16).next()
                        cn = min(grp, nch - c0)
                        for ci in range(cn):
                            c = c0 + ci
                            m = min(P, n - c * P)
                            ps, pb = pr.next()
                            for kc in range(KC):
                                S.op("pe", lambda e, ps=ps, st=st, wt=wt, kc=kc, c=c, m=m: e.matmul(
                                    ps[:m, 0:tl], wt[:, kc, c * P:c * P + m], st[:, kc, 0:tl],
                                    start=(kc == 0), stop=(kc == KC - 1)), rd=[sb, wb], wr=[pb])
                            if j["kind"] == "rope":
                                rope_evac(S, ps[:, 0:tl], pb, m, j.get("scale", 1.0), _RingView(xr, tl, 2),
                                          _RingView(pr2, tl, 2), _RingView(t1r, tl, 2), perm, bperm,
                                          cst[:, 0, 0:tl], cst[:, 1, 0:tl], bcs, ot[:m, ci, 0:tl], ob)
                            else:
                                func = j.get("func") or AF.Copy
                                S.op("act", lambda e, ot=ot, ps=ps, ci=ci, m=m, func=func, sc=j.get("scale", 1.0):
                                     e.activation(out=ot[:m, ci, 0:tl], in_=ps[:m, 0:tl], func=func, scale=sc),
                                     rd=[pb], wr=[ob])
                        if n >= P:
                            dv = j["dst"].rearrange("(c p) t -> p c t", p=P)
                            S.dma("sp", dv[:, c0:c0 + cn, t0:t0 + tl], ot[:, 0:cn, 0:tl], rd=[ob])
                        else:
                            S.dma("sp", j["dst"][:, t0:t0 + tl], ot[:n, 0, 0:tl], rd=[ob])

            for t0 in range(0, T, 512):
                tile(t0, min(512, T - t0))


def host_consts(LT, T):
    bf = ml_dtypes.bfloat16
    t = np.arange(LT)
    pos_row = (t // 64).astype(np.float32)
    pos_col = (t % 64).astype(np.float32)
    inv = (np.float32(10000.0) ** (-np.arange(0, 32, 2, dtype=np.float32) / np.float32(32))).astype(np.float32)
    ang = np.concatenate([pos_row[:, None] * inv, pos_col[:, None] * inv], -1)
    cos = np.ones((T, 32), np.float32)
    sin = np.zeros((T, 32), np.float32)
    cos[:LT] = np.cos(ang)
    sin[:LT] = np.sin(ang)
    d = np.arange(P) % 32
    cosT = np.ascontiguousarray(cos[:, d].T)
    sinT = np.ascontiguousarray(sin[:, d].T)
    perm = np.zeros((P, P), np.float32)
    for m in range(P):
        if m % 64 < 32:
            perm[m + 32, m] = -1.0
        else:
            perm[m - 32, m] = 1.0
    return dict(cosT=cosT, sinT=sinT, perm=perm.astype(bf), ident=np.eye(P, dtype=bf))


def proj_jobs(w_in_l, d):
    def W(o, n):
        return w_in_l[:, o:o + n]
    p1 = [
        dict(kind="fm", W=W(0, 512), n=512, dst=d.get("gqT"), scale=128 ** -0.5, dt="f32"),
        dict(kind="fm", W=W(512, 512), n=512, dst=d.get("gkT"), dt="f32"),
        dict(kind="tm", W=W(1024, 1024), n=1024, dst=d.get("gv")),
        dict(kind="fm", W=W(2048, 1024), n=1024, dst=d.get("grT"), func=AF.Silu),
        dict(kind="fm", W=W(3072, 32), n=32, dst=d.get("gaT"), dt="f32"),
        dict(kind="fm", W=W(6624, 3072), n=3072, dst=d.get("gatesT"), func=AF.Sigmoid),
    ]
    p2 = [
        dict(kind="rope", W=W(3104, 1024), n=1024, dst=d.get("dqT"), scale=0.125),
        dict(kind="rope", W=W(4128, 1024), n=1024, dst=d.get("dkT")),
        dict(kind="tm", W=W(5152, 1024), n=1024, dst=d.get("dv")),
        dict(kind="fm", W=W(6176, 256), n=256, dst=d.get("mqT"), dt="f32"),
        dict(kind="fm", W=W(6432, 128), n=128, dst=d.get("mkvT"), dt="f32"),
        dict(kind="rope", W=W(6560, 64), n=64, dst=d.get("krT")),
    ]
    return [p1, p2]


def feat_scratch(nc, T, kind="Internal"):
    def mk(name, shape, dt):
        return nc.dram_tensor(name, shape, dt, kind=kind).ap()
    return dict(
        gqT=mk("gqT", [512, T], F32), gkT=mk("gkT", [512, T], F32), gv=mk("gv", [T, 1024], BF16),
        grT=mk("grT", [1024, T], BF16), gaT=mk("gaT", [32, T], F32), gatesT=mk("gatesT", [3072, T], BF16),
        dqT=mk("dqT", [1024, T], BF16), dkT=mk("dkT", [1024, T], BF16), dv=mk("dv", [T, 1024], BF16),
        mqT=mk("mqT", [256, T], F32), mkvT=mk("mkvT", [128, T], F32), krT=mk("krT", [64, T], BF16),
    )


def rms_fm(S, xt, bx, nchunk, onesq, bones, gcol0, sv, bsv, sqr, pms, rvr, nq, bnq):
    sq, bsq = sqr.next()
    S.op("act", lambda e: e.activation(out=sq[:, 0:nchunk, :], in_=xt[:, 0:nchunk, :], func=AF.Square),
         rd=[bx], wr=[bsq])
    ps, pb = pms.next()
    for c in range(nchunk):
        S.op("pe", lambda e, c=c: e.matmul(ps[:], onesq[:], sq[:, c, :], start=(c == 0), stop=(c == nchunk - 1)),
             rd=[bsq, bones], wr=[pb])
    rv, brv = rvr.next()
    S.op("act", lambda e: e.activation(out=rv[:], in_=ps[:], func=AF.Sqrt, bias=EPS, scale=1.0 / (nchunk * P)),
         rd=[pb], wr=[brv])
    S.op("dve", lambda e: e.reciprocal(rv[:], rv[:]), rd=[brv], wr=[brv])
    for c in range(nchunk):
        S.op("dve", lambda e, c=c: e.scalar_tensor_tensor(
            out=nq[:, c, :], in0=xt[:, c, :], scalar=sv[:, gcol0 + c:gcol0 + c + 1], in1=rv[:],
            op0=ALU.mult, op1=ALU.mult), rd=[bx, brv, bsv], wr=[bnq])


def st_mla_up(S, mqT, mkvT, w_uq, w_ukv, smallv_l, qnT, qrT, knT, mv, cosT, sinT, permd, T):
    perm = S.sbuf("perm", [P, P], BF16)
    bperm = Buf()
    S.dma("sp", perm[:], permd, wr=[bperm])
    sv = S.sbuf("sv", [P, 16], F32)
    bsv = Buf()
    S.dma("sp", sv[:], smallv_l, wr=[bsv])
    ones = S.sbuf("ones", [P, P], F32)
    bones = Buf()
    S.op("dve", lambda e: e.memset(ones[:], 1.0), wr=[bones])
    wqn = S.sbuf("wqn", [P, 2, 8, 128], BF16)
    wqr = S.sbuf("wqr", [P, 2, 8, 64], BF16)
    wk = S.sbuf("wk", [P, 8, 128], BF16)
    wv = S.sbuf("wv", [P, 8, 128], BF16)
    bw = Buf()
    uqv = w_uq.rearrange("(kc p) (h j) -> p kc h j", p=P, j=192)
    cring = cast_ring(S)
    uqk = w_uq.rearrange("(kc p) n -> p kc n", p=P)
    for kc in range(2):
        stg, bs = cring.next()
        S.dma("sp", stg[:, 0:1536], uqk[:, kc, :], wr=[bs])
        sv3 = stg[:, 0:1536].rearrange("p (h j) -> p h j", j=192)
        S.op("pool", lambda e, kc=kc, sv3=sv3: e.tensor_copy(wqn[:, kc, :, :], sv3[:, :, 0:128]), rd=[bs], wr=[bw])
        S.op("pool", lambda e, kc=kc, sv3=sv3: e.tensor_copy(wqr[:, kc, :, :], sv3[:, :, 128:192]), rd=[bs], wr=[bw])
    stg, bs = cring.next()
    S.dma("sp", stg[:, 0:2048], w_ukv, wr=[bs])
    kv3 = stg[:, 0:2048].rearrange("p (h j) -> p h j", j=256)
    S.op("pool", lambda e, kv3=kv3: e.tensor_copy(wk[:], kv3[:, :, 0:128]), rd=[bs], wr=[bw])
    S.op("pool", lambda e, kv3=kv3: e.tensor_copy(wv[:], kv3[:, :, 128:256]), rd=[bs], wr=[bw])
    wvf = wv[:].rearrange("p h j -> p (h j)")
    mqv = mqT.rearrange("(c p) t -> p c t", p=P)
    xqr = Ring(S, "xq", [P, 2, 512], F32, 2)
    xkr = Ring(S, "xk", [P, 1, 512], F32, 2)
    sqr = Ring(S, "sq", [P, 2, 512], F32, 2)
    pms = Ring(S, "pms", [P, 512], F32, 1, psum=True)
    rvr = Ring(S, "rv", [P, 512], F32, 2)
    nqr = Ring(S, "nq", [P, 2, 512], BF16, 2)
    nkr = Ring(S, "nk", [P, 1, 512], BF16, 2)
    pr = Ring(S, "ps", [P, 512], F32, 4, psum=True)
    pr2 = Ring(S, "ps2", [P, 512], F32, 2, psum=True)
    o16 = Ring(S, "o16", [P, 8, 512], BF16, 3)
    xr = Ring(S, "xb", [P, 512], BF16, 2)
    t1r = Ring(S, "t1", [P, 512], F32, 4)
    csr = Ring(S, "cs", [P, 2, 512], F32, 2)
    qnv = qnT.rearrange("(c p) t -> p c t", p=P)
    qrv = qrT.rearrange("(c p) t -> p c t", p=P)
    knv = knT.rearrange("(c p) t -> p c t", p=P)
    def tile(t0, tl):
        W = slice(0, tl)
        xq, bxq = xqr.next()
        S.dma("sp", xq[:, :, W], mqv[:, :, t0:t0 + tl], wr=[bxq])
        xk, bxk = xkr.next()
        S.dma("sp", xk[:, 0, W], mkvT[:, t0:t0 + tl], wr=[bxk])
        cst, bcs = csr.next()
        S.dma("sp", cst[:, 0, W], cosT[:, t0:t0 + tl], wr=[bcs])
        S.dma("sp", cst[:, 1, W], sinT[:, t0:t0 + tl], wr=[bcs])
        sqv, pmv, rvv = _RingView(sqr, tl, 3), _RingView(pms, tl, 2), _RingView(rvr, tl, 2)
        nq, bnq = nqr.next()
        rms_fm(S, xq[:, :, W], bxq, 2, ones, bones, 0, sv, bsv, sqv, pmv, rvv, nq[:, :, W], bnq)
        nk, bnk = nkr.next()
        rms_fm(S, xk[:, :, W], bxk, 1, ones, bones, 2, sv, bsv, sqv, pmv, rvv, nk[:, :, W], bnk)
        ot, ob = o16.next()
        for h in range(8):
            ps, pb = pr.next()
            for kc in range(2):
                S.op("pe", lambda e, ps=ps, kc=kc, h=h: e.matmul(
                    ps[:, W], wqn[:, kc, h, :], nq[:, kc, W], start=(kc == 0), stop=(kc == 1)),
                    rd=[bw, bnq], wr=[pb])
            S.op("act", lambda e, ot=ot, ps=ps, h=h: e.activation(
                out=ot[:, h, W], in_=ps[:, W], func=AF.Copy, scale=MLA_SCALE), rd=[pb], wr=[ob])
        S.dma("sp", qnv[:, :, t0:t0 + tl], ot[:, :, W], rd=[ob])
        ot, ob = o16.next()
        for c in range(4):
            ps, pb = pr.next()
            for kc in range(2):
                S.op("pe", lambda e, ps=ps, kc=kc, c=c: e.matmul(
                    ps[:, W], wqr[:, kc, 2 * c:2 * c + 2, :].rearrange("p h j -> p (h j)"), nq[:, kc, W],
                    start=(kc == 0), stop=(kc == 1)), rd=[bw, bnq], wr=[pb])
            rope_evac(S, ps[:, W], pb, P, MLA_SCALE, _RingView(xr, tl, 2), _RingView(pr2, tl, 2),
                      _RingView(t1r, tl, 2), perm, bperm, cst[:, 0, W], cst[:, 1, W], bcs, ot[:, c, W], ob)
        S.dma("sp", qrv[:, :, t0:t0 + tl], ot[:, 0:4, W], rd=[ob])
        ot, ob = o16.next()
        for h in range(8):
            ps, pb = pr.next()
            S.op("pe", lambda e, ps=ps, h=h: e.matmul(
                ps[:, W], wk[:, h, :], nk[:, 0, W], start=True, stop=True), rd=[bw, bnk], wr=[pb])
            S.op("act", lambda e, ot=ot, ps=ps, h=h: e.copy(ot[:, h, W], ps[:, W]), rd=[pb], wr=[ob])
        S.dma("sp", knv[:, :, t0:t0 + tl], ot[:, :, W], rd=[ob])
        for s in range(tl // P):
            ot, ob = o16.next()
            otv = ot[:].rearrange("p a b -> p (a b)")
            for hf in range(2):
                ps, pb = pr.next()
                S.op("pe", lambda e, ps=ps, s=s, hf=hf: e.matmul(
                    ps[:], nk[:, 0, s * P:(s + 1) * P], wvf[:, hf * 512:(hf + 1) * 512], start=True, stop=True),
                    rd=[bw, bnk], wr=[pb])
                S.op("dve", lambda e, otv=otv, ps=ps, hf=hf: e.tensor_copy(
                    otv[:, hf * 512:(hf + 1) * 512], ps[:]), rd=[pb], wr=[ob])
            S.dma("sp", mv[t0 + s * P:t0 + (s + 1) * P, :], otv[:, 0:1024], rd=[ob])

    for t0 in range(0, T, 512):
        tile(t0, min(512, T - t0))


def host_smallv(inp):
    sv = np.zeros((DEPTH, P, 16), np.float32)
    for l in range(DEPTH):
        sv[l, :, 0:2] = inp["mla_q_norm_g"][l].reshape(2, P).T
        sv[l, :, 2] = inp["mla_kv_norm_g"][l]
        sv[l, :, 3:11] = inp["gla_b_a"][l].reshape(8, P).T
        sv[l, :, 11:13] = inp["gla_norm_g"][l].reshape(2, P).T
        sv[l, :, 13] = inp["diff_norm_g"][l]
    return sv


GSEG = 768


def st_gla(S, gqT, gkT, gv, gaT, grT, w_a2, smallv_l, yaT, identd, scanmaskd, blockmaskd, T, LT):
    NTL = T // P
    NCH = T // 64
    idt = S.sbuf("idt", [P, P], BF16)
    bid = Buf()
    S.dma("sp", idt[:], identd, wr=[bid])
    sv = S.sbuf("sv", [P, 16], F32)
    nb = S.sbuf("nb", [P, 8], F32)
    bsv = Buf()
    S.dma("sp", sv[:], smallv_l, wr=[bsv])
    S.op("dve", lambda e: e.tensor_scalar(nb[:], sv[:, 3:11], -1.0, None, ALU.mult), rd=[bsv], wr=[bsv])
    ones = S.sbuf("ones", [P, P], F32)
    bones = Buf()
    S.op("dve", lambda e: e.memset(ones[:], 1.0), wr=[bones])
    smask = S.sbuf("smask", [P, GSEG], F32)
    bmask = S.sbuf("bmask", [P, 2, P], F32)
    wa2 = S.sbuf("wa2", [16, 2, 512], F32)
    bcst = Buf()
    S.dma("sp", smask[:], scanmaskd, wr=[bcst])
    S.dma("sp", bmask[:], blockmaskd.rearrange("d p q -> p d q"), wr=[bcst])
    S.dma("sp", wa2[:], w_a2.rearrange("d r e -> r d e"), wr=[bcst])

    qt = [S.sbuf(f"qt{d}", [P, T], BF16) for d in range(2)]
    kt = [S.sbuf(f"kt{d}", [P, T], BF16) for d in range(2)]
    khtm = [S.sbuf(f"kh{d}", [P, NTL, P], BF16) for d in range(2)]
    dec = [S.sbuf(f"dec{d}", [P, NCH], F32) for d in range(2)]
    acc = S.sbuf("acc", [P, 2, T], BF16)
    Sst = [S.sbuf(f"Sst{d}", [P, 256], F32) for d in range(2)]
    Sbf = [S.sbuf(f"Sbf{d}", [P, 256], BF16) for d in range(2)]

    qs = Ring(S, "qs", [P, GSEG], F32, 1)
    ks = Ring(S, "ks", [P, GSEG], F32, 1)
    gas = Ring(S, "gas", [16, 2, GSEG], F32, 1)
    tg = Ring(S, "tg", [P, GSEG], F32, 2)
    tb = Ring(S, "tb", [P, GSEG], F32, 2)
    tx = Ring(S, "tx", [P, GSEG], F32, 3)
    tkh = Ring(S, "tkh", [P, GSEG], BF16, 2)
    pz = Ring(S, "pz", [P, 512], F32, 1, psum=True)
    ptp = Ring(S, "ptp", [P, 1024], BF16, 1, psum=True)
    pA = Ring(S, "pA", [P, 512], F32, 1, psum=True)
    pS = Ring(S, "pS", [P, 512], F32, 1, psum=True)
    pod = [[S.psum(f"po{d}{ec}", [P, 512], F32) for ec in range(2)] for d in range(2)]
    bpod = [Buf(), Buf()]
    Asb = Ring(S, "Asb", [P, P], BF16, 3)
    vtr = Ring(S, "vt", [P, 256], BF16, 4)
    sqr = Ring(S, "sq", [P, 2, 512], F32, 1)
    rvr = Ring(S, "rv", [P, 512], F32, 1)
    nqr = Ring(S, "nq", [P, 2, 512], BF16, 2)
    grr = Ring(S, "grt", [P, 2, 512], BF16, 2)
    gvv = gv.rearrange("(n p) c -> p n c", p=P)
    grv = grT.rearrange("(c p) t -> p c t", p=P)
    yav = yaT.rearrange("(c p) t -> p c t", p=P)

    tilesF = list(range(LT // P, NTL)) + list(range(0, LT // P))
    tilesB = list(range(NTL - 1, LT // P - 1, -1)) + list(range(LT // P - 1, -1, -1))

    bq = [Buf(), Buf()]
    bkh = [Buf(), Buf()]
    bdec = [Buf(), Buf()]
    bacc = [Buf() for _ in range(NTL)]
    bS = [Buf(), Buf()]
    bSb = [Buf(), Buf()]
    ball = Buf()
    def seg(h, s0, sl):
        if True:
            nch = sl // 64
            q_, bq_ = qs.next()
            k_, bk_ = ks.next()
            ga_, bga_ = gas.next()
            S.dma("sp", q_[:, 0:sl], gqT[h * P:(h + 1) * P, s0:s0 + sl], wr=[bq_])
            S.dma("sp", k_[:, 0:sl], gkT[h * P:(h + 1) * P, s0:s0 + sl], wr=[bk_])
            S.dma("sp", ga_[:, 0, 0:sl], gaT[0:16, s0:s0 + sl], wr=[bga_])
            S.dma("sp", ga_[:, 1, 0:sl], gaT[16:32, s0:s0 + sl], wr=[bga_])
            for d in range(2):
                g_, bg_ = tg.next()
                for c0 in range(0, sl, 512):
                    cl = min(512, sl - c0)
                    ps, pb = pz.next()
                    S.op("pe", lambda e, ps=ps, d=d, ga_=ga_, c0=c0, cl=cl: e.matmul(
                        ps[:, 0:cl], wa2[:, d, h * P:(h + 1) * P], ga_[:, d, c0:c0 + cl], start=True, stop=True),
                        rd=[bcst, bga_], wr=[pb])
                    S.op("act", lambda e, ps=ps, g_=g_, d=d, c0=c0, cl=cl: e.activation(
                        out=g_[:, c0:c0 + cl], in_=ps[:, 0:cl], func=AF.Exp, scale=-1.0,
                        bias=nb[:, d * 4 + h:d * 4 + h + 1]), rd=[pb, bsv], wr=[bg_])
                S.op("act", lambda e, g_=g_: e.activation(out=g_[:, 0:sl], in_=g_[:, 0:sl], func=AF.Ln, bias=1.0),
                     rd=[bg_], wr=[bg_])
                S.op("dve", lambda e, g_=g_: e.tensor_scalar(g_[:, 0:sl], g_[:, 0:sl], -1.0 / 16.0, None, ALU.mult),
                     rd=[bg_], wr=[bg_])
                b_, bb_ = tb.next()
                S.op("dve", lambda e, b_=b_, g_=g_: e.tensor_tensor_scan(
                    b_[:, 0:sl], smask[:, 0:sl], g_[:, 0:sl], 0.0, ALU.mult, ALU.add),
                    rd=[bg_, bcst], wr=[bb_])
                b3 = b_[:, 0:sl].rearrange("p (n c) -> p n c", c=64)
                lastbc = b3[:, :, 63:64].to_broadcast([P, nch, 64])
                x1, bx1 = tx.next()
                x13 = x1[:, 0:sl].rearrange("p (n c) -> p n c", c=64)
                if d == 1:
                    S.op("dve", lambda e, x13=x13, lastbc=lastbc, b3=b3: e.tensor_tensor(
                        x13, lastbc, b3, ALU.subtract), rd=[bb_], wr=[bx1])
                    S.op("dve", lambda e, b_=b_, x1=x1, g_=g_: e.tensor_tensor(
                        b_[:, 0:sl], x1[:, 0:sl], g_[:, 0:sl], ALU.add), rd=[bx1, bg_], wr=[bb_])
                    edge = b3[:, :, 0:1].to_broadcast([P, nch, 64])
                    ecol = 0
                else:
                    edge = lastbc
                    ecol = 63
                S.op("act", lambda e, x1=x1, b_=b_: e.activation(out=x1[:, 0:sl], in_=b_[:, 0:sl], func=AF.Exp),
                     rd=[bb_], wr=[bx1])
                S.op("dve", lambda e, d=d, x13=x13, ecol=ecol, s0=s0, nch=nch: e.tensor_copy(
                    dec[d][:, s0 // 64:s0 // 64 + nch].rearrange("p (n o) -> p n o", o=1),
                    x13[:, :, ecol:ecol + 1]), rd=[bx1], wr=[bdec[d]])
                S.op("dve", lambda e, d=d, q_=q_, x1=x1, s0=s0: e.tensor_tensor(
                    qt[d][:, s0:s0 + sl], q_[:, 0:sl], x1[:, 0:sl], ALU.mult), rd=[bq_, bx1], wr=[bq[d]])
                x2, bx2 = tx.next()
                S.op("act", lambda e, x2=x2, b_=b_: e.activation(
                    out=x2[:, 0:sl], in_=b_[:, 0:sl], func=AF.Exp, scale=-1.0), rd=[bb_], wr=[bx2])
                S.op("pool", lambda e, d=d, k_=k_, x2=x2, s0=s0: e.tensor_tensor(
                    kt[d][:, s0:s0 + sl], k_[:, 0:sl], x2[:, 0:sl], ALU.mult), rd=[bk_, bx2], wr=[bq[d]])
                x3, bx3 = tx.next()
                x33 = x3[:, 0:sl].rearrange("p (n c) -> p n c", c=64)
                S.op("dve", lambda e, x33=x33, edge=edge, b3=b3: e.tensor_tensor(x33, edge, b3, ALU.subtract),
                     rd=[bb_], wr=[bx3])
                S.op("act", lambda e, x3=x3: e.activation(out=x3[:, 0:sl], in_=x3[:, 0:sl], func=AF.Exp),
                     rd=[bx3], wr=[bx3])
                kh_, bkh_ = tkh.next()
                S.op("pool", lambda e, kh_=kh_, k_=k_, x3=x3: e.tensor_tensor(
                    kh_[:, 0:sl], k_[:, 0:sl], x3[:, 0:sl], ALU.mult), rd=[bk_, bx3], wr=[bkh_])
                for ti in range(sl // P):
                    tp, btp = ptp.next()
                    S.op("pe", lambda e, tp=tp, kh_=kh_, ti=ti: e.transpose(
                        tp[:, 0:P], kh_[:, ti * P:(ti + 1) * P], idt[:]), rd=[bkh_, bid], wr=[btp])
                    S.op("act", lambda e, tp=tp, d=d, ti=ti, s0=s0: e.copy(
                        khtm[d][:, s0 // P + ti, :], tp[:, 0:P]), rd=[btp], wr=[bkh[d]])
    def head(h):
        for s0 in range(0, T, GSEG):
            seg(h, s0, min(GSEG, T - s0))
        for d in range(2):
            S.op("dve", lambda e, d=d: e.memset(Sst[d][:], 0.0), wr=[bS[d]])
            S.op("pool", lambda e, d=d: e.memset(Sbf[d][:], 0.0), wr=[bSb[d]])
        S.op("pool", lambda e: e.memset(acc[:], 0.0), wr=bacc + [ball])
        for step in range(NTL):
            for d in range(2):
                n = (tilesF if d == 0 else tilesB)[step]
                tsl = slice(n * P, (n + 1) * P)
                psa, bpa = pA.next()
                S.op("pe", lambda e, psa=psa, d=d, tsl=tsl: e.matmul(
                    psa[:, 0:P], kt[d][:, tsl], qt[d][:, tsl], start=True, stop=True), rd=[bq[d]], wr=[bpa])
                a_, ba_ = Asb.next()
                S.op("dve", lambda e, a_=a_, psa=psa, d=d: e.tensor_tensor(a_[:], psa[:, 0:P], bmask[:, d, :], ALU.mult),
                     rd=[bpa, bcst], wr=[ba_])
                vt, bvt = vtr.next()
                S.dma("sp", vt[:], gvv[:, n, h * 256:(h + 1) * 256], wr=[bvt])
                pot, bpo = pod[d], bpod[d]
                for ec in range(2):
                    S.op("pe", lambda e, pot=pot, vt=vt, a_=a_, ec=ec: e.matmul(
                        pot[ec][:, 0:P], vt[:, ec * P:(ec + 1) * P], a_[:], start=True, stop=False),
                        rd=[bvt, ba_], wr=[bpo])
                order = (0, 1) if d == 0 else (1, 0)
                for oi, hh in enumerate(order):
                    c = 2 * n + hh
                    csl = slice(c * 64, (c + 1) * 64)
                    for ec in range(2):
                        S.op("pe", lambda e, pot=pot, d=d, ec=ec, hh=hh, csl=csl, oi=oi: e.matmul(
                            pot[ec][:, hh * 64:(hh + 1) * 64], Sbf[d][:, ec * P:(ec + 1) * P], qt[d][:, csl],
                            start=False, stop=(oi == 1)), rd=[bSb[d], bq[d]], wr=[bpo])
                    pst, bps = pS.next()
                    S.op("pe", lambda e, pst=pst, d=d, n=n, hh=hh, vt=vt: e.matmul(
                        pst[:, 0:256], khtm[d][hh * 64:(hh + 1) * 64, n, :], vt[hh * 64:(hh + 1) * 64, :],
                        start=True, stop=True), rd=[bkh[d], bvt], wr=[bps])
                    S.op("dve", lambda e, pst=pst, d=d, c=c: e.scalar_tensor_tensor(
                        out=Sst[d][:], in0=Sst[d][:], scalar=dec[d][:, c:c + 1], in1=pst[:, 0:256],
                        op0=ALU.mult, op1=ALU.add), rd=[bps, bdec[d]], wr=[bS[d]])
                    S.op("dve", lambda e, d=d: e.tensor_copy(Sbf[d][:], Sst[d][:]), rd=[bS[d]], wr=[bSb[d]])
                for ec in range(2):
                    S.op("dve", lambda e, pot=pot, tsl=tsl, ec=ec: e.tensor_tensor(
                        acc[:, ec, tsl], pot[ec][:, 0:P], acc[:, ec, tsl], ALU.add), rd=[bpo], wr=[bacc[n]])
        S.op("dve", lambda e: e.engine_nop(), rd=bacc, wr=[ball])
        def otile(t0, tl):
            gt, bgt = grr.next()
            S.dma("sp", gt[:, :, 0:tl], grv[:, 2 * h:2 * h + 2, t0:t0 + tl], wr=[bgt])
            nq, bnq = nqr.next()
            rms_fm(S, acc[:, :, t0:t0 + tl], ball, 2, ones, bones, 11, sv, bsv, _RingView(sqr, tl, 3),
                   _RingView(pz, tl, 2), _RingView(rvr, tl, 2), nq[:, :, 0:tl], bnq)
            S.op("pool", lambda e: e.tensor_tensor(nq[:, :, 0:tl], nq[:, :, 0:tl], gt[:, :, 0:tl], ALU.mult),
                 rd=[bnq, bgt], wr=[bnq])
            S.dma("sp", yav[:, 2 * h:2 * h + 2, t0:t0 + tl], nq[:, :, 0:tl], rd=[bnq])

        for t0 in range(0, T, 512):
            otile(t0, min(512, T - t0))

    for h in range(4):
        head(h)


def host_gla_consts():
    sm = np.ones((P, GSEG), np.float32)
    sm[:, ::64] = 0.0
    bm = np.zeros((2, P, P), np.float32)
    for j in range(P):
        for i in range(P):
            if j // 64 == i // 64:
                bm[0, j, i] = 1.0 if j <= i else 0.0
                bm[1, j, i] = 1.0 if j > i else 0.0
    return dict(scanmask=sm, blockmask=bm)


def st_attn(S, dqT, dkT, dv, qnT, qrT, knT, krT, mv, diff_lam_l, smallv_l, lam_init, ybT, ycT, T, LT, ctx_q):
    NT = T // P
    sv = S.sbuf("sv", [P, 16], F32)
    bsv = Buf()
    S.dma("sp", sv[:], smallv_l, wr=[bsv])
    ones = S.sbuf("ones", [P, P], F32)
    onesb = S.sbuf("onesb", [P, P], BF16)
    bones = Buf()
    S.op("dve", lambda e: e.memset(ones[:], 1.0), wr=[bones])
    S.op("dve", lambda e: e.memset(onesb[:], 1.0), wr=[bones])
    dl = S.sbuf("dl", [P, 4, 64], F32)
    lm = S.sbuf("lm", [P, 8], F32)
    blm = Buf()
    S.dma("sp", dl[:].rearrange("p a b -> p (a b)"),
          diff_lam_l.rearrange("a b -> (a b)").rearrange("(o n) -> o n", o=1).partition_broadcast(P), wr=[blm])
    pr_ = S.sbuf("prd", [P, 2, 64], F32)
    S.op("dve", lambda e: e.tensor_tensor(pr_[:, 0, :], dl[:, 0, :], dl[:, 1, :], ALU.mult), rd=[blm], wr=[blm])
    S.op("dve", lambda e: e.tensor_tensor(pr_[:, 1, :], dl[:, 2, :], dl[:, 3, :], ALU.mult), rd=[blm], wr=[blm])
    S.op("dve", lambda e: e.reduce_sum(lm[:, 0:2], pr_[:], AX.X), rd=[blm], wr=[blm])
    S.op("act", lambda e: e.activation(out=lm[:, 2:4], in_=lm[:, 0:2], func=AF.Exp), rd=[blm], wr=[blm])
    S.op("dve", lambda e: e.tensor_tensor(lm[:, 4:5], lm[:, 3:4], lm[:, 2:3], ALU.subtract), rd=[blm], wr=[blm])
    S.op("dve", lambda e: e.tensor_scalar(lm[:, 4:5], lm[:, 4:5], -lam_init, None, ALU.add), rd=[blm], wr=[blm])
    S.op("dve", lambda e: e.tensor_scalar(sv[:, 14:15], sv[:, 13:14], 1.0 - lam_init, None, ALU.mult),
         rd=[bsv], wr=[bsv])

    kT = Ring(S, "kT", [P, T], BF16, 2)
    qT = Ring(S, "qT", [P, T], BF16, 2)
    k2 = Ring(S, "k2", [P, T], BF16, 1)
    q2 = Ring(S, "q2", [P, T], BF16, 2)
    tmpr = Ring(S, "ptsum", [P, 1024], BF16, 3)
    vv = Ring(S, "vv", [P, NT, P], BF16, 2)
    ptr = Ring(S, "pt", [P, 1024], BF16, 6)
    psr = Ring(S, "pss", [P, 1024], F32, 3, psum=True)
    accr = Ring(S, "acc", [P, 1024], F32, 2)
    pacc = {i: S.psum(f"pacc{i}", [P, 512], F32) for i in (0, 2)}
    bpacc = {i: Buf() for i in (0, 2)}
    rr = Ring(S, "rr", [P, 512], F32, 2)
    tt = Ring(S, "tt", [P, 1, 512], F32, 3)
    sqr = Ring(S, "sq", [P, 1, 512], F32, 1)
    rvr = Ring(S, "rv", [P, 512], F32, 1)
    outr = Ring(S, "ob", [P, 1, 512], BF16, 3)
    dvv = dv.rearrange("(n p) c -> p n c", p=P)
    mvv = mv.rearrange("(n p) c -> p n c", p=P)

    qtiles = [(t0, 512, 0, NT) for t0 in range(0, LT, 512)]
    if ctx_q:
        qtiles.append((LT, T - LT, LT // P, NT))

    k2t, bk2 = k2.next()
    S.dma("sp", k2t[0:64, :], krT, wr=[bk2])
    S.dma("sp", k2t[64:128, :], krT, wr=[bk2])

    def run_head(h, kind):
        kt_, bkt = kT.next()
        qt_, bqt = qT.next()
        vt_, bvt = vv.next()
        if kind == "diff":
            S.dma("sp", kt_[:], dkT[h * P:(h + 1) * P, :], wr=[bkt])
            S.dma("sp", qt_[:], dqT[h * P:(h + 1) * P, :], wr=[bqt])
            for n0 in range(0, NT, 8):
                n1 = min(NT, n0 + 8)
                S.dma("sp", vt_[:, n0:n1, :], dvv[:, n0:n1, h * P:(h + 1) * P], wr=[bvt])
            nm = 2
        else:
            S.dma("sp", kt_[:], knT[h * P:(h + 1) * P, :], wr=[bkt])
            S.dma("sp", qt_[:], qnT[h * P:(h + 1) * P, :], wr=[bqt])
            for n0 in range(0, NT, 8):
                n1 = min(NT, n0 + 8)
                S.dma("sp", vt_[:, n0:n1, :], mvv[:, n0:n1, h * P:(h + 1) * P], wr=[bvt])
            q2t, bq2 = q2.next()
            S.dma("sp", q2t[0:64, :], qrT[h * 64:(h + 1) * 64, :], wr=[bq2])
            S.dma("sp", q2t[64:128, :], qrT[h * 64:(h + 1) * 64, :], wr=[bq2])
            nm = 1
        def qtile(t0, tl, kb0, kb1):
            qs_ = slice(t0, t0 + tl)
            if kind == "diff":
                units = [((kb, 0), (kb, 1)) for kb in range(kb0, kb1)]
            else:
                units = [((kb, 0), (kb + 1, 0)) for kb in range(kb0, kb1, 2)]
            LA = 3
            pend = {}
            held = [None]
            ac, bac = accr.next()
            ac3 = ac[:].rearrange("p (a b) -> p a b", b=512)[:, :, 0:tl]

            def emit_score(u):
                ps, pb = psr.next()
                if kind == "diff":
                    for hf, (kb, m) in enumerate(units[u]):
                        ks_ = slice(kb * P, (kb + 1) * P)
                        o = ps[:, hf * 512:hf * 512 + tl]
                        S.op("pe", lambda e, o=o, m=m, ks_=ks_: e.matmul(
                            o, kt_[m * 64:(m + 1) * 64, ks_], qt_[m * 64:(m + 1) * 64, qs_],
                            start=True, stop=True), rd=[bkt, bqt], wr=[pb])
                else:
                    for hf, (kb, m) in enumerate(units[u]):
                        ks_ = slice(kb * P, (kb + 1) * P)
                        o = ps[:, hf * 512:hf * 512 + tl]
                        S.op("pe", lambda e, o=o, ks_=ks_: e.matmul(
                            o, kt_[:, ks_], qt_[:, qs_], start=True, stop=False), rd=[bkt, bqt], wr=[pb])
                    for hf, (kb, m) in enumerate(units[u]):
                        ks_ = slice(kb * P, (kb + 1) * P)
                        o = ps[:, hf * 512:hf * 512 + tl]
                        rs = slice(hf * 64, (hf + 1) * 64)
                        S.op("pe", lambda e, o=o, ks_=ks_, rs=rs: e.matmul(
                            o, k2t[rs, ks_], q2t[rs, qs_], start=False, stop=True), rd=[bk2, bq2], wr=[pb])
                pend[u] = (ps, pb)

            def emit_rest(u):
                ps, pb = pend.pop(u)
                pt, bpt = ptr.next()
                ps3 = ps[:].rearrange("p (a b) -> p a b", b=512)[:, :, 0:tl]
                pt3 = pt[:].rearrange("p (a b) -> p a b", b=512)[:, :, 0:tl]
                S.op("act", lambda e: e.activation(out=pt3, in_=ps3, func=AF.Exp), rd=[pb], wr=[bpt])
                nu = len(units)
                if u % 2 == 0 and u + 1 < nu:
                    held[0] = (pt3, bpt)
                else:
                    if u % 2 == 1:
                        p0, bp0 = held[0]
                        tm, btm = tmpr.next()
                        tm3 = tm[:].rearrange("p (a b) -> p a b", b=512)[:, :, 0:tl]
                        peng = "pool" if (u // 2) % 2 == 1 else "dve"
                        S.op(peng, lambda e: e.tensor_tensor(tm3, p0, pt3, ALU.add), rd=[bp0, bpt], wr=[btm])
                        src, bsrc = tm3, btm
                    else:
                        src, bsrc = pt3, bpt
                    if u <= 1:
                        S.op("dve", lambda e: e.tensor_copy(ac3, src), rd=[bsrc], wr=[bac])
                    else:
                        S.op("dve", lambda e: e.tensor_tensor(ac3, ac3, src, ALU.add), rd=[bsrc, bac], wr=[bac])
                for hf, (kb, m) in enumerate(units[u]):
                    a = 2 * m if kind == "diff" else 2 * hf
                    first, last = (u == 0), (u == nu - 1)
                    S.op("pe", lambda e, hf=hf, kb=kb, a=a, first=first, last=last: e.matmul(
                        pacc[a][:, 0:tl], vt_[:, kb, :], pt[:, hf * 512:hf * 512 + tl], start=first, stop=last),
                        rd=[bvt, bpt], wr=[bpacc[a]])

            for u in range(min(LA, len(units))):
                emit_score(u)
            for u in range(len(units)):
                emit_rest(u)
                if u + LA < len(units):
                    emit_score(u + LA)
            pz, bpz = psr.next()
            if kind == "diff":
                for m in range(2):
                    S.op("pe", lambda e, m=m: e.matmul(pz[:, m * 512:m * 512 + tl], ones[:], ac[:, m * 512:m * 512 + tl],
                                                       start=True, stop=True), rd=[bones, bac], wr=[bpz])
            else:
                for hf in range(2):
                    S.op("pe", lambda e, hf=hf: e.matmul(pz[:, 0:tl], ones[:], ac[:, hf * 512:hf * 512 + tl],
                                                         start=(hf == 0), stop=(hf == 1)), rd=[bones, bac], wr=[bpz])
            ob, bob = outr.next()
            r0, br0 = rr.next()
            S.op("dve", lambda e, r0=r0: e.reciprocal(r0[:, 0:tl], pz[:, 0:tl]), rd=[bpz], wr=[br0])
            if kind == "diff":
                r1, br1 = rr.next()
                S.op("dve", lambda e, r1=r1: e.reciprocal(r1[:, 0:tl], pz[:, 512:512 + tl]), rd=[bpz], wr=[br1])
                ta, bta = tt.next()
                S.op("dve", lambda e, ta=ta, r0=r0: e.tensor_tensor(ta[:, 0, 0:tl], pacc[0][:, 0:tl], r0[:, 0:tl], ALU.mult),
                     rd=[bpacc[0], br0], wr=[bta])
                tb_, btb = tt.next()
                S.op("dve", lambda e, tb_=tb_, r1=r1: e.tensor_tensor(tb_[:, 0, 0:tl], pacc[2][:, 0:tl], r1[:, 0:tl], ALU.mult),
                     rd=[bpacc[2], br1], wr=[btb])
                S.op("dve", lambda e, ta=ta, tb_=tb_: e.scalar_tensor_tensor(
                    out=ta[:, 0, 0:tl], in0=tb_[:, 0, 0:tl], scalar=lm[:, 4:5], in1=ta[:, 0, 0:tl],
                    op0=ALU.mult, op1=ALU.add), rd=[btb, blm], wr=[bta])
                rms_fm(S, ta[:, :, 0:tl], bta, 1, ones, bones, 14, sv, bsv, sqr_v(sqr, tl), psr_v(psr, tl), rvr_v(rvr, tl),
                       ob[:, :, 0:tl], bob)
                S.dma("sp", ybT[h * P:(h + 1) * P, qs_], ob[:, 0, 0:tl], rd=[bob])
            else:
                ta, bta = tt.next()
                S.op("dve", lambda e, ta=ta, r0=r0: e.tensor_tensor(ta[:, 0, 0:tl], pacc[0][:, 0:tl], r0[:, 0:tl], ALU.mult),
                     rd=[bpacc[0], br0], wr=[bta])
                tb_, btb = tt.next()
                S.op("dve", lambda e, tb_=tb_, r0=r0: e.tensor_tensor(tb_[:, 0, 0:tl], pacc[2][:, 0:tl], r0[:, 0:tl], ALU.mult),
                     rd=[bpacc[2], br0], wr=[btb])
                S.op("pool", lambda e, ob=ob, ta=ta, tb_=tb_: e.tensor_tensor(ob[:, 0, 0:tl], ta[:, 0, 0:tl], tb_[:, 0, 0:tl], ALU.add),
                     rd=[bta, btb], wr=[bob])
                S.dma("sp", ycT[h * P:(h + 1) * P, qs_], ob[:, 0, 0:tl], rd=[bob])

        for qt4 in qtiles:
            qtile(*qt4)

    for h in range(8):
        run_head(h, "diff")
    for h in range(8):
        run_head(h, "mla")


class _RingView:
    def __init__(self, ring, tl, nd):
        self.ring, self.tl, self.nd = ring, tl, nd

    def next(self):
        t, b = self.ring.next()
        if self.nd == 3:
            return t[:, :, 0:self.tl], b
        return t[:, 0:self.tl], b


def sqr_v(r, tl):
    return _RingView(r, tl, 3)


def psr_v(r, tl):
    return _RingView(r, tl, 2)


def rvr_v(r, tl):
    return _RingView(r, tl, 2)


def resid_ln(S, ps2, bps2, xs_rows, out_rows, gbc, bgbc, lng, blng, lnb, blnb, R):
    xt, bx = R["x"].next()
    S.dma("sp", xt[:], xs_rows, wr=[bx])
    u, bu = R["u"].next()
    for hf in range(2):
        S.op("dve", lambda e, hf=hf: e.tensor_tensor(
            u[:, hf * 512:(hf + 1) * 512], ps2[hf][:], gbc[:, hf * 512:(hf + 1) * 512], ALU.mult),
            rd=[bps2[hf], bgbc], wr=[bu])
    S.op("dve", lambda e: e.scalar_tensor_tensor(out=u[:], in0=xt[:], scalar=ALPHA, in1=u[:],
                                                  op0=ALU.mult, op1=ALU.add), rd=[bx, bu], wr=[bu])
    st, bst = R["st"].next()
    ln_stats(S, u, bu, st, bst)
    xn, bn = R["xn"].next()
    S.op("act", lambda e: e.activation(out=xn[:], in_=u[:], func=AF.Identity, scale=st[:, 14:15], bias=st[:, 15:16]),
         rd=[bu, bst], wr=[bn])
    S.op("dve", lambda e: e.tensor_tensor(xn[:], xn[:], lng[:], ALU.mult), rd=[bn, blng], wr=[bn])
    S.op("pool", lambda e: e.tensor_tensor(xn[:], xn[:], lnb[:], ALU.add), rd=[bn, blnb], wr=[bn])
    S.dma("sp", out_rows, xn[:], rd=[bn])


def resid_rings(S):
    return dict(x=Ring(S, "rx", [P, D], F32, 1), u=Ring(S, "ru", [P, D], F32, 1),
                st=Ring(S, "rst", [P, 16], F32, 2), xn=Ring(S, "rxn", [P, D], F32, 1))


def st_merge(S, yT3, gatesT, w_branch, w_out, modv_l, ln_g, ln_b, xs, xo, T, LT, Tproc):
    wb = S.sbuf("wb", [P, 3, 8, D], BF16)
    wo = S.sbuf("wo", [P, 8, D], BF16)
    bw = Buf()
    cring = cast_ring(S)
    for i in range(3):
        wv = w_branch[i].rearrange("(kc p) n -> p kc n", p=P)
        for kc in range(8):
            load_cast(S, cring, wb[:, i, kc, :], wv[:, kc, :], P, D, bw)
    wv = w_out.rearrange("(kc p) n -> p kc n", p=P)
    for kc in range(8):
        load_cast(S, cring, wo[:, kc, :], wv[:, kc, :], P, D, bw)
    gb = [load_bc(S, f"g1{j}", modv_l[j:j + 1, 2 * D:3 * D]) for j in range(2)]
    lng, blng = load_bc(S, "lng", ln_g)
    lnb, blnb = load_bc(S, "lnb", ln_b)
    yr = Ring(S, "yt", [P, 8, 512], BF16, 2)
    gr_ = Ring(S, "gt", [P, 24, 512], BF16, 1)
    yacc = S.sbuf("yacc", [P, 8, 512], F32)
    byacc = Buf()
    ybf = S.sbuf("ybf", [P, 8, 512], BF16)
    bybf = Buf()
    tmp = Ring(S, "tmp", [P, 512], F32, 2)
    pr = Ring(S, "ps", [P, 512], F32, 4, psum=True)
    pr2 = Ring(S, "ps2", [P, 512], F32, 4, psum=True)
    RR = resid_rings(S)
    gv_ = gatesT.rearrange("(c p) t -> p c t", p=P)
    def tile(t0, tl):
        W = slice(0, tl)
        gt, bgt = gr_.next()
        S.dma("sp", gt[:, :, W], gv_[:, :, t0:t0 + tl], wr=[bgt])
        for i in range(3):
            yt, byt = yr.next()
            S.dma("sp", yt[:, :, W], yT3[i].rearrange("(c p) t -> p c t", p=P)[:, :, t0:t0 + tl], wr=[byt])
            for n in range(8):
                ps, pb = pr.next()
                for kc in range(8):
                    S.op("pe", lambda e, ps=ps, i=i, kc=kc, n=n, yt=yt: e.matmul(
                        ps[:, W], wb[:, i, kc, n * P:(n + 1) * P], yt[:, kc, W], start=(kc == 0), stop=(kc == 7)),
                        rd=[bw, byt], wr=[pb])
                if i == 0:
                    S.op("dve", lambda e, ps=ps, n=n: e.tensor_tensor(
                        yacc[:, n, W], ps[:, W], gt[:, n, W], ALU.mult), rd=[pb, bgt], wr=[byacc])
                else:
                    tm, btm = tmp.next()
                    S.op("dve", lambda e, ps=ps, n=n, tm=tm, i=i: e.tensor_tensor(
                        tm[:, W], ps[:, W], gt[:, i * 8 + n, W], ALU.mult), rd=[pb, bgt], wr=[btm])
                    if i == 1:
                        S.op("pool", lambda e, n=n, tm=tm: e.tensor_tensor(
                            yacc[:, n, W], yacc[:, n, W], tm[:, W], ALU.add), rd=[btm, byacc], wr=[byacc])
                    else:
                        S.op("pool", lambda e, n=n, tm=tm: e.tensor_tensor(
                            ybf[:, n, W], yacc[:, n, W], tm[:, W], ALU.add), rd=[btm, byacc], wr=[bybf])
        for s in range(tl // P):
            r0 = t0 + s * P
            ps2 = []
            bps2 = []
            for hf in range(2):
                ps, pb = pr2.next()
                for kc in range(8):
                    S.op("pe", lambda e, ps=ps, kc=kc, s=s, hf=hf: e.matmul(
                        ps[:], ybf[:, kc, s * P:(s + 1) * P], wo[:, kc, hf * 512:(hf + 1) * 512],
                        start=(kc == 0), stop=(kc == 7)), rd=[bw, bybf], wr=[pb])
                ps2.append(ps)
                bps2.append(pb)
            j = 0 if r0 < LT else 1
            resid_ln(S, ps2, bps2, xs[r0:r0 + P, :], xo[r0:r0 + P, :], gb[j][0], gb[j][1], lng, blng, lnb, blnb, RR)

    for t0 in range(0, Tproc, 512):
        tile(t0, min(512, Tproc - t0))


def st_ffn(S, h2T, w1d, w2d, modv_l, ln_g, ln_b, xs, xo, T, LT, Tproc):
    NJ = FH // P
    w1 = S.sbuf("w1", [P, 8, 2 * FH], BF16)
    w2 = S.sbuf("w2", [P, NJ, D], BF16)
    bw = Buf()
    cring = Ring(S, "cst", [P, CAST_W], F32, 1)
    wv = w1d.rearrange("(kc p) n -> p kc n", p=P)
    for kc in range(8):
        load_cast(S, cring, w1[:, kc, :], wv[:, kc, :], P, 2 * FH, bw)
    wv = w2d.rearrange("(j p) n -> p j n", p=P)
    for j in range(NJ):
        load_cast(S, cring, w2[:, j, :], wv[:, j, :], P, D, bw)
    gb = [load_bc(S, f"g2{j}", modv_l[j:j + 1, 5 * D:6 * D]) for j in range(2)]
    lng, blng = load_bc(S, "lng", ln_g)
    lnb, blnb = load_bc(S, "lnb", ln_b)
    hr = Ring(S, "h2", [P, 8, 512], BF16, 1)
    hmid = S.sbuf("hmid", [P, NJ, 512], BF16)
    bhm = Buf()
    ar = Ring(S, "ar", [P, 512], BF16, 2)
    pr = Ring(S, "ps", [P, 512], F32, 4, psum=True)
    pr2 = Ring(S, "ps2", [P, 512], F32, 4, psum=True)
    RR = resid_rings(S)
    hv = h2T.rearrange("(kc p) t -> p kc t", p=P)
    def tile(t0, tl):
        W = slice(0, tl)
        ht, bht = hr.next()
        S.dma("sp", ht[:, :, W], hv[:, :, t0:t0 + tl], wr=[bht])
        for j in range(NJ):
            pg, bpg = pr.next()
            pu, bpu = pr.next()
            for kc in range(8):
                S.op("pe", lambda e, pg=pg, kc=kc, j=j: e.matmul(
                    pg[:, W], w1[:, kc, j * P:(j + 1) * P], ht[:, kc, W], start=(kc == 0), stop=(kc == 7)),
                    rd=[bw, bht], wr=[bpg])
            for kc in range(8):
                S.op("pe", lambda e, pu=pu, kc=kc, j=j: e.matmul(
                    pu[:, W], w1[:, kc, FH + j * P:FH + (j + 1) * P], ht[:, kc, W], start=(kc == 0), stop=(kc == 7)),
                    rd=[bw, bht], wr=[bpu])
            a_, ba_ = ar.next()
            S.op("act", lambda e, a_=a_, pg=pg: e.activation(out=a_[:, W], in_=pg[:, W], func=AF.Silu),
                 rd=[bpg], wr=[ba_])
            S.op("dve", lambda e, a_=a_, pu=pu, j=j: e.tensor_tensor(hmid[:, j, W], pu[:, W], a_[:, W], ALU.mult),
                 rd=[bpu, ba_], wr=[bhm])
        for s in range(tl // P):
            r0 = t0 + s * P
            ps2 = []
            bps2 = []
            for hf in range(2):
                ps, pb = pr2.next()
                for j in range(NJ):
                    S.op("pe", lambda e, ps=ps, j=j, s=s, hf=hf: e.matmul(
                        ps[:], hmid[:, j, s * P:(s + 1) * P], w2[:, j, hf * 512:(hf + 1) * 512],
                        start=(j == 0), stop=(j == NJ - 1)), rd=[bw, bhm], wr=[pb])
                ps2.append(ps)
                bps2.append(pb)
            jj = 0 if r0 < LT else 1
            dst = xo[r0:r0 + P, :]
            resid_ln(S, ps2, bps2, xs[r0:r0 + P, :], dst, gb[jj][0], gb[jj][1], lng, blng, lnb, blnb, RR)

    for t0 in range(0, Tproc, 512):
        tile(t0, min(512, Tproc - t0))


def tensor_specs(LT):
    T = LT + CT
    sp = dict(
        xin=([T, D], F32), cin=([P, 16], F32), w_mod=([DEPTH, D, 6 * D], F32), b_mod=([DEPTH, 6 * D], F32),
        w_in=([DEPTH, D, NIN], F32), gla_w_a2=([DEPTH, 2, 16, 512], F32), diff_lam=([DEPTH, 4, 64], F32),
        mla_w_uq=([DEPTH, 256, 1536], F32), mla_w_ukv=([DEPTH, 128, 2048], F32),
        w_branch=([DEPTH, 3, D, D], F32), w_out=([DEPTH, D, D], F32), ln1_g=([DEPTH, D], F32),
        ln1_b=([DEPTH, D], F32), ffn_w_in=([DEPTH, D, 2 * FH], F32), ffn_w_out=([DEPTH, FH, D], F32),
        ln2_g=([DEPTH, D], F32), ln2_b=([DEPTH, D], F32), smallv=([DEPTH, P, 16], F32),
        cosT=([P, T], F32), sinT=([P, T], F32), perm=([P, P], BF16), ident=([P, P], BF16),
        scanmask=([P, GSEG], F32), blockmask=([2, P, P], F32),
        modv=([DEPTH, 2, 6 * D], F32), hT=([D, T], BF16), h2T=([D, T], BF16),
        gqT=([512, T], F32), gkT=([512, T], F32), gv=([T, D], BF16), grT=([D, T], BF16), gaT=([32, T], F32),
        gatesT=([3 * D, T], BF16), dqT=([D, T], BF16), dkT=([D, T], BF16), dv=([T, D], BF16),
        mqT=([256, T], F32), mkvT=([128, T], F32), krT=([64, T], BF16),
        qnT=([D, T], BF16), qrT=([512, T], BF16), knT=([D, T], BF16), mv=([T, D], BF16),
        yaT=([D, T], BF16), ybT=([D, T], BF16), ycT=([D, T], BF16),
        xs1=([T, D], F32), xs2=([T, D], F32), out=([LT, D], F32),
    )
    return sp


HOST_INPUTS = ("xin", "cin", "w_mod", "b_mod", "w_in", "gla_w_a2", "diff_lam", "mla_w_uq", "mla_w_ukv", "w_branch",
               "w_out", "ln1_g", "ln1_b", "ffn_w_in", "ffn_w_out", "ln2_g", "ln2_b", "smallv", "cosT", "sinT",
               "perm", "ident", "scanmask", "blockmask")
FEATS = ("gqT", "gkT", "gv", "grT", "gaT", "gatesT", "dqT", "dkT", "dv", "mqT", "mkvT", "krT")


def stage_plan(LT):
    T = LT + CT
    plan = [dict(name="mod", r=["cin", "w_mod", "b_mod"], w=["modv"],
                 fn=lambda S, t: st_mod(S, t["cin"], t["w_mod"], t["b_mod"], t["modv"]))]
    for l in range(DEPTH):
        last = l == DEPTH - 1
        lam_init = 0.8 - 0.6 * math.exp(-0.3 * l)
        Tproc = LT if last else T
        xsrc = "xin" if l == 0 else "xs2"
        xdst = "out" if last else "xs2"

        def add(name, r, w, fn):
            plan.append(dict(name=f"L{l}.{name}", r=r, w=w, fn=fn))

        add("modulate1", [xsrc, "modv", "ident"], ["hT"],
            lambda S, t, l=l, xsrc=xsrc: st_modulate(S, t[xsrc], t["modv"][l], D, 0, t["hT"], t["ident"], T, LT))
        for pi in range(2):
            outs = [j for j in (FEATS[0:6] if pi == 0 else FEATS[6:12])]
            add(f"proj{pi}", ["hT", "w_in", "cosT", "sinT", "perm"], outs,
                lambda S, t, l=l, pi=pi: st_proj(S, t["hT"], 8, T, proj_jobs(t["w_in"][l], t)[pi],
                                                 t["cosT"], t["sinT"], t["perm"]))
        add("mla_up", ["mqT", "mkvT", "mla_w_uq", "mla_w_ukv", "smallv", "cosT", "sinT", "perm"],
            ["qnT", "qrT", "knT", "mv"],
            lambda S, t, l=l: st_mla_up(S, t["mqT"], t["mkvT"], t["mla_w_uq"][l], t["mla_w_ukv"][l], t["smallv"][l],
                                        t["qnT"], t["qrT"], t["knT"], t["mv"], t["cosT"], t["sinT"], t["perm"], T))
        add("gla", ["gqT", "gkT", "gv", "gaT", "grT", "gla_w_a2", "smallv", "ident", "scanmask", "blockmask"], ["yaT"],
            lambda S, t, l=l: st_gla(S, t["gqT"], t["gkT"], t["gv"], t["gaT"], t["grT"], t["gla_w_a2"][l],
                                     t["smallv"][l], t["yaT"], t["ident"], t["scanmask"], t["blockmask"], T, LT))
        add("attn", ["dqT", "dkT", "dv", "qnT", "qrT", "knT", "krT", "mv", "diff_lam", "smallv"], ["ybT", "ycT"],
            lambda S, t, l=l, lam_init=lam_init, last=last: st_attn(
                S, t["dqT"], t["dkT"], t["dv"], t["qnT"], t["qrT"], t["knT"], t["krT"], t["mv"], t["diff_lam"][l],
                t["smallv"][l], lam_init, t["ybT"], t["ycT"], T, LT, not last))
        add("merge", ["yaT", "ybT", "ycT", "gatesT", "w_branch", "w_out", "modv", "ln1_g", "ln1_b", xsrc], ["xs1"],
            lambda S, t, l=l, xsrc=xsrc, Tproc=Tproc: st_merge(
                S, [t["yaT"], t["ybT"], t["ycT"]], t["gatesT"], t["w_branch"][l], t["w_out"][l], t["modv"][l],
                t["ln1_g"][l:l + 1, :], t["ln1_b"][l:l + 1, :], t[xsrc], t["xs1"], T, LT, Tproc))
        add("modulate2", ["xs1", "modv", "ident"], ["h2T"],
            lambda S, t, l=l, Tproc=Tproc: st_modulate(S, t["xs1"], t["modv"][l], 4 * D, 3 * D, t["h2T"], t["ident"],
                                                       Tproc, LT))
        add("ffn", ["h2T", "ffn_w_in", "ffn_w_out", "modv", "ln2_g", "ln2_b", "xs1"], [xdst],
            lambda S, t, l=l, xdst=xdst, Tproc=Tproc: st_ffn(
                S, t["h2T"], t["ffn_w_in"][l], t["ffn_w_out"][l], t["modv"][l], t["ln2_g"][l:l + 1, :],
                t["ln2_b"][l:l + 1, :], t["xs1"], t[xdst], T, LT, Tproc))
    return plan


def default_groups(nstages):
    g = os.environ.get("K_GROUPS")
    if g:
        out = []
        for part in g.split(","):
            a, b = part.split("-") if "-" in part else (part, part)
            out.append(list(range(int(a), int(b) + 1)))
        return out
    return GROUPS(nstages)


def GROUPS(nstages):
    return [list(range(nstages))]


def build_launches(LT):
    specs = tensor_specs(LT)
    plan = stage_plan(LT)
    groups = default_groups(len(plan))
    launches = []
    for gi, g in enumerate(groups):
        later_reads = set()
        for g2 in groups[gi + 1:]:
            for si in g2:
                later_reads.update(plan[si]["r"])
        written, ext_in = set(), []
        for si in g:
            for n in plan[si]["r"]:
                if n not in written and n not in ext_in:
                    ext_in.append(n)
            written.update(plan[si]["w"])
        ext_out = [n for n in sorted(written) if n in later_reads or n == "out"]
        assert not (set(ext_in) & set(ext_out)), (ext_in, ext_out)
        nc = bass.Bass("TRN2", target_bir_lowering=False)
        t = {}
        names = list(ext_in) + [n for n in sorted(written) if n not in ext_in]
        for n in names:
            shape, dt = specs[n]
            kind = "ExternalInput" if n in ext_in else ("ExternalOutput" if n in ext_out else "Internal")
            t[n] = nc.dram_tensor(n, list(shape), dt, kind=kind).ap()
        for si in g:
            stage(nc, plan[si]["fn"], t)
        launches.append((nc, ext_in, ext_out))
    return launches


def host_inputs(inp, b, LT):
    T = LT + CT
    c = np.asarray(inp["c"][b], np.float32)
    cc = np.asarray(inp["c_ctx"], np.float32)
    cin = np.stack([c.reshape(8, P).T, cc.reshape(8, P).T], axis=-1).reshape(P, 16).astype(np.float32)
    xin = np.concatenate([np.asarray(inp["x"][b, :LT], np.float32), np.asarray(inp["ctx"][b], np.float32)], 0)
    return dict(xin=np.ascontiguousarray(xin), cin=cin)


_PROG = {}


def kernel(**inputs):
    LT = inputs["x"].shape[1]
    nb = inputs["x"].shape[0]
    T = LT + CT
    if LT not in _PROG:
        _PROG[LT] = build_launches(LT)
    launches = _PROG[LT]
    hc = host_consts(LT, T)
    gc = host_gla_consts()
    shared = dict(smallv=host_smallv(inputs), cosT=hc["cosT"], sinT=hc["sinT"], perm=hc["perm"], ident=hc["ident"],
                  scanmask=gc["scanmask"], blockmask=gc["blockmask"])
    for k in ("w_mod", "b_mod", "w_in", "gla_w_a2", "diff_lam", "mla_w_uq", "mla_w_ukv", "w_branch", "w_out",
              "ln1_g", "ln1_b", "ffn_w_in", "ffn_w_out", "ln2_g", "ln2_b"):
        shared[k] = np.ascontiguousarray(np.asarray(inputs[k], np.float32))
    percore = [host_inputs(inputs, b, LT) for b in range(nb)]
    for li, (nc, ext_in, ext_out) in enumerate(launches):
        if os.environ.get("K_VERBOSE"):
            print(f"[kernel] launch {li}: in={ext_in} out={ext_out}", flush=True)
        in_maps = [{n: (percore[b][n] if n in percore[b] else shared[n]) for n in ext_in} for b in range(nb)]
        res = run_bass_kernel_spmd(nc, in_maps, core_ids=list(range(nb)))
        for b in range(nb):
            for n in ext_out:
                percore[b][n] = res.results[b][n]
    return np.stack([np.asarray(percore[b]["out"], np.float32) for b in range(nb)], 0)
```

```python
from contextlib import ExitStack
import math
import numpy as np
import ml_dtypes
import concourse.bass as bass
import concourse.mybir as mybir
from concourse.bass_utils import run_bass_kernel_spmd

F32 = mybir.dt.float32
BF16 = mybir.dt.bfloat16
AF = mybir.ActivationFunctionType
ALU = mybir.AluOpType
AX = mybir.AxisListType
P = 128

D = 1024
DEPTH = 2
CT = 256
NIN = 9696
EPS = 1e-6
ALPHA = (2 * DEPTH) ** 0.25
FH = 2816
MLA_SCALE = (128 + 64) ** -0.5

ENGS = ("pe", "act", "dve", "pool", "sp")
SEM_LIM = 30000
_SEM_POOL = {}


class Buf:
    __slots__ = ("w", "r")

    def __init__(self):
        self.w = None
        self.r = []


class Op:
    __slots__ = ("eng", "fn", "deps", "dma", "flag", "sem", "target")

    def __init__(self, eng, fn, deps, dma):
        self.eng = eng
        self.fn = fn
        self.deps = deps
        self.dma = dma
        self.flag = False
        self.sem = None
        self.target = 0


class Sched:
    N_DMA_SEMS = 10
    _uid = 0

    def __init__(self, nc, es):
        self.nc = nc
        self.es = es
        self.ops = {e: [] for e in ENGS}

    def sbuf(self, name, shape, dt):
        Sched._uid += 1
        return self.es.enter_context(self.nc.sbuf_tensor(f"{name}_{Sched._uid}", list(shape), dt))

    def psum(self, name, shape, dt):
        Sched._uid += 1
        return self.es.enter_context(self.nc.psum_tensor(f"{name}_{Sched._uid}", list(shape), dt))

    def op(self, eng, fn, rd=(), wr=(), dma=False, deps=()):
        dl = [d for d in deps if d is not None]
        for b in rd:
            if b.w is not None:
                dl.append(b.w)
        for b in wr:
            if b.w is not None:
                dl.append(b.w)
            dl.extend(b.r)
        o = Op(eng, fn, None, dma)
        seen = set()
        dd = []
        for d in dl:
            if d is o or id(d) in seen:
                continue
            if (not d.dma) and (not dma) and d.eng == "pe" and eng == "pe":
                continue
            seen.add(id(d))
            dd.append(d)
            if not d.dma:
                d.flag = True
        o.deps = dd
        self.ops[eng].append(o)
        for b in rd:
            b.r.append(o)
        for b in wr:
            b.w = o
            b.r = []
        return o

    def dma(self, eng, out, in_, rd=(), wr=(), deps=(), **kw):
        return self.op(eng, lambda e: e.dma_start(out=out, in_=in_, **kw), rd, wr, dma=True, deps=deps)

    def emit(self):
        nc = self.nc
        es = self.es
        Sched._uid += 1
        u = Sched._uid
        dq = ("sp", "act", "pool")
        handles = []
        pool = _SEM_POOL.setdefault(id(nc), {})

        def new_sem(name):
            if name not in pool:
                pool[name] = nc.alloc_semaphore(name=name)
            h = pool[name]
            handles.append(h)
            return h

        dsem = {e: [new_sem(f"sd_{e}{i}") for i in range(self.N_DMA_SEMS)] for e in dq}
        duse = {e: [0] * self.N_DMA_SEMS for e in dq}
        esems = {}
        efinal = []
        for e in ENGS:
            c = 0
            k = 0
            cur = None
            for o in self.ops[e]:
                if o.dma:
                    slot = k % self.N_DMA_SEMS
                    k += 1
                    duse[e][slot] += 1
                    o.sem = dsem[e][slot]
                    o.target = 16 * duse[e][slot]
                elif o.flag:
                    if c % SEM_LIM == 0:
                        cur = new_sem(f"se_{e}{c // SEM_LIM}")
                        esems.setdefault(e, []).append(cur)
                    c += 1
                    o.sem = cur
                    o.target = (c - 1) % SEM_LIM + 1
            if c:
                efinal.append((cur, (c - 1) % SEM_LIM + 1))
        finals = []
        for e in dq:
            for slot in range(self.N_DMA_SEMS):
                if duse[e][slot]:
                    finals.append((dsem[e][slot], 16 * duse[e][slot]))

        def run(e, eng):
            waited = {}
            for o in self.ops[e]:
                needs = [(d.sem, d.target) for d in o.deps]
                if o.dma and o.target > 16:
                    needs.append((o.sem, o.target - 16))
                for sem, tgt in needs:
                    key = id(sem)
                    if waited.get(key, 0) >= tgt:
                        continue
                    waited[key] = tgt
                    eng.wait_ge(sem, tgt)
                ins = o.fn(eng)
                if o.dma:
                    ins.then_inc(o.sem, 16)
                elif o.flag:
                    ins.then_inc(o.sem, 1)
            for sem, tgt in finals + efinal:
                if waited.get(id(sem), 0) < tgt:
                    eng.wait_ge(sem, tgt)

        nc.all_engine_barrier()
        for h in handles:
            nc.gpsimd.sem_clear(h)
        nc.all_engine_barrier()
        self._handles = handles

        with nc.Block() as block:
            @block.tensor
            def _(eng):
                run("pe", eng)

            @block.scalar
            def _(eng):
                run("act", eng)

            @block.vector
            def _(eng):
                run("dve", eng)

            @block.gpsimd
            def _(eng):
                run("pool", eng)

            @block.sync
            def _(eng):
                run("sp", eng)


import os
_STAGE_CNT = [0]


def stage(nc, build, *a, **k):
    with ExitStack() as es:
        S = Sched(nc, es)
        build(S, *a, **k)
        S.emit()


class Ring:
    def __init__(self, S, name, shape, dt, n, psum=False):
        mk = S.psum if psum else S.sbuf
        self.t = [mk(f"{name}{i}", shape, dt) for i in range(n)]
        self.b = [Buf() for _ in range(n)]
        self.i = -1

    def next(self):
        self.i = (self.i + 1) % len(self.t)
        return self.t[self.i], self.b[self.i]


CAST_W = 2048


def cast_ring(S):
    return Ring(S, "cst", [P, CAST_W], F32, 2)


def load_cast(S, ring, dst, src, rows, ncol, wb, shape3=None):
    for c0 in range(0, ncol, CAST_W):
        cl = min(CAST_W, ncol - c0)
        stg, bs = ring.next()
        S.dma("sp", stg[:rows, 0:cl], src[:, c0:c0 + cl], wr=[bs])
        S.op("pool", lambda e, stg=stg, c0=c0, cl=cl: e.tensor_copy(dst[:, c0:c0 + cl], stg[:rows, 0:cl]),
             rd=[bs], wr=[wb])


def st_mod(S, cin, w_mod, b_mod, modv):
    cs = S.sbuf("cs", [P, 16], F32)
    ca = S.sbuf("ca", [P, 16], F32)
    bcs, bca = Buf(), Buf()
    S.dma("sp", cs[:], cin, wr=[bcs])
    S.op("act", lambda e: e.activation(out=ca[:], in_=cs[:], func=AF.Silu), rd=[bcs], wr=[bca])
    wr_ = Ring(S, "wm", [P, 8, 512], F32, 2)
    pr_ = Ring(S, "pm", [2, 512], F32, 2, psum=True)
    for l in range(DEPTH):
        row = S.sbuf("row", [2, 6144], F32)
        brow = Buf()
        S.dma("sp", row[0:1, :], b_mod[l:l + 1, :], wr=[brow])
        S.dma("sp", row[1:2, :], b_mod[l:l + 1, :], wr=[brow])
        for n in range(12):
            wt, wb = wr_.next()
            S.dma("sp", wt[:],
                  w_mod[l, :, n * 512:(n + 1) * 512].rearrange("(kc p) n -> p kc n", p=P), wr=[wb])
            ps, pb = pr_.next()
            for kc in range(8):
                S.op("pe", lambda e, ps=ps, wt=wt, kc=kc: e.matmul(
                    ps[:], ca[:, kc * 2:kc * 2 + 2], wt[:, kc, :], start=(kc == 0), stop=(kc == 7)),
                    rd=[bca, wb], wr=[pb])
            add1 = 1.0 if n in (2, 3, 8, 9) else 0.0
            sl = row[:, n * 512:(n + 1) * 512]
            S.op("dve", lambda e, sl=sl, ps=ps, add1=add1: e.scalar_tensor_tensor(
                out=sl, in0=ps[:], scalar=add1, in1=sl, op0=ALU.add, op1=ALU.add),
                rd=[pb], wr=[brow])
        S.dma("sp", modv[l], row[:], rd=[brow])


def ln_stats(S, xt, bx, st, bst):
    S.op("dve", lambda e: e.bn_stats(st[:, 0:6], xt[:, 0:512]), rd=[bx], wr=[bst])
    S.op("dve", lambda e: e.bn_stats(st[:, 6:12], xt[:, 512:1024]), rd=[bx], wr=[bst])
    S.op("dve", lambda e: e.bn_aggr(st[:, 12:14], st[:, 0:12]), rd=[bst], wr=[bst])
    S.op("dve", lambda e: e.tensor_scalar(st[:, 14:15], st[:, 13:14], EPS, None, ALU.add),
         rd=[bst], wr=[bst])
    S.op("act", lambda e: e.activation(out=st[:, 14:15], in_=st[:, 14:15], func=AF.Sqrt), rd=[bst], wr=[bst])
    S.op("dve", lambda e: e.reciprocal(st[:, 14:15], st[:, 14:15]), rd=[bst], wr=[bst])
    S.op("dve", lambda e: e.scalar_tensor_tensor(
        out=st[:, 15:16], in0=st[:, 12:13], scalar=-1.0, in1=st[:, 14:15], op0=ALU.mult, op1=ALU.mult),
        rd=[bst], wr=[bst])


def load_bc(S, name, src_row, eng="sp"):
    n = src_row.shape[-1]
    t = S.sbuf(name, [P, n], F32)
    b = Buf()
    S.dma(eng, t[:], src_row.partition_broadcast(P), wr=[b])
    return t, b


def st_modulate(S, xs, modv_l, sc_off, sh_off, hT, ident, T, LT):
    idt = S.sbuf("idt", [P, P], BF16)
    bid = Buf()
    S.dma("sp", idt[:], ident, wr=[bid])
    bc = {}
    for j in range(2):
        bc[j] = (load_bc(S, f"sc{j}", modv_l[j:j + 1, sc_off:sc_off + D]),
                 load_bc(S, f"sh{j}", modv_l[j:j + 1, sh_off:sh_off + D]))
    xr = Ring(S, "xt", [P, D], F32, 2)
    sr = Ring(S, "st", [P, 16], F32, 2)
    nr = Ring(S, "xn", [P, D], F32, 2)
    hr = Ring(S, "hb", [P, D], BF16, 2)
    tr = Ring(S, "tp", [P, 8, P], BF16, 2, psum=True)
    outr = Ring(S, "ho", [P, 8, 512], BF16, 2)
    hTv = hT.rearrange("(kc p) t -> p kc t", p=P)
    for t0 in range(0, T, 512):
        tl = min(512, T - t0)
        ot, ob = outr.next()
        for s in range(tl // P):
            i = t0 // P + s
            j = 0 if i * P < LT else 1
            (sct, scb), (sht, shb) = bc[j]
            xt, bx = xr.next()
            S.dma("sp", xt[:], xs[i * P:(i + 1) * P, :], wr=[bx])
            st, bst = sr.next()
            ln_stats(S, xt, bx, st, bst)
            xn, bn = nr.next()
            S.op("act", lambda e, xn=xn, xt=xt, st=st: e.activation(
                out=xn[:], in_=xt[:], func=AF.Identity, scale=st[:, 14:15], bias=st[:, 15:16]),
                rd=[bx, bst], wr=[bn])
            S.op("dve", lambda e, xn=xn, sct=sct: e.tensor_tensor(xn[:], xn[:], sct[:], ALU.mult),
                 rd=[bn, scb], wr=[bn])
            hb, bh = hr.next()
            S.op("pool", lambda e, hb=hb, xn=xn, sht=sht: e.tensor_tensor(hb[:], xn[:], sht[:], ALU.add),
                 rd=[bn, shb], wr=[bh])
            tp, btp = tr.next()
            for kc in range(8):
                S.op("pe", lambda e, tp=tp, hb=hb, kc=kc: e.transpose(
                    tp[:, kc, :], hb[:, kc * P:(kc + 1) * P], idt[:]), rd=[bh, bid], wr=[btp])
            S.op("act", lambda e, ot=ot, tp=tp, s=s: e.copy(ot[:, :, s * P:(s + 1) * P], tp[:]),
                 rd=[btp], wr=[ob])
        S.dma("sp", hTv[:, :, t0:t0 + tl], ot[:, :, 0:tl], rd=[ob])


def load_w_bf16(S, name, W, KC, n, ring):
    wt = S.sbuf(name, [P, KC, n], BF16)
    wb = Buf()
    Wv = W.rearrange("(kc p) n -> p kc n", p=P)
    for kc in range(KC):
        load_cast(S, ring, wt[:, kc, :], Wv[:, kc, :], P, n, wb)
    return wt, wb


def rope_evac(S, ps, pb, m, scale, xr, pr2, t1r, perm, bperm, cs, sn, bcs, dst_ap, bdst):
    xb, bxb = xr.next()
    S.op("act", lambda e: e.activation(out=xb[:m, :], in_=ps[:m, :], func=AF.Copy, scale=scale),
         rd=[pb], wr=[bxb])
    p2, bp2 = pr2.next()
    S.op("pe", lambda e: e.matmul(p2[:m, :], perm[:m, :m], xb[:m, :], start=True, stop=True),
         rd=[bxb, bperm], wr=[bp2])
    t1, bt1 = t1r.next()
    S.op("pool", lambda e: e.tensor_tensor(t1[:m, :], xb[:m, :], cs[:m, :], ALU.mult),
         rd=[bxb, bcs], wr=[bt1])
    t2, bt2 = t1r.next()
    S.op("dve", lambda e: e.tensor_tensor(t2[:m, :], p2[:m, :], sn[:m, :], ALU.mult),
         rd=[bp2, bcs], wr=[bt2])
    S.op("pool", lambda e: e.tensor_tensor(dst_ap, t1[:m, :], t2[:m, :], ALU.add),
         rd=[bt1, bt2], wr=[bdst])


def st_proj(S, srcT, KC, T, jobs, cosT, sinT, permd):
    perm = S.sbuf("perm", [P, P], BF16)
    bperm = Buf()
    S.dma("sp", perm[:], permd, wr=[bperm])
    srcv = srcT.rearrange("(kc p) t -> p kc t", p=P)
    if True:
        if True:
            S2 = S
            cring = cast_ring(S)
            for j in jobs:
                j["wt"], j["wb"] = load_w_bf16(S2, "w", j["W"], KC, j["n"], cring)
            sr = Ring(S2, "src", [P, KC, 512], BF16, 2)
            pr = Ring(S2, "ps", [P, 512], F32, 4, psum=True)
            pr2 = Ring(S2, "ps2", [P, 512], F32, 2, psum=True)
            o16 = Ring(S2, "o16", [P, 8, 512], BF16, 3)
            o32 = Ring(S2, "o32", [P, 4, 512], F32, 2)
            xr = Ring(S2, "xb", [P, 512], BF16, 2)
            t1r = Ring(S2, "t1", [P, 512], F32, 4)
            csr = Ring(S2, "cs", [P, 2, 512], F32, 2)
            has_rope = any(j["kind"] == "rope" for j in jobs)
            def tile(t0, tl):
                st, sb = sr.next()
                S.dma("sp", st[:, :, 0:tl], srcv[:, :, t0:t0 + tl], wr=[sb])
                cst = bcs = None
                if has_rope:
                    cst, bcs = csr.next()
                    S.dma("sp", cst[:, 0, 0:tl], cosT[:, t0:t0 + tl], wr=[bcs])
                    S.dma("sp", cst[:, 1, 0:tl], sinT[:, t0:t0 + tl], wr=[bcs])
                for j in jobs:
                    wt, wb, n = j["wt"], j["wb"], j["n"]
                    if j["kind"] == "tm":
                        for s in range(tl // P):
                            ot, ob = o16.next()
                            otv = ot[:].rearrange("p a b -> p (a b)")
                            for hf in range(n // 512):
                                ps, pb = pr.next()
                                for kc in range(KC):
                                    S.op("pe", lambda e, ps=ps, st=st, wt=wt, kc=kc, s=s, hf=hf: e.matmul(
                                        ps[:], st[:, kc, s * P:(s + 1) * P], wt[:, kc, hf * 512:(hf + 1) * 512],
                                        start=(kc == 0), stop=(kc == KC - 1)), rd=[sb, wb], wr=[pb])
                                if hf % 2 == 0:
                                    S.op("act", lambda e, otv=otv, ps=ps, hf=hf: e.copy(
                                        otv[:, hf * 512:(hf + 1) * 512], ps[:]), rd=[pb], wr=[ob])
                                else:
                                    S.op("dve", lambda e, otv=otv, ps=ps, hf=hf: e.tensor_copy(
                                        otv[:, hf * 512:(hf + 1) * 512], ps[:]), rd=[pb], wr=[ob])
                            S.dma("sp", j["dst"][t0 + s * P:t0 + (s + 1) * P, :], otv[:, 0:n], rd=[ob])
                        continue
                    nch = (n + P - 1) // P
                    f32out = j.get("dt") == "f32"
                    grp = 4 if f32out else 8
                    for c0 in range(0, nch, grp):
                        ot, ob = (o32 if f32out else o16).next()
                        cn = min(grp, nch - c0)
                        for ci in range(cn):
                            c = c0 + ci
                            m = min(P, n - c * P)
                            ps, pb = pr.next()
                            for kc in range(KC):
                                S.op("pe", lambda e, ps=ps, st=st, wt=wt, kc=kc, c=c, m=m: e.matmul(
                                    ps[:m, 0:tl], wt[:, kc, c * P:c * P + m], st[:, kc, 0:tl],
                                    start=(kc == 0), stop=(kc == KC - 1)), rd=[sb, wb], wr=[pb])
                            if j["kind"] == "rope":
                                rope_evac(S, ps[:, 0:tl], pb, m, j.get("scale", 1.0), _RingView(xr, tl, 2),
                                          _RingView(pr2, tl, 2), _RingView(t1r, tl, 2), perm, bperm,
                                          cst[:, 0, 0:tl], cst[:, 1, 0:tl], bcs, ot[:m, ci, 0:tl], ob)
                            else:
                                func = j.get("func") or AF.Copy
                                S.op("act", lambda e, ot=ot, ps=ps, ci=ci, m=m, func=func, sc=j.get("scale", 1.0):
                                     e.activation(out=ot[:m, ci, 0:tl], in_=ps[:m, 0:tl], func=func, scale=sc),
                                     rd=[pb], wr=[ob])
                        if n >= P:
                            dv = j["dst"].rearrange("(c p) t -> p c t", p=P)
                            S.dma("sp", dv[:, c0:c0 + cn, t0:t0 + tl], ot[:, 0:cn, 0:tl], rd=[ob])
                        else:
                            S.dma("sp", j["dst"][:, t0:t0 + tl], ot[:n, 0, 0:tl], rd=[ob])

            for t0 in range(0, T, 512):
                tile(t0, min(512, T - t0))


def host_consts(LT, T):
    bf = ml_dtypes.bfloat16
    t = np.arange(LT)
    pos_row = (t // 64).astype(np.float32)
    pos_col = (t % 64).astype(np.float32)
    inv = (np.float32(10000.0) ** (-np.arange(0, 32, 2, dtype=np.float32) / np.float32(32))).astype(np.float32)
    ang = np.concatenate([pos_row[:, None] * inv, pos_col[:, None] * inv], -1)
    cos = np.ones((T, 32), np.float32)
    sin = np.zeros((T, 32), np.float32)
    cos[:LT] = np.cos(ang)
    sin[:LT] = np.sin(ang)
    d = np.arange(P) % 32
    cosT = np.ascontiguousarray(cos[:, d].T)
    sinT = np.ascontiguousarray(sin[:, d].T)
    perm = np.zeros((P, P), np.float32)
    for m in range(P):
        if m % 64 < 32:
            perm[m + 32, m] = -1.0
        else:
            perm[m - 32, m] = 1.0
    return dict(cosT=cosT, sinT=sinT, perm=perm.astype(bf), ident=np.eye(P, dtype=bf))


def proj_jobs(w_in_l, d):
    def W(o, n):
        return w_in_l[:, o:o + n]
    p1 = [
        dict(kind="fm", W=W(0, 512), n=512, dst=d.get("gqT"), scale=128 ** -0.5, dt="f32"),
        dict(kind="fm", W=W(512, 512), n=512, dst=d.get("gkT"), dt="f32"),
        dict(kind="tm", W=W(1024, 1024), n=1024, dst=d.get("gv")),
        dict(kind="fm", W=W(2048, 1024), n=1024, dst=d.get("grT"), func=AF.Silu),
        dict(kind="fm", W=W(3072, 32), n=32, dst=d.get("gaT"), dt="f32"),
        dict(kind="fm", W=W(6624, 3072), n=3072, dst=d.get("gatesT"), func=AF.Sigmoid),
    ]
    p2 = [
        dict(kind="rope", W=W(3104, 1024), n=1024, dst=d.get("dqT"), scale=0.125),
        dict(kind="rope", W=W(4128, 1024), n=1024, dst=d.get("dkT")),
        dict(kind="tm", W=W(5152, 1024), n=1024, dst=d.get("dv")),
        dict(kind="fm", W=W(6176, 256), n=256, dst=d.get("mqT"), dt="f32"),
        dict(kind="fm", W=W(6432, 128), n=128, dst=d.get("mkvT"), dt="f32"),
        dict(kind="rope", W=W(6560, 64), n=64, dst=d.get("krT")),
    ]
    return [p1, p2]


def feat_scratch(nc, T, kind="Internal"):
    def mk(name, shape, dt):
        return nc.dram_tensor(name, shape, dt, kind=kind).ap()
    return dict(
        gqT=mk("gqT", [512, T], F32), gkT=mk("gkT", [512, T], F32), gv=mk("gv", [T, 1024], BF16),
        grT=mk("grT", [1024, T], BF16), gaT=mk("gaT", [32, T], F32), gatesT=mk("gatesT", [3072, T], BF16),
        dqT=mk("dqT", [1024, T], BF16), dkT=mk("dkT", [1024, T], BF16), dv=mk("dv", [T, 1024], BF16),
        mqT=mk("mqT", [256, T], F32), mkvT=mk("mkvT", [128, T], F32), krT=mk("krT", [64, T], BF16),
    )


def rms_fm(S, xt, bx, nchunk, onesq, bones, gcol0, sv, bsv, sqr, pms, rvr, nq, bnq):
    sq, bsq = sqr.next()
    S.op("act", lambda e: e.activation(out=sq[:, 0:nchunk, :], in_=xt[:, 0:nchunk, :], func=AF.Square),
         rd=[bx], wr=[bsq])
    ps, pb = pms.next()
    for c in range(nchunk):
        S.op("pe", lambda e, c=c: e.matmul(ps[:], onesq[:], sq[:, c, :], start=(c == 0), stop=(c == nchunk - 1)),
             rd=[bsq, bones], wr=[pb])
    rv, brv = rvr.next()
    S.op("act", lambda e: e.activation(out=rv[:], in_=ps[:], func=AF.Sqrt, bias=EPS, scale=1.0 / (nchunk * P)),
         rd=[pb], wr=[brv])
    S.op("dve", lambda e: e.reciprocal(rv[:], rv[:]), rd=[brv], wr=[brv])
    for c in range(nchunk):
        S.op("dve", lambda e, c=c: e.scalar_tensor_tensor(
            out=nq[:, c, :], in0=xt[:, c, :], scalar=sv[:, gcol0 + c:gcol0 + c + 1], in1=rv[:],
            op0=ALU.mult, op1=ALU.mult), rd=[bx, brv, bsv], wr=[bnq])


def st_mla_up(S, mqT, mkvT, w_uq, w_ukv, smallv_l, qnT, qrT, knT, mv, cosT, sinT, permd, T):
    perm = S.sbuf("perm", [P, P], BF16)
    bperm = Buf()
    S.dma("sp", perm[:], permd, wr=[bperm])
    sv = S.sbuf("sv", [P, 16], F32)
    bsv = Buf()
    S.dma("sp", sv[:], smallv_l, wr=[bsv])
    ones = S.sbuf("ones", [P, P], F32)
    bones = Buf()
    S.op("dve", lambda e: e.memset(ones[:], 1.0), wr=[bones])
    wqn = S.sbuf("wqn", [P, 2, 8, 128], BF16)
    wqr = S.sbuf("wqr", [P, 2, 8, 64], BF16)
    wk = S.sbuf("wk", [P, 8, 128], BF16)
    wv = S.sbuf("wv", [P, 8, 128], BF16)
    bw = Buf()
    uqv = w_uq.rearrange("(kc p) (h j) -> p kc h j", p=P, j=192)
    cring = cast_ring(S)
    uqk = w_uq.rearrange("(kc p) n -> p kc n", p=P)
    for kc in range(2):
        stg, bs = cring.next()
        S.dma("sp", stg[:, 0:1536], uqk[:, kc, :], wr=[bs])
        sv3 = stg[:, 0:1536].rearrange("p (h j) -> p h j", j=192)
        S.op("pool", lambda e, kc=kc, sv3=sv3: e.tensor_copy(wqn[:, kc, :, :], sv3[:, :, 0:128]), rd=[bs], wr=[bw])
        S.op("pool", lambda e, kc=kc, sv3=sv3: e.tensor_copy(wqr[:, kc, :, :], sv3[:, :, 128:192]), rd=[bs], wr=[bw])
    stg, bs = cring.next()
    S.dma("sp", stg[:, 0:2048], w_ukv, wr=[bs])
    kv3 = stg[:, 0:2048].rearrange("p (h j) -> p h j", j=256)
    S.op("pool", lambda e, kv3=kv3: e.tensor_copy(wk[:], kv3[:, :, 0:128]), rd=[bs], wr=[bw])
    S.op("pool", lambda e, kv3=kv3: e.tensor_copy(wv[:], kv3[:, :, 128:256]), rd=[bs], wr=[bw])
    wvf = wv[:].rearrange("p h j -> p (h j)")
    mqv = mqT.rearrange("(c p) t -> p c t", p=P)
    xqr = Ring(S, "xq", [P, 2, 512], F32, 2)
    xkr = Ring(S, "xk", [P, 1, 512], F32, 2)
    sqr = Ring(S, "sq", [P, 2, 512], F32, 2)
    pms = Ring(S, "pms", [P, 512], F32, 1, psum=True)
    rvr = Ring(S, "rv", [P, 512], F32, 2)
    nqr = Ring(S, "nq", [P, 2, 512], BF16, 2)
    nkr = Ring(S, "nk", [P, 1, 512], BF16, 2)
    pr = Ring(S, "ps", [P, 512], F32, 4, psum=True)
    pr2 = Ring(S, "ps2", [P, 512], F32, 2, psum=True)
    o16 = Ring(S, "o16", [P, 8, 512], BF16, 3)
    xr = Ring(S, "xb", [P, 512], BF16, 2)
    t1r = Ring(S, "t1", [P, 512], F32, 4)
    csr = Ring(S, "cs", [P, 2, 512], F32, 2)
    qnv = qnT.rearrange("(c p) t -> p c t", p=P)
    qrv = qrT.rearrange("(c p) t -> p c t", p=P)
    knv = knT.rearrange("(c p) t -> p c t", p=P)
    def tile(t0, tl):
        W = slice(0, tl)
        xq, bxq = xqr.next()
        S.dma("sp", xq[:, :, W], mqv[:, :, t0:t0 + tl], wr=[bxq])
        xk, bxk = xkr.next()
        S.dma("sp", xk[:, 0, W], mkvT[:, t0:t0 + tl], wr=[bxk])
        cst, bcs = csr.next()
        S.dma("sp", cst[:, 0, W], cosT[:, t0:t0 + tl], wr=[bcs])
        S.dma("sp", cst[:, 1, W], sinT[:, t0:t0 + tl], wr=[bcs])
        sqv, pmv, rvv = _RingView(sqr, tl, 3), _RingView(pms, tl, 2), _RingView(rvr, tl, 2)
        nq, bnq = nqr.next()
        rms_fm(S, xq[:, :, W], bxq, 2, ones, bones, 0, sv, bsv, sqv, pmv, rvv, nq[:, :, W], bnq)
        nk, bnk = nkr.next()
        rms_fm(S, xk[:, :, W], bxk, 1, ones, bones, 2, sv, bsv, sqv, pmv, rvv, nk[:, :, W], bnk)
        ot, ob = o16.next()
        for h in range(8):
            ps, pb = pr.next()
            for kc in range(2):
                S.op("pe", lambda e, ps=ps, kc=kc, h=h: e.matmul(
                    ps[:, W], wqn[:, kc, h, :], nq[:, kc, W], start=(kc == 0), stop=(kc == 1)),
                    rd=[bw, bnq], wr=[pb])
            S.op("act", lambda e, ot=ot, ps=ps, h=h: e.activation(
                out=ot[:, h, W], in_=ps[:, W], func=AF.Copy, scale=MLA_SCALE), rd=[pb], wr=[ob])
        S.dma("sp", qnv[:, :, t0:t0 + tl], ot[:, :, W], rd=[ob])
        ot, ob = o16.next()
        for c in range(4):
            ps, pb = pr.next()
            for kc in range(2):
                S.op("pe", lambda e, ps=ps, kc=kc, c=c: e.matmul(
                    ps[:, W], wqr[:, kc, 2 * c:2 * c + 2, :].rearrange("p h j -> p (h j)"), nq[:, kc, W],
                    start=(kc == 0), stop=(kc == 1)), rd=[bw, bnq], wr=[pb])
            rope_evac(S, ps[:, W], pb, P, MLA_SCALE, _RingView(xr, tl, 2), _RingView(pr2, tl, 2),
                      _RingView(t1r, tl, 2), perm, bperm, cst[:, 0, W], cst[:, 1, W], bcs, ot[:, c, W], ob)
        S.dma("sp", qrv[:, :, t0:t0 + tl], ot[:, 0:4, W], rd=[ob])
        ot, ob = o16.next()
        for h in range(8):
            ps, pb = pr.next()
            S.op("pe", lambda e, ps=ps, h=h: e.matmul(
                ps[:, W], wk[:, h, :], nk[:, 0, W], start=True, stop=True), rd=[bw, bnk], wr=[pb])
            S.op("act", lambda e, ot=ot, ps=ps, h=h: e.copy(ot[:, h, W], ps[:, W]), rd=[pb], wr=[ob])
        S.dma("sp", knv[:, :, t0:t0 + tl], ot[:, :, W], rd=[ob])
        for s in range(tl // P):
            ot, ob = o16.next()
            otv = ot[:].rearrange("p a b -> p (a b)")
            for hf in range(2):
                ps, pb = pr.next()
                S.op("pe", lambda e, ps=ps, s=s, hf=hf: e.matmul(
                    ps[:], nk[:, 0, s * P:(s + 1) * P], wvf[:, hf * 512:(hf + 1) * 512], start=True, stop=True),
                    rd=[bw, bnk], wr=[pb])
                S.op("dve", lambda e, otv=otv, ps=ps, hf=hf: e.tensor_copy(
                    otv[:, hf * 512:(hf + 1) * 512], ps[:]), rd=[pb], wr=[ob])
            S.dma("sp", mv[t0 + s * P:t0 + (s + 1) * P, :], otv[:, 0:1024], rd=[ob])

    for t0 in range(0, T, 512):
        tile(t0, min(512, T - t0))


def host_smallv(inp):
    sv = np.zeros((DEPTH, P, 16), np.float32)
    for l in range(DEPTH):
        sv[l, :, 0:2] = inp["mla_q_norm_g"][l].reshape(2, P).T
        sv[l, :, 2] = inp["mla_kv_norm_g"][l]
        sv[l, :, 3:11] = inp["gla_b_a"][l].reshape(8, P).T
        sv[l, :, 11:13] = inp["gla_norm_g"][l].reshape(2, P).T
        sv[l, :, 13] = inp["diff_norm_g"][l]
    return sv


GSEG = 768


def st_gla(S, gqT, gkT, gv, gaT, grT, w_a2, smallv_l, yaT, identd, scanmaskd, blockmaskd, T, LT):
    NTL = T // P
    NCH = T // 64
    idt = S.sbuf("idt", [P, P], BF16)
    bid = Buf()
    S.dma("sp", idt[:], identd, wr=[bid])
    sv = S.sbuf("sv", [P, 16], F32)
    nb = S.sbuf("nb", [P, 8], F32)
    bsv = Buf()
    S.dma("sp", sv[:], smallv_l, wr=[bsv])
    S.op("dve", lambda e: e.tensor_scalar(nb[:], sv[:, 3:11], -1.0, None, ALU.mult), rd=[bsv], wr=[bsv])
    ones = S.sbuf("ones", [P, P], F32)
    bones = Buf()
    S.op("dve", lambda e: e.memset(ones[:], 1.0), wr=[bones])
    smask = S.sbuf("smask", [P, GSEG], F32)
    bmask = S.sbuf("bmask", [P, 2, P], F32)
    wa2 = S.sbuf("wa2", [16, 2, 512], F32)
    bcst = Buf()
    S.dma("sp", smask[:], scanmaskd, wr=[bcst])
    S.dma("sp", bmask[:], blockmaskd.rearrange("d p q -> p d q"), wr=[bcst])
    S.dma("sp", wa2[:], w_a2.rearrange("d r e -> r d e"), wr=[bcst])

    qt = [S.sbuf(f"qt{d}", [P, T], BF16) for d in range(2)]
    kt = [S.sbuf(f"kt{d}", [P, T], BF16) for d in range(2)]
    khtm = [S.sbuf(f"kh{d}", [P, NTL, P], BF16) for d in range(2)]
    dec = [S.sbuf(f"dec{d}", [P, NCH], F32) for d in range(2)]
    acc = S.sbuf("acc", [P, 2, T], BF16)
    Sst = [S.sbuf(f"Sst{d}", [P, 256], F32) for d in range(2)]
    Sbf = [S.sbuf(f"Sbf{d}", [P, 256], BF16) for d in range(2)]

    qs = Ring(S, "qs", [P, GSEG], F32, 1)
    ks = Ring(S, "ks", [P, GSEG], F32, 1)
    gas = Ring(S, "gas", [16, 2, GSEG], F32, 1)
    tg = Ring(S, "tg", [P, GSEG], F32, 2)
    tb = Ring(S, "tb", [P, GSEG], F32, 2)
    tx = Ring(S, "tx", [P, GSEG], F32, 3)
    tkh = Ring(S, "tkh", [P, GSEG], BF16, 2)
    pz = Ring(S, "pz", [P, 512], F32, 1, psum=True)
    ptp = Ring(S, "ptp", [P, 1024], BF16, 1, psum=True)
    pA = Ring(S, "pA", [P, 512], F32, 1, psum=True)
    pS = Ring(S, "pS", [P, 512], F32, 1, psum=True)
    pod = [[S.psum(f"po{d}{ec}", [P, 512], F32) for ec in range(2)] for d in range(2)]
    bpod = [Buf(), Buf()]
    Asb = Ring(S, "Asb", [P, P], BF16, 3)
    vtr = Ring(S, "vt", [P, 256], BF16, 4)
    sqr = Ring(S, "sq", [P, 2, 512], F32, 1)
    rvr = Ring(S, "rv", [P, 512], F32, 1)
    nqr = Ring(S, "nq", [P, 2, 512], BF16, 2)
    grr = Ring(S, "grt", [P, 2, 512], BF16, 2)
    gvv = gv.rearrange("(n p) c -> p n c", p=P)
    grv = grT.rearrange("(c p) t -> p c t", p=P)
    yav = yaT.rearrange("(c p) t -> p c t", p=P)

    tilesF = list(range(LT // P, NTL)) + list(range(0, LT // P))
    tilesB = list(range(NTL - 1, LT // P - 1, -1)) + list(range(LT // P - 1, -1, -1))

    bq = [Buf(), Buf()]
    bkh = [Buf(), Buf()]
    bdec = [Buf(), Buf()]
    bacc = [Buf() for _ in range(NTL)]
    bS = [Buf(), Buf()]
    bSb = [Buf(), Buf()]
    ball = Buf()
    def seg(h, s0, sl):
        if True:
            nch = sl // 64
            q_, bq_ = qs.next()
            k_, bk_ = ks.next()
            ga_, bga_ = gas.next()
            S.dma("sp", q_[:, 0:sl], gqT[h * P:(h + 1) * P, s0:s0 + sl], wr=[bq_])
            S.dma("sp", k_[:, 0:sl], gkT[h * P:(h + 1) * P, s0:s0 + sl], wr=[bk_])
            S.dma("sp", ga_[:, 0, 0:sl], gaT[0:16, s0:s0 + sl], wr=[bga_])
            S.dma("sp", ga_[:, 1, 0:sl], gaT[16:32, s0:s0 + sl], wr=[bga_])
            for d in range(2):
                g_, bg_ = tg.next()
                for c0 in range(0, sl, 512):
                    cl = min(512, sl - c0)
                    ps, pb = pz.next()
                    S.op("pe", lambda e, ps=ps, d=d, ga_=ga_, c0=c0, cl=cl: e.matmul(
                        ps[:, 0:cl], wa2[:, d, h * P:(h + 1) * P], ga_[:, d, c0:c0 + cl], start=True, stop=True),
                        rd=[bcst, bga_], wr=[pb])
                    S.op("act", lambda e, ps=ps, g_=g_, d=d, c0=c0, cl=cl: e.activation(
                        out=g_[:, c0:c0 + cl], in_=ps[:, 0:cl], func=AF.Exp, scale=-1.0,
                        bias=nb[:, d * 4 + h:d * 4 + h + 1]), rd=[pb, bsv], wr=[bg_])
                S.op("act", lambda e, g_=g_: e.activation(out=g_[:, 0:sl], in_=g_[:, 0:sl], func=AF.Ln, bias=1.0),
                     rd=[bg_], wr=[bg_])
                S.op("dve", lambda e, g_=g_: e.tensor_scalar(g_[:, 0:sl], g_[:, 0:sl], -1.0 / 16.0, None, ALU.mult),
                     rd=[bg_], wr=[bg_])
                b_, bb_ = tb.next()
                S.op("dve", lambda e, b_=b_, g_=g_: e.tensor_tensor_scan(
                    b_[:, 0:sl], smask[:, 0:sl], g_[:, 0:sl], 0.0, ALU.mult, ALU.add),
                    rd=[bg_, bcst], wr=[bb_])
                b3 = b_[:, 0:sl].rearrange("p (n c) -> p n c", c=64)
                lastbc = b3[:, :, 63:64].to_broadcast([P, nch, 64])
                x1, bx1 = tx.next()
                x13 = x1[:, 0:sl].rearrange("p (n c) -> p n c", c=64)
                if d == 1:
                    S.op("dve", lambda e, x13=x13, lastbc=lastbc, b3=b3: e.tensor_tensor(
                        x13, lastbc, b3, ALU.subtract), rd=[bb_], wr=[bx1])
                    S.op("dve", lambda e, b_=b_, x1=x1, g_=g_: e.tensor_tensor(
                        b_[:, 0:sl], x1[:, 0:sl], g_[:, 0:sl], ALU.add), rd=[bx1, bg_], wr=[bb_])
                    edge = b3[:, :, 0:1].to_broadcast([P, nch, 64])
                    ecol = 0
                else:
                    edge = lastbc
                    ecol = 63
                S.op("act", lambda e, x1=x1, b_=b_: e.activation(out=x1[:, 0:sl], in_=b_[:, 0:sl], func=AF.Exp),
                     rd=[bb_], wr=[bx1])
                S.op("dve", lambda e, d=d, x13=x13, ecol=ecol, s0=s0, nch=nch: e.tensor_copy(
                    dec[d][:, s0 // 64:s0 // 64 + nch].rearrange("p (n o) -> p n o", o=1),
                    x13[:, :, ecol:ecol + 1]), rd=[bx1], wr=[bdec[d]])
                S.op("dve", lambda e, d=d, q_=q_, x1=x1, s0=s0: e.tensor_tensor(
                    qt[d][:, s0:s0 + sl], q_[:, 0:sl], x1[:, 0:sl], ALU.mult), rd=[bq_, bx1], wr=[bq[d]])
                x2, bx2 = tx.next()
                S.op("act", lambda e, x2=x2, b_=b_: e.activation(
                    out=x2[:, 0:sl], in_=b_[:, 0:sl], func=AF.Exp, scale=-1.0), rd=[bb_], wr=[bx2])
                S.op("pool", lambda e, d=d, k_=k_, x2=x2, s0=s0: e.tensor_tensor(
                    kt[d][:, s0:s0 + sl], k_[:, 0:sl], x2[:, 0:sl], ALU.mult), rd=[bk_, bx2], wr=[bq[d]])
                x3, bx3 = tx.next()
                x33 = x3[:, 0:sl].rearrange("p (n c) -> p n c", c=64)
                S.op("dve", lambda e, x33=x33, edge=edge, b3=b3: e.tensor_tensor(x33, edge, b3, ALU.subtract),
                     rd=[bb_], wr=[bx3])
                S.op("act", lambda e, x3=x3: e.activation(out=x3[:, 0:sl], in_=x3[:, 0:sl], func=AF.Exp),
                     rd=[bx3], wr=[bx3])
                kh_, bkh_ = tkh.next()
                S.op("pool", lambda e, kh_=kh_, k_=k_, x3=x3: e.tensor_tensor(
                    kh_[:, 0:sl], k_[:, 0:sl], x3[:, 0:sl], ALU.mult), rd=[bk_, bx3], wr=[bkh_])
                for ti in range(sl // P):
                    tp, btp = ptp.next()
                    S.op("pe", lambda e, tp=tp, kh_=kh_, ti=ti: e.transpose(
                        tp[:, 0:P], kh_[:, ti * P:(ti + 1) * P], idt[:]), rd=[bkh_, bid], wr=[btp])
                    S.op("act", lambda e, tp=tp, d=d, ti=ti, s0=s0: e.copy(
                        khtm[d][:, s0 // P + ti, :], tp[:, 0:P]), rd=[btp], wr=[bkh[d]])
    def head(h):
        for s0 in range(0, T, GSEG):
            seg(h, s0, min(GSEG, T - s0))
        for d in range(2):
            S.op("dve", lambda e, d=d: e.memset(Sst[d][:], 0.0), wr=[bS[d]])
            S.op("pool", lambda e, d=d: e.memset(Sbf[d][:], 0.0), wr=[bSb[d]])
        S.op("pool", lambda e: e.memset(acc[:], 0.0), wr=bacc + [ball])
        for step in range(NTL):
            for d in range(2):
                n = (tilesF if d == 0 else tilesB)[step]
                tsl = slice(n * P, (n + 1) * P)
                psa, bpa = pA.next()
                S.op("pe", lambda e, psa=psa, d=d, tsl=tsl: e.matmul(
                    psa[:, 0:P], kt[d][:, tsl], qt[d][:, tsl], start=True, stop=True), rd=[bq[d]], wr=[bpa])
                a_, ba_ = Asb.next()
                S.op("dve", lambda e, a_=a_, psa=psa, d=d: e.tensor_tensor(a_[:], psa[:, 0:P], bmask[:, d, :], ALU.mult),
                     rd=[bpa, bcst], wr=[ba_])
                vt, bvt = vtr.next()
                S.dma("sp", vt[:], gvv[:, n, h * 256:(h + 1) * 256], wr=[bvt])
                pot, bpo = pod[d], bpod[d]
                for ec in range(2):
                    S.op("pe", lambda e, pot=pot, vt=vt, a_=a_, ec=ec: e.matmul(
                        pot[ec][:, 0:P], vt[:, ec * P:(ec + 1) * P], a_[:], start=True, stop=False),
                        rd=[bvt, ba_], wr=[bpo])
                order = (0, 1) if d == 0 else (1, 0)
                for oi, hh in enumerate(order):
                    c = 2 * n + hh
                    csl = slice(c * 64, (c + 1) * 64)
                    for ec in range(2):
                        S.op("pe", lambda e, pot=pot, d=d, ec=ec, hh=hh, csl=csl, oi=oi: e.matmul(
                            pot[ec][:, hh * 64:(hh + 1) * 64], Sbf[d][:, ec * P:(ec + 1) * P], qt[d][:, csl],
                            start=False, stop=(oi == 1)), rd=[bSb[d], bq[d]], wr=[bpo])
                    pst, bps = pS.next()
                    S.op("pe", lambda e, pst=pst, d=d, n=n, hh=hh, vt=vt: e.matmul(
                        pst[:, 0:256], khtm[d][hh * 64:(hh + 1) * 64, n, :], vt[hh * 64:(hh + 1) * 64, :],
                        start=True, stop=True), rd=[bkh[d], bvt], wr=[bps])
                    S.op("dve", lambda e, pst=pst, d=d, c=c: e.scalar_tensor_tensor(
                        out=Sst[d][:], in0=Sst[d][:], scalar=dec[d][:, c:c + 1], in1=pst[:, 0:256],
                        op0=ALU.mult, op1=ALU.add), rd=[bps, bdec[d]], wr=[bS[d]])
                    S.op("dve", lambda e, d=d: e.tensor_copy(Sbf[d][:], Sst[d][:]), rd=[bS[d]], wr=[bSb[d]])
                for ec in range(2):
                    S.op("dve", lambda e, pot=pot, tsl=tsl, ec=ec: e.tensor_tensor(
                        acc[:, ec, tsl], pot[ec][:, 0:P], acc[:, ec, tsl], ALU.add), rd=[bpo], wr=[bacc[n]])
        S.op("dve", lambda e: e.engine_nop(), rd=bacc, wr=[ball])
        def otile(t0, tl):
            gt, bgt = grr.next()
            S.dma("sp", gt[:, :, 0:tl], grv[:, 2 * h:2 * h + 2, t0:t0 + tl], wr=[bgt])
            nq, bnq = nqr.next()
            rms_fm(S, acc[:, :, t0:t0 + tl], ball, 2, ones, bones, 11, sv, bsv, _RingView(sqr, tl, 3),
                   _RingView(pz, tl, 2), _RingView(rvr, tl, 2), nq[:, :, 0:tl], bnq)
            S.op("pool", lambda e: e.tensor_tensor(nq[:, :, 0:tl], nq[:, :, 0:tl], gt[:, :, 0:tl], ALU.mult),
                 rd=[bnq, bgt], wr=[bnq])
            S.dma("sp", yav[:, 2 * h:2 * h + 2, t0:t0 + tl], nq[:, :, 0:tl], rd=[bnq])

        for t0 in range(0, T, 512):
            otile(t0, min(512, T - t0))

    for h in range(4):
        head(h)


def host_gla_consts():
    sm = np.ones((P, GSEG), np.float32)
    sm[:, ::64] = 0.0
    bm = np.zeros((2, P, P), np.float32)
    for j in range(P):
        for i in range(P):
            if j // 64 == i // 64:
                bm[0, j, i] = 1.0 if j <= i else 0.0
                bm[1, j, i] = 1.0 if j > i else 0.0
    return dict(scanmask=sm, blockmask=bm)


def st_attn(S, dqT, dkT, dv, qnT, qrT, knT, krT, mv, diff_lam_l, smallv_l, lam_init, ybT, ycT, T, LT, ctx_q):
    NT = T // P
    sv = S.sbuf("sv", [P, 16], F32)
    bsv = Buf()
    S.dma("sp", sv[:], smallv_l, wr=[bsv])
    ones = S.sbuf("ones", [P, P], F32)
    onesb = S.sbuf("onesb", [P, P], BF16)
    bones = Buf()
    S.op("dve", lambda e: e.memset(ones[:], 1.0), wr=[bones])
    S.op("dve", lambda e: e.memset(onesb[:], 1.0), wr=[bones])
    dl = S.sbuf("dl", [P, 4, 64], F32)
    lm = S.sbuf("lm", [P, 8], F32)
    blm = Buf()
    S.dma("sp", dl[:].rearrange("p a b -> p (a b)"),
          diff_lam_l.rearrange("a b -> (a b)").rearrange("(o n) -> o n", o=1).partition_broadcast(P), wr=[blm])
    pr_ = S.sbuf("prd", [P, 2, 64], F32)
    S.op("dve", lambda e: e.tensor_tensor(pr_[:, 0, :], dl[:, 0, :], dl[:, 1, :], ALU.mult), rd=[blm], wr=[blm])
    S.op("dve", lambda e: e.tensor_tensor(pr_[:, 1, :], dl[:, 2, :], dl[:, 3, :], ALU.mult), rd=[blm], wr=[blm])
    S.op("dve", lambda e: e.reduce_sum(lm[:, 0:2], pr_[:], AX.X), rd=[blm], wr=[blm])
    S.op("act", lambda e: e.activation(out=lm[:, 2:4], in_=lm[:, 0:2], func=AF.Exp), rd=[blm], wr=[blm])
    S.op("dve", lambda e: e.tensor_tensor(lm[:, 4:5], lm[:, 3:4], lm[:, 2:3], ALU.subtract), rd=[blm], wr=[blm])
    S.op("dve", lambda e: e.tensor_scalar(lm[:, 4:5], lm[:, 4:5], -lam_init, None, ALU.add), rd=[blm], wr=[blm])
    S.op("dve", lambda e: e.tensor_scalar(sv[:, 14:15], sv[:, 13:14], 1.0 - lam_init, None, ALU.mult),
         rd=[bsv], wr=[bsv])

    kT = Ring(S, "kT", [P, T], BF16, 2)
    qT = Ring(S, "qT", [P, T], BF16, 2)
    k2 = Ring(S, "k2", [P, T], BF16, 1)
    q2 = Ring(S, "q2", [P, T], BF16, 2)
    tmpr = Ring(S, "ptsum", [P, 1024], BF16, 4)
    vv = Ring(S, "vv", [P, NT, P], BF16, 2)
    ptr = Ring(S, "pt", [P, 1024], BF16, 4)
    psr = Ring(S, "pss", [P, 1024], F32, 3, psum=True)
    accr = Ring(S, "acc", [P, 1024], F32, 2)
    pacc = {i: S.psum(f"pacc{i}", [P, 512], F32) for i in (0, 2)}
    bpacc = {i: Buf() for i in (0, 2)}
    rr = Ring(S, "rr", [P, 512], F32, 2)
    tt = Ring(S, "tt", [P, 1, 512], F32, 3)
    sqr = Ring(S, "sq", [P, 1, 512], F32, 1)
    rvr = Ring(S, "rv", [P, 512], F32, 1)
    outr = Ring(S, "ob", [P, 1, 512], BF16, 3)
    dvv = dv.rearrange("(n p) c -> p n c", p=P)
    mvv = mv.rearrange("(n p) c -> p n c", p=P)

    qtiles = [(t0, 512, 0, NT) for t0 in range(0, LT, 512)]
    if ctx_q:
        qtiles.append((LT, T - LT, LT // P, NT))

    k2t, bk2 = k2.next()
    S.dma("sp", k2t[0:64, :], krT, wr=[bk2])
    S.dma("sp", k2t[64:128, :], krT, wr=[bk2])

    def run_head(h, kind):
        kt_, bkt = kT.next()
        qt_, bqt = qT.next()
        vt_, bvt = vv.next()
        if kind == "diff":
            S.dma("sp", kt_[:], dkT[h * P:(h + 1) * P, :], wr=[bkt])
            S.dma("sp", qt_[:], dqT[h * P:(h + 1) * P, :], wr=[bqt])
            for n0 in range(0, NT, 8):
                n1 = min(NT, n0 + 8)
                S.dma("sp", vt_[:, n0:n1, :], dvv[:, n0:n1, h * P:(h + 1) * P], wr=[bvt])
            nm = 2
        else:
            S.dma("sp", kt_[:], knT[h * P:(h + 1) * P, :], wr=[bkt])
            S.dma("sp", qt_[:], qnT[h * P:(h + 1) * P, :], wr=[bqt])
            for n0 in range(0, NT, 8):
                n1 = min(NT, n0 + 8)
                S.dma("sp", vt_[:, n0:n1, :], mvv[:, n0:n1, h * P:(h + 1) * P], wr=[bvt])
            q2t, bq2 = q2.next()
            S.dma("sp", q2t[0:64, :], qrT[h * 64:(h + 1) * 64, :], wr=[bq2])
            S.dma("sp", q2t[64:128, :], qrT[h * 64:(h + 1) * 64, :], wr=[bq2])
            nm = 1
        def qtile(t0, tl, kb0, kb1):
            qs_ = slice(t0, t0 + tl)
            if kind == "diff":
                units = [((kb, 0), (kb, 1)) for kb in range(kb0, kb1)]
            else:
                units = [((kb, 0), (kb + 1, 0)) for kb in range(kb0, kb1, 2)]
            LA = 3
            pend = {}
            lvl0, lvl1, started = [], [], [False]
            ac, bac = accr.next()
            ac3 = ac[:].rearrange("p (a b) -> p a b", b=512)[:, :, 0:tl]

            def emit_score(u):
                ps, pb = psr.next()
                if kind == "diff":
                    for hf, (kb, m) in enumerate(units[u]):
                        ks_ = slice(kb * P, (kb + 1) * P)
                        o = ps[:, hf * 512:hf * 512 + tl]
                        S.op("pe", lambda e, o=o, m=m, ks_=ks_: e.matmul(
                            o, kt_[m * 64:(m + 1) * 64, ks_], qt_[m * 64:(m + 1) * 64, qs_],
                            start=True, stop=True), rd=[bkt, bqt], wr=[pb])
                else:
                    for hf, (kb, m) in enumerate(units[u]):
                        ks_ = slice(kb * P, (kb + 1) * P)
                        o = ps[:, hf * 512:hf * 512 + tl]
                        S.op("pe", lambda e, o=o, ks_=ks_: e.matmul(
                            o, kt_[:, ks_], qt_[:, qs_], start=True, stop=False), rd=[bkt, bqt], wr=[pb])
                    for hf, (kb, m) in enumerate(units[u]):
                        ks_ = slice(kb * P, (kb + 1) * P)
                        o = ps[:, hf * 512:hf * 512 + tl]
                        rs = slice(hf * 64, (hf + 1) * 64)
                        S.op("pe", lambda e, o=o, ks_=ks_, rs=rs: e.matmul(
                            o, k2t[rs, ks_], q2t[rs, qs_], start=False, stop=True), rd=[bk2, bq2], wr=[pb])
                pend[u] = (ps, pb)

            def emit_rest(u):
                ps, pb = pend.pop(u)
                pt, bpt = ptr.next()
                ps3 = ps[:].rearrange("p (a b) -> p a b", b=512)[:, :, 0:tl]
                pt3 = pt[:].rearrange("p (a b) -> p a b", b=512)[:, :, 0:tl]
                S.op("act", lambda e: e.activation(out=pt3, in_=ps3, func=AF.Exp), rd=[pb], wr=[bpt])
                nu = len(units)

                def acc_add(src, bsrc):
                    if not started[0]:
                        started[0] = True
                        S.op("dve", lambda e, src=src: e.tensor_copy(ac3, src), rd=[bsrc], wr=[bac])
                    else:
                        S.op("dve", lambda e, src=src: e.tensor_tensor(ac3, ac3, src, ALU.add), rd=[bsrc, bac], wr=[bac])

                def bf_add(x, bx, y, by):
                    tm, btm = tmpr.next()
                    o3 = tm[:].rearrange("p (a b) -> p a b", b=512)[:, :, 0:tl]
                    S.op("dve", lambda e, o3=o3, x=x, y=y: e.tensor_tensor(o3, x, y, ALU.add), rd=[bx, by], wr=[btm])
                    return o3, btm

                lvl0.append((pt3, bpt))
                if len(lvl0) == 2:
                    (x, bx), (y, by) = lvl0
                    del lvl0[:]
                    lvl1.append(bf_add(x, bx, y, by))
                    if len(lvl1) == 2:
                        (x, bx), (y, by) = lvl1
                        del lvl1[:]
                        q3, bq3 = bf_add(x, bx, y, by)
                        acc_add(q3, bq3)
                if u == nu - 1:
                    for (x, bx) in lvl1 + lvl0:
                        acc_add(x, bx)
                    del lvl1[:]
                    del lvl0[:]
                for hf, (kb, m) in enumerate(units[u]):
                    a = 2 * m if kind == "diff" else 2 * hf
                    first, last = (u == 0), (u == nu - 1)
                    S.op("pe", lambda e, hf=hf, kb=kb, a=a, first=first, last=last: e.matmul(
                        pacc[a][:, 0:tl], vt_[:, kb, :], pt[:, hf * 512:hf * 512 + tl], start=first, stop=last),
                        rd=[bvt, bpt], wr=[bpacc[a]])

            for u in range(min(LA, len(units))):
                emit_score(u)
            for u in range(len(units)):
                emit_rest(u)
                if u + LA < len(units):
                    emit_score(u + LA)
            pz, bpz = psr.next()
            if kind == "diff":
                for m in range(2):
                    S.op("pe", lambda e, m=m: e.matmul(pz[:, m * 512:m * 512 + tl], ones[:], ac[:, m * 512:m * 512 + tl],
                                                       start=True, stop=True), rd=[bones, bac], wr=[bpz])
            else:
                for hf in range(2):
                    S.op("pe", lambda e, hf=hf: e.matmul(pz[:, 0:tl], ones[:], ac[:, hf * 512:hf * 512 + tl],
                                                         start=(hf == 0), stop=(hf == 1)), rd=[bones, bac], wr=[bpz])
            ob, bob = outr.next()
            r0, br0 = rr.next()
            S.op("dve", lambda e, r0=r0: e.reciprocal(r0[:, 0:tl], pz[:, 0:tl]), rd=[bpz], wr=[br0])
            if kind == "diff":
                r1, br1 = rr.next()
                S.op("dve", lambda e, r1=r1: e.reciprocal(r1[:, 0:tl], pz[:, 512:512 + tl]), rd=[bpz], wr=[br1])
                ta, bta = tt.next()
                S.op("dve", lambda e, ta=ta, r0=r0: e.tensor_tensor(ta[:, 0, 0:tl], pacc[0][:, 0:tl], r0[:, 0:tl], ALU.mult),
                     rd=[bpacc[0], br0], wr=[bta])
                tb_, btb = tt.next()
                S.op("dve", lambda e, tb_=tb_, r1=r1: e.tensor_tensor(tb_[:, 0, 0:tl], pacc[2][:, 0:tl], r1[:, 0:tl], ALU.mult),
                     rd=[bpacc[2], br1], wr=[btb])
                S.op("dve", lambda e, ta=ta, tb_=tb_: e.scalar_tensor_tensor(
                    out=ta[:, 0, 0:tl], in0=tb_[:, 0, 0:tl], scalar=lm[:, 4:5], in1=ta[:, 0, 0:tl],
                    op0=ALU.mult, op1=ALU.add), rd=[btb, blm], wr=[bta])
                rms_fm(S, ta[:, :, 0:tl], bta, 1, ones, bones, 14, sv, bsv, sqr_v(sqr, tl), psr_v(psr, tl), rvr_v(rvr, tl),
                       ob[:, :, 0:tl], bob)
                S.dma("sp", ybT[h * P:(h + 1) * P, qs_], ob[:, 0, 0:tl], rd=[bob])
            else:
                ta, bta = tt.next()
                S.op("dve", lambda e, ta=ta, r0=r0: e.tensor_tensor(ta[:, 0, 0:tl], pacc[0][:, 0:tl], r0[:, 0:tl], ALU.mult),
                     rd=[bpacc[0], br0], wr=[bta])
                tb_, btb = tt.next()
                S.op("dve", lambda e, tb_=tb_, r0=r0: e.tensor_tensor(tb_[:, 0, 0:tl], pacc[2][:, 0:tl], r0[:, 0:tl], ALU.mult),
                     rd=[bpacc[2], br0], wr=[btb])
                S.op("pool", lambda e, ob=ob, ta=ta, tb_=tb_: e.tensor_tensor(ob[:, 0, 0:tl], ta[:, 0, 0:tl], tb_[:, 0, 0:tl], ALU.add),
                     rd=[bta, btb], wr=[bob])
                S.dma("sp", ycT[h * P:(h + 1) * P, qs_], ob[:, 0, 0:tl], rd=[bob])

        for qt4 in qtiles:
            qtile(*qt4)

    for h in range(8):
        run_head(h, "diff")
    for h in range(8):
        run_head(h, "mla")


class _RingView:
    def __init__(self, ring, tl, nd):
        self.ring, self.tl, self.nd = ring, tl, nd

    def next(self):
        t, b = self.ring.next()
        if self.nd == 3:
            return t[:, :, 0:self.tl], b
        return t[:, 0:self.tl], b


def sqr_v(r, tl):
    return _RingView(r, tl, 3)


def psr_v(r, tl):
    return _RingView(r, tl, 2)


def rvr_v(r, tl):
    return _RingView(r, tl, 2)


def resid_ln(S, ps2, bps2, xs_rows, out_rows, gbc, bgbc, lng, blng, lnb, blnb, R):
    xt, bx = R["x"].next()
    S.dma("sp", xt[:], xs_rows, wr=[bx])
    u, bu = R["u"].next()
    for hf in range(2):
        S.op("dve", lambda e, hf=hf: e.tensor_tensor(
            u[:, hf * 512:(hf + 1) * 512], ps2[hf][:], gbc[:, hf * 512:(hf + 1) * 512], ALU.mult),
            rd=[bps2[hf], bgbc], wr=[bu])
    S.op("dve", lambda e: e.scalar_tensor_tensor(out=u[:], in0=xt[:], scalar=ALPHA, in1=u[:],
                                                  op0=ALU.mult, op1=ALU.add), rd=[bx, bu], wr=[bu])
    st, bst = R["st"].next()
    ln_stats(S, u, bu, st, bst)
    xn, bn = R["xn"].next()
    S.op("act", lambda e: e.activation(out=xn[:], in_=u[:], func=AF.Identity, scale=st[:, 14:15], bias=st[:, 15:16]),
         rd=[bu, bst], wr=[bn])
    S.op("dve", lambda e: e.tensor_tensor(xn[:], xn[:], lng[:], ALU.mult), rd=[bn, blng], wr=[bn])
    S.op("pool", lambda e: e.tensor_tensor(xn[:], xn[:], lnb[:], ALU.add), rd=[bn, blnb], wr=[bn])
    S.dma("sp", out_rows, xn[:], rd=[bn])


def resid_rings(S):
    return dict(x=Ring(S, "rx", [P, D], F32, 1), u=Ring(S, "ru", [P, D], F32, 1),
                st=Ring(S, "rst", [P, 16], F32, 2), xn=Ring(S, "rxn", [P, D], F32, 1))


def st_merge(S, yT3, gatesT, w_branch, w_out, modv_l, ln_g, ln_b, xs, xo, T, LT, Tproc):
    wb = S.sbuf("wb", [P, 3, 8, D], BF16)
    wo = S.sbuf("wo", [P, 8, D], BF16)
    bw = Buf()
    cring = cast_ring(S)
    for i in range(3):
        wv = w_branch[i].rearrange("(kc p) n -> p kc n", p=P)
        for kc in range(8):
            load_cast(S, cring, wb[:, i, kc, :], wv[:, kc, :], P, D, bw)
    wv = w_out.rearrange("(kc p) n -> p kc n", p=P)
    for kc in range(8):
        load_cast(S, cring, wo[:, kc, :], wv[:, kc, :], P, D, bw)
    gb = [load_bc(S, f"g1{j}", modv_l[j:j + 1, 2 * D:3 * D]) for j in range(2)]
    lng, blng = load_bc(S, "lng", ln_g)
    lnb, blnb = load_bc(S, "lnb", ln_b)
    yr = Ring(S, "yt", [P, 8, 512], BF16, 2)
    gr_ = Ring(S, "gt", [P, 24, 512], BF16, 1)
    yacc = S.sbuf("yacc", [P, 8, 512], F32)
    byacc = Buf()
    ybf = S.sbuf("ybf", [P, 8, 512], BF16)
    bybf = Buf()
    tmp = Ring(S, "tmp", [P, 512], F32, 2)
    pr = Ring(S, "ps", [P, 512], F32, 4, psum=True)
    pr2 = Ring(S, "ps2", [P, 512], F32, 4, psum=True)
    RR = resid_rings(S)
    gv_ = gatesT.rearrange("(c p) t -> p c t", p=P)
    def tile(t0, tl):
        W = slice(0, tl)
        gt, bgt = gr_.next()
        S.dma("sp", gt[:, :, W], gv_[:, :, t0:t0 + tl], wr=[bgt])
        for i in range(3):
            yt, byt = yr.next()
            S.dma("sp", yt[:, :, W], yT3[i].rearrange("(c p) t -> p c t", p=P)[:, :, t0:t0 + tl], wr=[byt])
            for n in range(8):
                ps, pb = pr.next()
                for kc in range(8):
                    S.op("pe", lambda e, ps=ps, i=i, kc=kc, n=n, yt=yt: e.matmul(
                        ps[:, W], wb[:, i, kc, n * P:(n + 1) * P], yt[:, kc, W], start=(kc == 0), stop=(kc == 7)),
                        rd=[bw, byt], wr=[pb])
                if i == 0:
                    S.op("dve", lambda e, ps=ps, n=n: e.tensor_tensor(
                        yacc[:, n, W], ps[:, W], gt[:, n, W], ALU.mult), rd=[pb, bgt], wr=[byacc])
                else:
                    tm, btm = tmp.next()
                    S.op("dve", lambda e, ps=ps, n=n, tm=tm, i=i: e.tensor_tensor(
                        tm[:, W], ps[:, W], gt[:, i * 8 + n, W], ALU.mult), rd=[pb, bgt], wr=[btm])
                    if i == 1:
                        S.op("pool", lambda e, n=n, tm=tm: e.tensor_tensor(
                            yacc[:, n, W], yacc[:, n, W], tm[:, W], ALU.add), rd=[btm, byacc], wr=[byacc])
                    else:
                        S.op("pool", lambda e, n=n, tm=tm: e.tensor_tensor(
                            ybf[:, n, W], yacc[:, n, W], tm[:, W], ALU.add), rd=[btm, byacc], wr=[bybf])
        for s in range(tl // P):
            r0 = t0 + s * P
            ps2 = []
            bps2 = []
            for hf in range(2):
                ps, pb = pr2.next()
                for kc in range(8):
                    S.op("pe", lambda e, ps=ps, kc=kc, s=s, hf=hf: e.matmul(
                        ps[:], ybf[:, kc, s * P:(s + 1) * P], wo[:, kc, hf * 512:(hf + 1) * 512],
                        start=(kc == 0), stop=(kc == 7)), rd=[bw, bybf], wr=[pb])
                ps2.append(ps)
                bps2.append(pb)
            j = 0 if r0 < LT else 1
            resid_ln(S, ps2, bps2, xs[r0:r0 + P, :], xo[r0:r0 + P, :], gb[j][0], gb[j][1], lng, blng, lnb, blnb, RR)

    for t0 in range(0, Tproc, 512):
        tile(t0, min(512, Tproc - t0))


def st_ffn(S, h2T, w1d, w2d, modv_l, ln_g, ln_b, xs, xo, T, LT, Tproc):
    NJ = FH // P
    w1 = S.sbuf("w1", [P, 8, 2 * FH], BF16)
    w2 = S.sbuf("w2", [P, NJ, D], BF16)
    bw = Buf()
    cring = Ring(S, "cst", [P, CAST_W], F32, 1)
    wv = w1d.rearrange("(kc p) n -> p kc n", p=P)
    for kc in range(8):
        load_cast(S, cring, w1[:, kc, :], wv[:, kc, :], P, 2 * FH, bw)
    wv = w2d.rearrange("(j p) n -> p j n", p=P)
    for j in range(NJ):
        load_cast(S, cring, w2[:, j, :], wv[:, j, :], P, D, bw)
    gb = [load_bc(S, f"g2{j}", modv_l[j:j + 1, 5 * D:6 * D]) for j in range(2)]
    lng, blng = load_bc(S, "lng", ln_g)
    lnb, blnb = load_bc(S, "lnb", ln_b)
    hr = Ring(S, "h2", [P, 8, 512], BF16, 1)
    hmid = S.sbuf("hmid", [P, NJ, 512], BF16)
    bhm = Buf()
    ar = Ring(S, "ar", [P, 512], BF16, 2)
    pr = Ring(S, "ps", [P, 512], F32, 4, psum=True)
    pr2 = Ring(S, "ps2", [P, 512], F32, 4, psum=True)
    RR = resid_rings(S)
    hv = h2T.rearrange("(kc p) t -> p kc t", p=P)
    def tile(t0, tl):
        W = slice(0, tl)
        ht, bht = hr.next()
        S.dma("sp", ht[:, :, W], hv[:, :, t0:t0 + tl], wr=[bht])
        for j in range(NJ):
            pg, bpg = pr.next()
            pu, bpu = pr.next()
            for kc in range(8):
                S.op("pe", lambda e, pg=pg, kc=kc, j=j: e.matmul(
                    pg[:, W], w1[:, kc, j * P:(j + 1) * P], ht[:, kc, W], start=(kc == 0), stop=(kc == 7)),
                    rd=[bw, bht], wr=[bpg])
            for kc in range(8):
                S.op("pe", lambda e, pu=pu, kc=kc, j=j: e.matmul(
                    pu[:, W], w1[:, kc, FH + j * P:FH + (j + 1) * P], ht[:, kc, W], start=(kc == 0), stop=(kc == 7)),
                    rd=[bw, bht], wr=[bpu])
            a_, ba_ = ar.next()
            S.op("act", lambda e, a_=a_, pg=pg: e.activation(out=a_[:, W], in_=pg[:, W], func=AF.Silu),
                 rd=[bpg], wr=[ba_])
            S.op("dve", lambda e, a_=a_, pu=pu, j=j: e.tensor_tensor(hmid[:, j, W], pu[:, W], a_[:, W], ALU.mult),
                 rd=[bpu, ba_], wr=[bhm])
        for s in range(tl // P):
            r0 = t0 + s * P
            ps2 = []
            bps2 = []
            for hf in range(2):
                ps, pb = pr2.next()
                for j in range(NJ):
                    S.op("pe", lambda e, ps=ps, j=j, s=s, hf=hf: e.matmul(
                        ps[:], hmid[:, j, s * P:(s + 1) * P], w2[:, j, hf * 512:(hf + 1) * 512],
                        start=(j == 0), stop=(j == NJ - 1)), rd=[bw, bhm], wr=[pb])
                ps2.append(ps)
                bps2.append(pb)
            jj = 0 if r0 < LT else 1
            dst = xo[r0:r0 + P, :]
            resid_ln(S, ps2, bps2, xs[r0:r0 + P, :], dst, gb[jj][0], gb[jj][1], lng, blng, lnb, blnb, RR)

    for t0 in range(0, Tproc, 512):
        tile(t0, min(512, Tproc - t0))


def tensor_specs(LT):
    T = LT + CT
    sp = dict(
        xin=([T, D], F32), cin=([P, 16], F32), w_mod=([DEPTH, D, 6 * D], F32), b_mod=([DEPTH, 6 * D], F32),
        w_in=([DEPTH, D, NIN], F32), gla_w_a2=([DEPTH, 2, 16, 512], F32), diff_lam=([DEPTH, 4, 64], F32),
        mla_w_uq=([DEPTH, 256, 1536], F32), mla_w_ukv=([DEPTH, 128, 2048], F32),
        w_branch=([DEPTH, 3, D, D], F32), w_out=([DEPTH, D, D], F32), ln1_g=([DEPTH, D], F32),
        ln1_b=([DEPTH, D], F32), ffn_w_in=([DEPTH, D, 2 * FH], F32), ffn_w_out=([DEPTH, FH, D], F32),
        ln2_g=([DEPTH, D], F32), ln2_b=([DEPTH, D], F32), smallv=([DEPTH, P, 16], F32),
        cosT=([P, T], F32), sinT=([P, T], F32), perm=([P, P], BF16), ident=([P, P], BF16),
        scanmask=([P, GSEG], F32), blockmask=([2, P, P], F32),
        modv=([DEPTH, 2, 6 * D], F32), hT=([D, T], BF16), h2T=([D, T], BF16),
        gqT=([512, T], F32), gkT=([512, T], F32), gv=([T, D], BF16), grT=([D, T], BF16), gaT=([32, T], F32),
        gatesT=([3 * D, T], BF16), dqT=([D, T], BF16), dkT=([D, T], BF16), dv=([T, D], BF16),
        mqT=([256, T], F32), mkvT=([128, T], F32), krT=([64, T], BF16),
        qnT=([D, T], BF16), qrT=([512, T], BF16), knT=([D, T], BF16), mv=([T, D], BF16),
        yaT=([D, T], BF16), ybT=([D, T], BF16), ycT=([D, T], BF16),
        xs1=([T, D], F32), xs2=([T, D], F32), out=([LT, D], F32),
    )
    return sp


HOST_INPUTS = ("xin", "cin", "w_mod", "b_mod", "w_in", "gla_w_a2", "diff_lam", "mla_w_uq", "mla_w_ukv", "w_branch",
               "w_out", "ln1_g", "ln1_b", "ffn_w_in", "ffn_w_out", "ln2_g", "ln2_b", "smallv", "cosT", "sinT",
               "perm", "ident", "scanmask", "blockmask")
FEATS = ("gqT", "gkT", "gv", "grT", "gaT", "gatesT", "dqT", "dkT", "dv", "mqT", "mkvT", "krT")


def stage_plan(LT):
    T = LT + CT
    plan = [dict(name="mod", r=["cin", "w_mod", "b_mod"], w=["modv"],
                 fn=lambda S, t: st_mod(S, t["cin"], t["w_mod"], t["b_mod"], t["modv"]))]
    for l in range(DEPTH):
        last = l == DEPTH - 1
        lam_init = 0.8 - 0.6 * math.exp(-0.3 * l)
        Tproc = LT if last else T
        xsrc = "xin" if l == 0 else "xs2"
        xdst = "out" if last else "xs2"

        def add(name, r, w, fn):
            plan.append(dict(name=f"L{l}.{name}", r=r, w=w, fn=fn))

        add("modulate1", [xsrc, "modv", "ident"], ["hT"],
            lambda S, t, l=l, xsrc=xsrc: st_modulate(S, t[xsrc], t["modv"][l], D, 0, t["hT"], t["ident"], T, LT))
        for pi in range(2):
            outs = [j for j in (FEATS[0:6] if pi == 0 else FEATS[6:12])]
            add(f"proj{pi}", ["hT", "w_in", "cosT", "sinT", "perm"], outs,
                lambda S, t, l=l, pi=pi: st_proj(S, t["hT"], 8, T, proj_jobs(t["w_in"][l], t)[pi],
                                                 t["cosT"], t["sinT"], t["perm"]))
        add("mla_up", ["mqT", "mkvT", "mla_w_uq", "mla_w_ukv", "smallv", "cosT", "sinT", "perm"],
            ["qnT", "qrT", "knT", "mv"],
            lambda S, t, l=l: st_mla_up(S, t["mqT"], t["mkvT"], t["mla_w_uq"][l], t["mla_w_ukv"][l], t["smallv"][l],
                                        t["qnT"], t["qrT"], t["knT"], t["mv"], t["cosT"], t["sinT"], t["perm"], T))
        add("gla", ["gqT", "gkT", "gv", "gaT", "grT", "gla_w_a2", "smallv", "ident", "scanmask", "blockmask"], ["yaT"],
            lambda S, t, l=l: st_gla(S, t["gqT"], t["gkT"], t["gv"], t["gaT"], t["grT"], t["gla_w_a2"][l],
                                     t["smallv"][l], t["yaT"], t["ident"], t["scanmask"], t["blockmask"], T, LT))
        add("attn", ["dqT", "dkT", "dv", "qnT", "qrT", "knT", "krT", "mv", "diff_lam", "smallv"], ["ybT", "ycT"],
            lambda S, t, l=l, lam_init=lam_init, last=last: st_attn(
                S, t["dqT"], t["dkT"], t["dv"], t["qnT"], t["qrT"], t["knT"], t["krT"], t["mv"], t["diff_lam"][l],
                t["smallv"][l], lam_init, t["ybT"], t["ycT"], T, LT, not last))
        add("merge", ["yaT", "ybT", "ycT", "gatesT", "w_branch", "w_out", "modv", "ln1_g", "ln1_b", xsrc], ["xs1"],
            lambda S, t, l=l, xsrc=xsrc, Tproc=Tproc: st_merge(
                S, [t["yaT"], t["ybT"], t["ycT"]], t["gatesT"], t["w_branch"][l], t["w_out"][l], t["modv"][l],
                t["ln1_g"][l:l + 1, :], t["ln1_b"][l:l + 1, :], t[xsrc], t["xs1"], T, LT, Tproc))
        add("modulate2", ["xs1", "modv", "ident"], ["h2T"],
            lambda S, t, l=l, Tproc=Tproc: st_modulate(S, t["xs1"], t["modv"][l], 4 * D, 3 * D, t["h2T"], t["ident"],
                                                       Tproc, LT))
        add("ffn", ["h2T", "ffn_w_in", "ffn_w_out", "modv", "ln2_g", "ln2_b", "xs1"], [xdst],
            lambda S, t, l=l, xdst=xdst, Tproc=Tproc: st_ffn(
                S, t["h2T"], t["ffn_w_in"][l], t["ffn_w_out"][l], t["modv"][l], t["ln2_g"][l:l + 1, :],
                t["ln2_b"][l:l + 1, :], t["xs1"], t[xdst], T, LT, Tproc))
    return plan


def default_groups(nstages):
    g = os.environ.get("K_GROUPS")
    if g:
        out = []
        for part in g.split(","):
            a, b = part.split("-") if "-" in part else (part, part)
            out.append(list(range(int(a), int(b) + 1)))
        return out
    return GROUPS(nstages)


def GROUPS(nstages):
    return [list(range(nstages))]


def build_launches(LT):
    specs = tensor_specs(LT)
    plan = stage_plan(LT)
    groups = default_groups(len(plan))
    launches = []
    for gi, g in enumerate(groups):
        later_reads = set()
        for g2 in groups[gi + 1:]:
            for si in g2:
                later_reads.update(plan[si]["r"])
        written, ext_in = set(), []
        for si in g:
            for n in plan[si]["r"]:
                if n not in written and n not in ext_in:
                    ext_in.append(n)
            written.update(plan[si]["w"])
        ext_out = [n for n in sorted(written) if n in later_reads or n == "out"]
        assert not (set(ext_in) & set(ext_out)), (ext_in, ext_out)
        nc = bass.Bass("TRN2", target_bir_lowering=False)
        t = {}
        names = list(ext_in) + [n for n in sorted(written) if n not in ext_in]
        for n in names:
            shape, dt = specs[n]
            kind = "ExternalInput" if n in ext_in else ("ExternalOutput" if n in ext_out else "Internal")
            t[n] = nc.dram_tensor(n, list(shape), dt, kind=kind).ap()
        for si in g:
            stage(nc, plan[si]["fn"], t)
        launches.append((nc, ext_in, ext_out))
    return launches


def host_inputs(inp, b, LT):
    T = LT + CT
    c = np.asarray(inp["c"][b], np.float32)
    cc = np.asarray(inp["c_ctx"], np.float32)
    cin = np.stack([c.reshape(8, P).T, cc.reshape(8, P).T], axis=-1).reshape(P, 16).astype(np.float32)
    xin = np.concatenate([np.asarray(inp["x"][b, :LT], np.float32), np.asarray(inp["ctx"][b], np.float32)], 0)
    return dict(xin=np.ascontiguousarray(xin), cin=cin)


_PROG = {}


def kernel(**inputs):
    LT = inputs["x"].shape[1]
    nb = inputs["x"].shape[0]
    T = LT + CT
    if LT not in _PROG:
        _PROG[LT] = build_launches(LT)
    launches = _PROG[LT]
    hc = host_consts(LT, T)
    gc = host_gla_consts()
    shared = dict(smallv=host_smallv(inputs), cosT=hc["cosT"], sinT=hc["sinT"], perm=hc["perm"], ident=hc["ident"],
                  scanmask=gc["scanmask"], blockmask=gc["blockmask"])
    for k in ("w_mod", "b_mod", "w_in", "gla_w_a2", "diff_lam", "mla_w_uq", "mla_w_ukv", "w_branch", "w_out",
              "ln1_g", "ln1_b", "ffn_w_in", "ffn_w_out", "ln2_g", "ln2_b"):
        shared[k] = np.ascontiguousarray(np.asarray(inputs[k], np.float32))
    percore = [host_inputs(inputs, b, LT) for b in range(nb)]
    for li, (nc, ext_in, ext_out) in enumerate(launches):
        if os.environ.get("K_VERBOSE"):
            print(f"[kernel] launch {li}: in={ext_in} out={ext_out}", flush=True)
        in_maps = [{n: (percore[b][n] if n in percore[b] else shared[n]) for n in ext_in} for b in range(nb)]
        res = run_bass_kernel_spmd(nc, in_maps, core_ids=list(range(nb)))
        for b in range(nb):
            for n in ext_out:
                percore[b][n] = res.results[b][n]
    return np.stack([np.asarray(percore[b]["out"], np.float32) for b in range(nb)], 0)
```

```python
from contextlib import ExitStack
import math
import numpy as np
import ml_dtypes
import concourse.bass as bass
import concourse.mybir as mybir
from concourse.bass_utils import run_bass_kernel_spmd

F32 = mybir.dt.float32
BF16 = mybir.dt.bfloat16
AF = mybir.ActivationFunctionType
ALU = mybir.AluOpType
AX = mybir.AxisListType
P = 128

D = 1024
DEPTH = 2
CT = 256
NIN = 9696
EPS = 1e-6
ALPHA = (2 * DEPTH) ** 0.25
FH = 2816
MLA_SCALE = (128 + 64) ** -0.5

ENGS = ("pe", "act", "dve", "pool", "sp")
SEM_LIM = 30000
_SEM_POOL = {}


class Buf:
    __slots__ = ("w", "r")

    def __init__(self):
        self.w = None
        self.r = []


class Op:
    __slots__ = ("eng", "fn", "deps", "dma", "flag", "sem", "target")

    def __init__(self, eng, fn, deps, dma):
        self.eng = eng
        self.fn = fn
        self.deps = deps
        self.dma = dma
        self.flag = False
        self.sem = None
        self.target = 0


class Sched:
    N_DMA_SEMS = 10
    _uid = 0

    def __init__(self, nc, es):
        self.nc = nc
        self.es = es
        self.ops = {e: [] for e in ENGS}

    def sbuf(self, name, shape, dt):
        Sched._uid += 1
        return self.es.enter_context(self.nc.sbuf_tensor(f"{name}_{Sched._uid}", list(shape), dt))

    def psum(self, name, shape, dt):
        Sched._uid += 1
        return self.es.enter_context(self.nc.psum_tensor(f"{name}_{Sched._uid}", list(shape), dt))

    def op(self, eng, fn, rd=(), wr=(), dma=False, deps=()):
        dl = [d for d in deps if d is not None]
        for b in rd:
            if b.w is not None:
                dl.append(b.w)
        for b in wr:
            if b.w is not None:
                dl.append(b.w)
            dl.extend(b.r)
        o = Op(eng, fn, None, dma)
        seen = set()
        dd = []
        for d in dl:
            if d is o or id(d) in seen:
                continue
            if (not d.dma) and (not dma) and d.eng == "pe" and eng == "pe":
                continue
            seen.add(id(d))
            dd.append(d)
            if not d.dma:
                d.flag = True
        o.deps = dd
        self.ops[eng].append(o)
        for b in rd:
            b.r.append(o)
        for b in wr:
            b.w = o
            b.r = []
        return o

    def dma(self, eng, out, in_, rd=(), wr=(), deps=(), **kw):
        return self.op(eng, lambda e: e.dma_start(out=out, in_=in_, **kw), rd, wr, dma=True, deps=deps)

    def emit(self):
        nc = self.nc
        es = self.es
        Sched._uid += 1
        u = Sched._uid
        dq = ("sp", "act", "pool")
        handles = []
        pool = _SEM_POOL.setdefault(id(nc), {})

        def new_sem(name):
            if name not in pool:
                pool[name] = nc.alloc_semaphore(name=name)
            h = pool[name]
            handles.append(h)
            return h

        dsem = {e: [new_sem(f"sd_{e}{i}") for i in range(self.N_DMA_SEMS)] for e in dq}
        duse = {e: [0] * self.N_DMA_SEMS for e in dq}
        esems = {}
        efinal = []
        for e in ENGS:
            c = 0
            k = 0
            cur = None
            for o in self.ops[e]:
                if o.dma:
                    slot = k % self.N_DMA_SEMS
                    k += 1
                    duse[e][slot] += 1
                    o.sem = dsem[e][slot]
                    o.target = 16 * duse[e][slot]
                elif o.flag:
                    if c % SEM_LIM == 0:
                        cur = new_sem(f"se_{e}{c // SEM_LIM}")
                        esems.setdefault(e, []).append(cur)
                    c += 1
                    o.sem = cur
                    o.target = (c - 1) % SEM_LIM + 1
            if c:
                efinal.append((cur, (c - 1) % SEM_LIM + 1))
        finals = []
        for e in dq:
            for slot in range(self.N_DMA_SEMS):
                if duse[e][slot]:
                    finals.append((dsem[e][slot], 16 * duse[e][slot]))

        def run(e, eng):
            waited = {}
            for o in self.ops[e]:
                needs = [(d.sem, d.target) for d in o.deps]
                if o.dma and o.target > 16:
                    needs.append((o.sem, o.target - 16))
                for sem, tgt in needs:
                    key = id(sem)
                    if waited.get(key, 0) >= tgt:
                        continue
                    waited[key] = tgt
                    eng.wait_ge(sem, tgt)
                ins = o.fn(eng)
                if o.dma:
                    ins.then_inc(o.sem, 16)
                elif o.flag:
                    ins.then_inc(o.sem, 1)
            for sem, tgt in finals + efinal:
                if waited.get(id(sem), 0) < tgt:
                    eng.wait_ge(sem, tgt)

        nc.all_engine_barrier()
        for h in handles:
            nc.gpsimd.sem_clear(h)
        nc.all_engine_barrier()
        self._handles = handles

        with nc.Block() as block:
            @block.tensor
            def _(eng):
                run("pe", eng)

            @block.scalar
            def _(eng):
                run("act", eng)

            @block.vector
            def _(eng):
                run("dve", eng)

            @block.gpsimd
            def _(eng):
                run("pool", eng)

            @block.sync
            def _(eng):
                run("sp", eng)


import os
_STAGE_CNT = [0]


def stage(nc, build, *a, **k):
    with ExitStack() as es:
        S = Sched(nc, es)
        build(S, *a, **k)
        S.emit()


class Ring:
    def __init__(self, S, name, shape, dt, n, psum=False):
        mk = S.psum if psum else S.sbuf
        self.t = [mk(f"{name}{i}", shape, dt) for i in range(n)]
        self.b = [Buf() for _ in range(n)]
        self.i = -1

    def next(self):
        self.i = (self.i + 1) % len(self.t)
        return self.t[self.i], self.b[self.i]


CAST_W = 2048


def cast_ring(S):
    return Ring(S, "cst", [P, CAST_W], F32, 2)


def load_cast(S, ring, dst, src, rows, ncol, wb, shape3=None):
    for c0 in range(0, ncol, CAST_W):
        cl = min(CAST_W, ncol - c0)
        stg, bs = ring.next()
        S.dma("sp", stg[:rows, 0:cl], src[:, c0:c0 + cl], wr=[bs])
        S.op("pool", lambda e, stg=stg, c0=c0, cl=cl: e.tensor_copy(dst[:, c0:c0 + cl], stg[:rows, 0:cl]),
             rd=[bs], wr=[wb])


def st_mod(S, cin, w_mod, b_mod, modv):
    cs = S.sbuf("cs", [P, 16], F32)
    ca = S.sbuf("ca", [P, 16], F32)
    bcs, bca = Buf(), Buf()
    S.dma("sp", cs[:], cin, wr=[bcs])
    S.op("act", lambda e: e.activation(out=ca[:], in_=cs[:], func=AF.Silu), rd=[bcs], wr=[bca])
    wr_ = Ring(S, "wm", [P, 8, 512], F32, 2)
    pr_ = Ring(S, "pm", [2, 512], F32, 2, psum=True)
    for l in range(DEPTH):
        row = S.sbuf("row", [2, 6144], F32)
        brow = Buf()
        S.dma("sp", row[0:1, :], b_mod[l:l + 1, :], wr=[brow])
        S.dma("sp", row[1:2, :], b_mod[l:l + 1, :], wr=[brow])
        for n in range(12):
            wt, wb = wr_.next()
            S.dma("sp", wt[:],
                  w_mod[l, :, n * 512:(n + 1) * 512].rearrange("(kc p) n -> p kc n", p=P), wr=[wb])
            ps, pb = pr_.next()
            for kc in range(8):
                S.op("pe", lambda e, ps=ps, wt=wt, kc=kc: e.matmul(
                    ps[:], ca[:, kc * 2:kc * 2 + 2], wt[:, kc, :], start=(kc == 0), stop=(kc == 7)),
                    rd=[bca, wb], wr=[pb])
            add1 = 1.0 if n in (2, 3, 8, 9) else 0.0
            sl = row[:, n * 512:(n + 1) * 512]
            S.op("dve", lambda e, sl=sl, ps=ps, add1=add1: e.scalar_tensor_tensor(
                out=sl, in0=ps[:], scalar=add1, in1=sl, op0=ALU.add, op1=ALU.add),
                rd=[pb], wr=[brow])
        S.dma("sp", modv[l], row[:], rd=[brow])


def ln_stats(S, xt, bx, st, bst):
    S.op("dve", lambda e: e.bn_stats(st[:, 0:6], xt[:, 0:512]), rd=[bx], wr=[bst])
    S.op("dve", lambda e: e.bn_stats(st[:, 6:12], xt[:, 512:1024]), rd=[bx], wr=[bst])
    S.op("dve", lambda e: e.bn_aggr(st[:, 12:14], st[:, 0:12]), rd=[bst], wr=[bst])
    S.op("dve", lambda e: e.tensor_scalar(st[:, 14:15], st[:, 13:14], EPS, None, ALU.add),
         rd=[bst], wr=[bst])
    S.op("act", lambda e: e.activation(out=st[:, 14:15], in_=st[:, 14:15], func=AF.Sqrt), rd=[bst], wr=[bst])
    S.op("dve", lambda e: e.reciprocal(st[:, 14:15], st[:, 14:15]), rd=[bst], wr=[bst])
    S.op("dve", lambda e: e.scalar_tensor_tensor(
        out=st[:, 15:16], in0=st[:, 12:13], scalar=-1.0, in1=st[:, 14:15], op0=ALU.mult, op1=ALU.mult),
        rd=[bst], wr=[bst])


def load_bc(S, name, src_row, eng="sp"):
    n = src_row.shape[-1]
    t = S.sbuf(name, [P, n], F32)
    b = Buf()
    S.dma(eng, t[:], src_row.partition_broadcast(P), wr=[b])
    return t, b


def st_modulate(S, xs, modv_l, sc_off, sh_off, hT, ident, T, LT):
    idt = S.sbuf("idt", [P, P], BF16)
    bid = Buf()
    S.dma("sp", idt[:], ident, wr=[bid])
    bc = {}
    for j in range(2):
        bc[j] = (load_bc(S, f"sc{j}", modv_l[j:j + 1, sc_off:sc_off + D]),
                 load_bc(S, f"sh{j}", modv_l[j:j + 1, sh_off:sh_off + D]))
    xr = Ring(S, "xt", [P, D], F32, 2)
    sr = Ring(S, "st", [P, 16], F32, 2)
    nr = Ring(S, "xn", [P, D], F32, 2)
    hr = Ring(S, "hb", [P, D], BF16, 2)
    tr = Ring(S, "tp", [P, 8, P], BF16, 2, psum=True)
    outr = Ring(S, "ho", [P, 8, 512], BF16, 2)
    hTv = hT.rearrange("(kc p) t -> p kc t", p=P)
    for t0 in range(0, T, 512):
        tl = min(512, T - t0)
        ot, ob = outr.next()
        for s in range(tl // P):
            i = t0 // P + s
            j = 0 if i * P < LT else 1
            (sct, scb), (sht, shb) = bc[j]
            xt, bx = xr.next()
            S.dma("sp", xt[:], xs[i * P:(i + 1) * P, :], wr=[bx])
            st, bst = sr.next()
            ln_stats(S, xt, bx, st, bst)
            xn, bn = nr.next()
            S.op("act", lambda e, xn=xn, xt=xt, st=st: e.activation(
                out=xn[:], in_=xt[:], func=AF.Identity, scale=st[:, 14:15], bias=st[:, 15:16]),
                rd=[bx, bst], wr=[bn])
            S.op("dve", lambda e, xn=xn, sct=sct: e.tensor_tensor(xn[:], xn[:], sct[:], ALU.mult),
                 rd=[bn, scb], wr=[bn])
            hb, bh = hr.next()
            S.op("pool", lambda e, hb=hb, xn=xn, sht=sht: e.tensor_tensor(hb[:], xn[:], sht[:], ALU.add),
                 rd=[bn, shb], wr=[bh])
            tp, btp = tr.next()
            for kc in range(8):
                S.op("pe", lambda e, tp=tp, hb=hb, kc=kc: e.transpose(
                    tp[:, kc, :], hb[:, kc * P:(kc + 1) * P], idt[:]), rd=[bh, bid], wr=[btp])
            S.op("act", lambda e, ot=ot, tp=tp, s=s: e.copy(ot[:, :, s * P:(s + 1) * P], tp[:]),
                 rd=[btp], wr=[ob])
        S.dma("sp", hTv[:, :, t0:t0 + tl], ot[:, :, 0:tl], rd=[ob])


def load_w_bf16(S, name, W, KC, n, ring):
    wt = S.sbuf(name, [P, KC, n], BF16)
    wb = Buf()
    Wv = W.rearrange("(kc p) n -> p kc n", p=P)
    for kc in range(KC):
        load_cast(S, ring, wt[:, kc, :], Wv[:, kc, :], P, n, wb)
    return wt, wb


def rope_evac(S, ps, pb, m, scale, xr, pr2, t1r, perm, bperm, cs, sn, bcs, dst_ap, bdst):
    xb, bxb = xr.next()
    S.op("act", lambda e: e.activation(out=xb[:m, :], in_=ps[:m, :], func=AF.Copy, scale=scale),
         rd=[pb], wr=[bxb])
    p2, bp2 = pr2.next()
    S.op("pe", lambda e: e.matmul(p2[:m, :], perm[:m, :m], xb[:m, :], start=True, stop=True),
         rd=[bxb, bperm], wr=[bp2])
    t1, bt1 = t1r.next()
    S.op("pool", lambda e: e.tensor_tensor(t1[:m, :], xb[:m, :], cs[:m, :], ALU.mult),
         rd=[bxb, bcs], wr=[bt1])
    t2, bt2 = t1r.next()
    S.op("dve", lambda e: e.tensor_tensor(t2[:m, :], p2[:m, :], sn[:m, :], ALU.mult),
         rd=[bp2, bcs], wr=[bt2])
    S.op("pool", lambda e: e.tensor_tensor(dst_ap, t1[:m, :], t2[:m, :], ALU.add),
         rd=[bt1, bt2], wr=[bdst])


def st_proj(S, srcT, KC, T, jobs, cosT, sinT, permd):
    perm = S.sbuf("perm", [P, P], BF16)
    bperm = Buf()
    S.dma("sp", perm[:], permd, wr=[bperm])
    srcv = srcT.rearrange("(kc p) t -> p kc t", p=P)
    if True:
        if True:
            S2 = S
            cring = cast_ring(S)
            for j in jobs:
                j["wt"], j["wb"] = load_w_bf16(S2, "w", j["W"], KC, j["n"], cring)
            sr = Ring(S2, "src", [P, KC, 512], BF16, 2)
            pr = Ring(S2, "ps", [P, 512], F32, 4, psum=True)
            pr2 = Ring(S2, "ps2", [P, 512], F32, 2, psum=True)
            o16 = Ring(S2, "o16", [P, 8, 512], BF16, 3)
            o32 = Ring(S2, "o32", [P, 4, 512], F32, 2)
            xr = Ring(S2, "xb", [P, 512], BF16, 2)
            t1r = Ring(S2, "t1", [P, 512], F32, 4)
            csr = Ring(S2, "cs", [P, 2, 512], F32, 2)
            has_rope = any(j["kind"] == "rope" for j in jobs)
            def tile(t0, tl):
                st, sb = sr.next()
                S.dma("sp", st[:, :, 0:tl], srcv[:, :, t0:t0 + tl], wr=[sb])
                cst = bcs = None
                if has_rope:
                    cst, bcs = csr.next()
                    S.dma("sp", cst[:, 0, 0:tl], cosT[:, t0:t0 + tl], wr=[bcs])
                    S.dma("sp", cst[:, 1, 0:tl], sinT[:, t0:t0 + tl], wr=[bcs])
                for j in jobs:
                    wt, wb, n = j["wt"], j["wb"], j["n"]
                    if j["kind"] == "tm":
                        for s in range(tl // P):
                            ot, ob = o16.next()
                            otv = ot[:].rearrange("p a b -> p (a b)")
                            for hf in range(n // 512):
                                ps, pb = pr.next()
                                for kc in range(KC):
                                    S.op("pe", lambda e, ps=ps, st=st, wt=wt, kc=kc, s=s, hf=hf: e.matmul(
                                        ps[:], st[:, kc, s * P:(s + 1) * P], wt[:, kc, hf * 512:(hf + 1) * 512],
                                        start=(kc == 0), stop=(kc == KC - 1)), rd=[sb, wb], wr=[pb])
                                if hf % 2 == 0:
                                    S.op("act", lambda e, otv=otv, ps=ps, hf=hf: e.copy(
                                        otv[:, hf * 512:(hf + 1) * 512], ps[:]), rd=[pb], wr=[ob])
                                else:
                                    S.op("dve", lambda e, otv=otv, ps=ps, hf=hf: e.tensor_copy(
                                        otv[:, hf * 512:(hf + 1) * 512], ps[:]), rd=[pb], wr=[ob])
                            S.dma("sp", j["dst"][t0 + s * P:t0 + (s + 1) * P, :], otv[:, 0:n], rd=[ob])
                        continue
                    nch = (n + P - 1) // P
                    f32out = j.get("dt") == "f32"
                    grp = 4 if f32out else 8
                    for c0 in range(0, nch, grp):
                        ot, ob = (o32 if f32out else o16).next()
                        cn = min(grp, nch - c0)
                        for ci in range(cn):
                            c = c0 + ci
                            m = min(P, n - c * P)
                            ps, pb = pr.next()
                            for kc in range(KC):
                                S.op("pe", lambda e, ps=ps, st=st, wt=wt, kc=kc, c=c, m=m: e.matmul(
                                    ps[:m, 0:tl], wt[:, kc, c * P:c * P + m], st[:, kc, 0:tl],
                                    start=(kc == 0), stop=(kc == KC - 1)), rd=[sb, wb], wr=[pb])
                            if j["kind"] == "rope":
                                rope_evac(S, ps[:, 0:tl], pb, m, j.get("scale", 1.0), _RingView(xr, tl, 2),
                                          _RingView(pr2, tl, 2), _RingView(t1r, tl, 2), perm, bperm,
                                          cst[:, 0, 0:tl], cst[:, 1, 0:tl], bcs, ot[:m, ci, 0:tl], ob)
                            else:
                                func = j.get("func") or AF.Copy
                                S.op("act", lambda e, ot=ot, ps=ps, ci=ci, m=m, func=func, sc=j.get("scale", 1.0):
                                     e.activation(out=ot[:m, ci, 0:tl], in_=ps[:m, 0:tl], func=func, scale=sc),
                                     rd=[pb], wr=[ob])
                        if n >= P:
                            dv = j["dst"].rearrange("(c p) t -> p c t", p=P)
                            S.dma("sp", dv[:, c0:c0 + cn, t0:t0 + tl], ot[:, 0:cn, 0:tl], rd=[ob])
                        else:
                            S.dma("sp", j["dst"][:, t0:t0 + tl], ot[:n, 0, 0:tl], rd=[ob])

            for t0 in range(0, T, 512):
                tile(t0, min(512, T - t0))


def host_consts(LT, T):
    bf = ml_dtypes.bfloat16
    t = np.arange(LT)
    pos_row = (t // 64).astype(np.float32)
    pos_col = (t % 64).astype(np.float32)
    inv = (np.float32(10000.0) ** (-np.arange(0, 32, 2, dtype=np.float32) / np.float32(32))).astype(np.float32)
    ang = np.concatenate([pos_row[:, None] * inv, pos_col[:, None] * inv], -1)
    cos = np.ones((T, 32), np.float32)
    sin = np.zeros((T, 32), np.float32)
    cos[:LT] = np.cos(ang)
    sin[:LT] = np.sin(ang)
    d = np.arange(P) % 32
    cosT = np.ascontiguousarray(cos[:, d].T)
    sinT = np.ascontiguousarray(sin[:, d].T)
    perm = np.zeros((P, P), np.float32)
    for m in range(P):
        if m % 64 < 32:
            perm[m + 32, m] = -1.0
        else:
            perm[m - 32, m] = 1.0
    return dict(cosT=cosT, sinT=sinT, perm=perm.astype(bf), ident=np.eye(P, dtype=bf))


def proj_jobs(w_in_l, d):
    def W(o, n):
        return w_in_l[:, o:o + n]
    p1 = [
        dict(kind="fm", W=W(0, 512), n=512, dst=d.get("gqT"), scale=128 ** -0.5, dt="f32"),
        dict(kind="fm", W=W(512, 512), n=512, dst=d.get("gkT"), dt="f32"),
        dict(kind="tm", W=W(1024, 1024), n=1024, dst=d.get("gv")),
        dict(kind="fm", W=W(2048, 1024), n=1024, dst=d.get("grT"), func=AF.Silu),
        dict(kind="fm", W=W(3072, 32), n=32, dst=d.get("gaT"), dt="f32"),
        dict(kind="fm", W=W(6624, 3072), n=3072, dst=d.get("gatesT"), func=AF.Sigmoid),
    ]
    p2 = [
        dict(kind="rope", W=W(3104, 1024), n=1024, dst=d.get("dqT"), scale=0.125),
        dict(kind="rope", W=W(4128, 1024), n=1024, dst=d.get("dkT")),
        dict(kind="tm", W=W(5152, 1024), n=1024, dst=d.get("dv")),
        dict(kind="fm", W=W(6176, 256), n=256, dst=d.get("mqT"), dt="f32"),
        dict(kind="fm", W=W(6432, 128), n=128, dst=d.get("mkvT"), dt="f32"),
        dict(kind="rope", W=W(6560, 64), n=64, dst=d.get("krT")),
    ]
    return [p1, p2]


def feat_scratch(nc, T, kind="Internal"):
    def mk(name, shape, dt):
        return nc.dram_tensor(name, shape, dt, kind=kind).ap()
    return dict(
        gqT=mk("gqT", [512, T], F32), gkT=mk("gkT", [512, T], F32), gv=mk("gv", [T, 1024], BF16),
        grT=mk("grT", [1024, T], BF16), gaT=mk("gaT", [32, T], F32), gatesT=mk("gatesT", [3072, T], BF16),
        dqT=mk("dqT", [1024, T], BF16), dkT=mk("dkT", [1024, T], BF16), dv=mk("dv", [T, 1024], BF16),
        mqT=mk("mqT", [256, T], F32), mkvT=mk("mkvT", [128, T], F32), krT=mk("krT", [64, T], BF16),
    )


def rms_fm(S, xt, bx, nchunk, onesq, bones, gcol0, sv, bsv, sqr, pms, rvr, nq, bnq):
    sq, bsq = sqr.next()
    S.op("act", lambda e: e.activation(out=sq[:, 0:nchunk, :], in_=xt[:, 0:nchunk, :], func=AF.Square),
         rd=[bx], wr=[bsq])
    ps, pb = pms.next()
    for c in range(nchunk):
        S.op("pe", lambda e, c=c: e.matmul(ps[:], onesq[:], sq[:, c, :], start=(c == 0), stop=(c == nchunk - 1)),
             rd=[bsq, bones], wr=[pb])
    rv, brv = rvr.next()
    S.op("act", lambda e: e.activation(out=rv[:], in_=ps[:], func=AF.Ln, bias=EPS, scale=1.0 / (nchunk * P)),
         rd=[pb], wr=[brv])
    S.op("act", lambda e: e.activation(out=rv[:], in_=rv[:], func=AF.Exp, scale=-0.5), rd=[brv], wr=[brv])
    for c in range(nchunk):
        S.op("dve", lambda e, c=c: e.scalar_tensor_tensor(
            out=nq[:, c, :], in0=xt[:, c, :], scalar=sv[:, gcol0 + c:gcol0 + c + 1], in1=rv[:],
            op0=ALU.mult, op1=ALU.mult), rd=[bx, brv, bsv], wr=[bnq])


def st_mla_up(S, mqT, mkvT, w_uq, w_ukv, smallv_l, qnT, qrT, knT, mv, cosT, sinT, permd, T):
    perm = S.sbuf("perm", [P, P], BF16)
    bperm = Buf()
    S.dma("sp", perm[:], permd, wr=[bperm])
    sv = S.sbuf("sv", [P, 16], F32)
    bsv = Buf()
    S.dma("sp", sv[:], smallv_l, wr=[bsv])
    ones = S.sbuf("ones", [P, P], F32)
    bones = Buf()
    S.op("dve", lambda e: e.memset(ones[:], 1.0), wr=[bones])
    wqn = S.sbuf("wqn", [P, 2, 8, 128], BF16)
    wqr = S.sbuf("wqr", [P, 2, 8, 64], BF16)
    wk = S.sbuf("wk", [P, 8, 128], BF16)
    wv = S.sbuf("wv", [P, 8, 128], BF16)
    bw = Buf()
    uqv = w_uq.rearrange("(kc p) (h j) -> p kc h j", p=P, j=192)
    cring = cast_ring(S)
    uqk = w_uq.rearrange("(kc p) n -> p kc n", p=P)
    for kc in range(2):
        stg, bs = cring.next()
        S.dma("sp", stg[:, 0:1536], uqk[:, kc, :], wr=[bs])
        sv3 = stg[:, 0:1536].rearrange("p (h j) -> p h j", j=192)
        S.op("pool", lambda e, kc=kc, sv3=sv3: e.tensor_copy(wqn[:, kc, :, :], sv3[:, :, 0:128]), rd=[bs], wr=[bw])
        S.op("pool", lambda e, kc=kc, sv3=sv3: e.tensor_copy(wqr[:, kc, :, :], sv3[:, :, 128:192]), rd=[bs], wr=[bw])
    stg, bs = cring.next()
    S.dma("sp", stg[:, 0:2048], w_ukv, wr=[bs])
    kv3 = stg[:, 0:2048].rearrange("p (h j) -> p h j", j=256)
    S.op("pool", lambda e, kv3=kv3: e.tensor_copy(wk[:], kv3[:, :, 0:128]), rd=[bs], wr=[bw])
    S.op("pool", lambda e, kv3=kv3: e.tensor_copy(wv[:], kv3[:, :, 128:256]), rd=[bs], wr=[bw])
    wvf = wv[:].rearrange("p h j -> p (h j)")
    mqv = mqT.rearrange("(c p) t -> p c t", p=P)
    xqr = Ring(S, "xq", [P, 2, 512], F32, 2)
    xkr = Ring(S, "xk", [P, 1, 512], F32, 2)
    sqr = Ring(S, "sq", [P, 2, 512], F32, 2)
    pms = Ring(S, "pms", [P, 512], F32, 1, psum=True)
    rvr = Ring(S, "rv", [P, 512], F32, 2)
    nqr = Ring(S, "nq", [P, 2, 512], BF16, 2)
    nkr = Ring(S, "nk", [P, 1, 512], BF16, 2)
    pr = Ring(S, "ps", [P, 512], F32, 4, psum=True)
    pr2 = Ring(S, "ps2", [P, 512], F32, 2, psum=True)
    o16 = Ring(S, "o16", [P, 8, 512], BF16, 3)
    xr = Ring(S, "xb", [P, 512], BF16, 2)
    t1r = Ring(S, "t1", [P, 512], F32, 4)
    csr = Ring(S, "cs", [P, 2, 512], F32, 2)
    qnv = qnT.rearrange("(c p) t -> p c t", p=P)
    qrv = qrT.rearrange("(c p) t -> p c t", p=P)
    knv = knT.rearrange("(c p) t -> p c t", p=P)
    def tile(t0, tl):
        W = slice(0, tl)
        xq, bxq = xqr.next()
        S.dma("sp", xq[:, :, W], mqv[:, :, t0:t0 + tl], wr=[bxq])
        xk, bxk = xkr.next()
        S.dma("sp", xk[:, 0, W], mkvT[:, t0:t0 + tl], wr=[bxk])
        cst, bcs = csr.next()
        S.dma("sp", cst[:, 0, W], cosT[:, t0:t0 + tl], wr=[bcs])
        S.dma("sp", cst[:, 1, W], sinT[:, t0:t0 + tl], wr=[bcs])
        sqv, pmv, rvv = _RingView(sqr, tl, 3), _RingView(pms, tl, 2), _RingView(rvr, tl, 2)
        nq, bnq = nqr.next()
        rms_fm(S, xq[:, :, W], bxq, 2, ones, bones, 0, sv, bsv, sqv, pmv, rvv, nq[:, :, W], bnq)
        nk, bnk = nkr.next()
        rms_fm(S, xk[:, :, W], bxk, 1, ones, bones, 2, sv, bsv, sqv, pmv, rvv, nk[:, :, W], bnk)
        ot, ob = o16.next()
        for h in range(8):
            ps, pb = pr.next()
            for kc in range(2):
                S.op("pe", lambda e, ps=ps, kc=kc, h=h: e.matmul(
                    ps[:, W], wqn[:, kc, h, :], nq[:, kc, W], start=(kc == 0), stop=(kc == 1)),
                    rd=[bw, bnq], wr=[pb])
            S.op("act", lambda e, ot=ot, ps=ps, h=h: e.activation(
                out=ot[:, h, W], in_=ps[:, W], func=AF.Copy, scale=MLA_SCALE), rd=[pb], wr=[ob])
        S.dma("sp", qnv[:, :, t0:t0 + tl], ot[:, :, W], rd=[ob])
        ot, ob = o16.next()
        for c in range(4):
            ps, pb = pr.next()
            for kc in range(2):
                S.op("pe", lambda e, ps=ps, kc=kc, c=c: e.matmul(
                    ps[:, W], wqr[:, kc, 2 * c:2 * c + 2, :].rearrange("p h j -> p (h j)"), nq[:, kc, W],
                    start=(kc == 0), stop=(kc == 1)), rd=[bw, bnq], wr=[pb])
            rope_evac(S, ps[:, W], pb, P, MLA_SCALE, _RingView(xr, tl, 2), _RingView(pr2, tl, 2),
                      _RingView(t1r, tl, 2), perm, bperm, cst[:, 0, W], cst[:, 1, W], bcs, ot[:, c, W], ob)
        S.dma("sp", qrv[:, :, t0:t0 + tl], ot[:, 0:4, W], rd=[ob])
        ot, ob = o16.next()
        for h in range(8):
            ps, pb = pr.next()
            S.op("pe", lambda e, ps=ps, h=h: e.matmul(
                ps[:, W], wk[:, h, :], nk[:, 0, W], start=True, stop=True), rd=[bw, bnk], wr=[pb])
            S.op("act", lambda e, ot=ot, ps=ps, h=h: e.copy(ot[:, h, W], ps[:, W]), rd=[pb], wr=[ob])
        S.dma("sp", knv[:, :, t0:t0 + tl], ot[:, :, W], rd=[ob])
        for s in range(tl // P):
            ot, ob = o16.next()
            otv = ot[:].rearrange("p a b -> p (a b)")
            for hf in range(2):
                ps, pb = pr.next()
                S.op("pe", lambda e, ps=ps, s=s, hf=hf: e.matmul(
                    ps[:], nk[:, 0, s * P:(s + 1) * P], wvf[:, hf * 512:(hf + 1) * 512], start=True, stop=True),
                    rd=[bw, bnk], wr=[pb])
                S.op("dve", lambda e, otv=otv, ps=ps, hf=hf: e.tensor_copy(
                    otv[:, hf * 512:(hf + 1) * 512], ps[:]), rd=[pb], wr=[ob])
            S.dma("sp", mv[t0 + s * P:t0 + (s + 1) * P, :], otv[:, 0:1024], rd=[ob])

    for t0 in range(0, T, 512):
        tile(t0, min(512, T - t0))


def host_smallv(inp):
    sv = np.zeros((DEPTH, P, 16), np.float32)
    for l in range(DEPTH):
        sv[l, :, 0:2] = inp["mla_q_norm_g"][l].reshape(2, P).T
        sv[l, :, 2] = inp["mla_kv_norm_g"][l]
        sv[l, :, 3:11] = inp["gla_b_a"][l].reshape(8, P).T
        sv[l, :, 11:13] = inp["gla_norm_g"][l].reshape(2, P).T
        sv[l, :, 13] = inp["diff_norm_g"][l]
    return sv


GSEG = 768


def st_gla(S, gqT, gkT, gv, gaT, grT, w_a2, smallv_l, yaT, identd, scanmaskd, blockmaskd, T, LT):
    NTL = T // P
    NCH = T // 64
    idt = S.sbuf("idt", [P, P], BF16)
    bid = Buf()
    S.dma("sp", idt[:], identd, wr=[bid])
    sv = S.sbuf("sv", [P, 16], F32)
    nb = S.sbuf("nb", [P, 8], F32)
    bsv = Buf()
    S.dma("sp", sv[:], smallv_l, wr=[bsv])
    S.op("dve", lambda e: e.tensor_scalar(nb[:], sv[:, 3:11], -1.0, None, ALU.mult), rd=[bsv], wr=[bsv])
    ones = S.sbuf("ones", [P, P], F32)
    bones = Buf()
    S.op("dve", lambda e: e.memset(ones[:], 1.0), wr=[bones])
    smask = S.sbuf("smask", [P, GSEG], F32)
    bmask = S.sbuf("bmask", [P, 2, P], F32)
    wa2 = S.sbuf("wa2", [16, 2, 512], F32)
    bcst = Buf()
    S.dma("sp", smask[:], scanmaskd, wr=[bcst])
    S.dma("sp", bmask[:], blockmaskd.rearrange("d p q -> p d q"), wr=[bcst])
    S.dma("sp", wa2[:], w_a2.rearrange("d r e -> r d e"), wr=[bcst])

    qt = [S.sbuf(f"qt{d}", [P, T], BF16) for d in range(2)]
    kt = [S.sbuf(f"kt{d}", [P, T], BF16) for d in range(2)]
    khtm = [S.sbuf(f"kh{d}", [P, NTL, P], BF16) for d in range(2)]
    dec = [S.sbuf(f"dec{d}", [P, NCH], F32) for d in range(2)]
    acc = S.sbuf("acc", [P, 2, T], BF16)
    Sst = [S.sbuf(f"Sst{d}", [P, 256], F32) for d in range(2)]
    Sbf = [S.sbuf(f"Sbf{d}", [P, 256], BF16) for d in range(2)]

    qs = Ring(S, "qs", [P, GSEG], F32, 1)
    ks = Ring(S, "ks", [P, GSEG], F32, 1)
    gas = Ring(S, "gas", [16, 2, GSEG], F32, 1)
    tg = Ring(S, "tg", [P, GSEG], F32, 2)
    tb = Ring(S, "tb", [P, GSEG], F32, 2)
    tx = Ring(S, "tx", [P, GSEG], F32, 3)
    tkh = Ring(S, "tkh", [P, GSEG], BF16, 2)
    pz = Ring(S, "pz", [P, 512], F32, 1, psum=True)
    ptp = Ring(S, "ptp", [P, 1024], BF16, 1, psum=True)
    pA = Ring(S, "pA", [P, 512], F32, 1, psum=True)
    pS = Ring(S, "pS", [P, 512], F32, 1, psum=True)
    pod = [[S.psum(f"po{d}{ec}", [P, 512], F32) for ec in range(2)] for d in range(2)]
    bpod = [Buf(), Buf()]
    Asb = Ring(S, "Asb", [P, P], BF16, 3)
    vtr = Ring(S, "vt", [P, 256], BF16, 4)
    sqr = Ring(S, "sq", [P, 2, 512], F32, 1)
    rvr = Ring(S, "rv", [P, 512], F32, 1)
    nqr = Ring(S, "nq", [P, 2, 512], BF16, 2)
    grr = Ring(S, "grt", [P, 2, 512], BF16, 2)
    gvv = gv.rearrange("(n p) c -> p n c", p=P)
    grv = grT.rearrange("(c p) t -> p c t", p=P)
    yav = yaT.rearrange("(c p) t -> p c t", p=P)

    tilesF = list(range(LT // P, NTL)) + list(range(0, LT // P))
    tilesB = list(range(NTL - 1, LT // P - 1, -1)) + list(range(LT // P - 1, -1, -1))

    bq = [Buf(), Buf()]
    bkh = [Buf(), Buf()]
    bdec = [Buf(), Buf()]
    bacc = [Buf() for _ in range(NTL)]
    bS = [Buf(), Buf()]
    bSb = [Buf(), Buf()]
    ball = Buf()
    def seg(h, s0, sl):
        if True:
            nch = sl // 64
            q_, bq_ = qs.next()
            k_, bk_ = ks.next()
            ga_, bga_ = gas.next()
            S.dma("sp", q_[:, 0:sl], gqT[h * P:(h + 1) * P, s0:s0 + sl], wr=[bq_])
            S.dma("sp", k_[:, 0:sl], gkT[h * P:(h + 1) * P, s0:s0 + sl], wr=[bk_])
            S.dma("sp", ga_[:, 0, 0:sl], gaT[0:16, s0:s0 + sl], wr=[bga_])
            S.dma("sp", ga_[:, 1, 0:sl], gaT[16:32, s0:s0 + sl], wr=[bga_])
            for d in range(2):
                g_, bg_ = tg.next()
                for c0 in range(0, sl, 512):
                    cl = min(512, sl - c0)
                    ps, pb = pz.next()
                    S.op("pe", lambda e, ps=ps, d=d, ga_=ga_, c0=c0, cl=cl: e.matmul(
                        ps[:, 0:cl], wa2[:, d, h * P:(h + 1) * P], ga_[:, d, c0:c0 + cl], start=True, stop=True),
                        rd=[bcst, bga_], wr=[pb])
                    S.op("act", lambda e, ps=ps, g_=g_, d=d, c0=c0, cl=cl: e.activation(
                        out=g_[:, c0:c0 + cl], in_=ps[:, 0:cl], func=AF.Exp, scale=-1.0,
                        bias=nb[:, d * 4 + h:d * 4 + h + 1]), rd=[pb, bsv], wr=[bg_])
                S.op("act", lambda e, g_=g_: e.activation(out=g_[:, 0:sl], in_=g_[:, 0:sl], func=AF.Ln, bias=1.0),
                     rd=[bg_], wr=[bg_])
                S.op("dve", lambda e, g_=g_: e.tensor_scalar(g_[:, 0:sl], g_[:, 0:sl], -1.0 / 16.0, None, ALU.mult),
                     rd=[bg_], wr=[bg_])
                b_, bb_ = tb.next()
                S.op("dve", lambda e, b_=b_, g_=g_: e.tensor_tensor_scan(
                    b_[:, 0:sl], smask[:, 0:sl], g_[:, 0:sl], 0.0, ALU.mult, ALU.add),
                    rd=[bg_, bcst], wr=[bb_])
                b3 = b_[:, 0:sl].rearrange("p (n c) -> p n c", c=64)
                lastbc = b3[:, :, 63:64].to_broadcast([P, nch, 64])
                x1, bx1 = tx.next()
                x13 = x1[:, 0:sl].rearrange("p (n c) -> p n c", c=64)
                if d == 1:
                    S.op("dve", lambda e, x13=x13, lastbc=lastbc, b3=b3: e.tensor_tensor(
                        x13, lastbc, b3, ALU.subtract), rd=[bb_], wr=[bx1])
                    S.op("dve", lambda e, b_=b_, x1=x1, g_=g_: e.tensor_tensor(
                        b_[:, 0:sl], x1[:, 0:sl], g_[:, 0:sl], ALU.add), rd=[bx1, bg_], wr=[bb_])
                    edge = b3[:, :, 0:1].to_broadcast([P, nch, 64])
                    ecol = 0
                else:
                    edge = lastbc
                    ecol = 63
                S.op("act", lambda e, x1=x1, b_=b_: e.activation(out=x1[:, 0:sl], in_=b_[:, 0:sl], func=AF.Exp),
                     rd=[bb_], wr=[bx1])
                S.op("dve", lambda e, d=d, x13=x13, ecol=ecol, s0=s0, nch=nch: e.tensor_copy(
                    dec[d][:, s0 // 64:s0 // 64 + nch].rearrange("p (n o) -> p n o", o=1),
                    x13[:, :, ecol:ecol + 1]), rd=[bx1], wr=[bdec[d]])
                S.op("dve", lambda e, d=d, q_=q_, x1=x1, s0=s0: e.tensor_tensor(
                    qt[d][:, s0:s0 + sl], q_[:, 0:sl], x1[:, 0:sl], ALU.mult), rd=[bq_, bx1], wr=[bq[d]])
                x2, bx2 = tx.next()
                S.op("act", lambda e, x2=x2, b_=b_: e.activation(
                    out=x2[:, 0:sl], in_=b_[:, 0:sl], func=AF.Exp, scale=-1.0), rd=[bb_], wr=[bx2])
                S.op("pool", lambda e, d=d, k_=k_, x2=x2, s0=s0: e.tensor_tensor(
                    kt[d][:, s0:s0 + sl], k_[:, 0:sl], x2[:, 0:sl], ALU.mult), rd=[bk_, bx2], wr=[bq[d]])
                x3, bx3 = tx.next()
                x33 = x3[:, 0:sl].rearrange("p (n c) -> p n c", c=64)
                S.op("dve", lambda e, x33=x33, edge=edge, b3=b3: e.tensor_tensor(x33, edge, b3, ALU.subtract),
                     rd=[bb_], wr=[bx3])
                S.op("act", lambda e, x3=x3: e.activation(out=x3[:, 0:sl], in_=x3[:, 0:sl], func=AF.Exp),
                     rd=[bx3], wr=[bx3])
                kh_, bkh_ = tkh.next()
                S.op("pool", lambda e, kh_=kh_, k_=k_, x3=x3: e.tensor_tensor(
                    kh_[:, 0:sl], k_[:, 0:sl], x3[:, 0:sl], ALU.mult), rd=[bk_, bx3], wr=[bkh_])
                for ti in range(sl // P):
                    tp, btp = ptp.next()
                    S.op("pe", lambda e, tp=tp, kh_=kh_, ti=ti: e.transpose(
                        tp[:, 0:P], kh_[:, ti * P:(ti + 1) * P], idt[:]), rd=[bkh_, bid], wr=[btp])
                    S.op("act", lambda e, tp=tp, d=d, ti=ti, s0=s0: e.copy(
                        khtm[d][:, s0 // P + ti, :], tp[:, 0:P]), rd=[btp], wr=[bkh[d]])
    def head(h):
        for s0 in range(0, T, GSEG):
            seg(h, s0, min(GSEG, T - s0))
        for d in range(2):
            S.op("dve", lambda e, d=d: e.memset(Sst[d][:], 0.0), wr=[bS[d]])
            S.op("pool", lambda e, d=d: e.memset(Sbf[d][:], 0.0), wr=[bSb[d]])
        S.op("pool", lambda e: e.memset(acc[:], 0.0), wr=bacc + [ball])
        for step in range(NTL):
            for d in range(2):
                n = (tilesF if d == 0 else tilesB)[step]
                tsl = slice(n * P, (n + 1) * P)
                psa, bpa = pA.next()
                S.op("pe", lambda e, psa=psa, d=d, tsl=tsl: e.matmul(
                    psa[:, 0:P], kt[d][:, tsl], qt[d][:, tsl], start=True, stop=True), rd=[bq[d]], wr=[bpa])
                a_, ba_ = Asb.next()
                S.op("dve", lambda e, a_=a_, psa=psa, d=d: e.tensor_tensor(a_[:], psa[:, 0:P], bmask[:, d, :], ALU.mult),
                     rd=[bpa, bcst], wr=[ba_])
                vt, bvt = vtr.next()
                S.dma("sp", vt[:], gvv[:, n, h * 256:(h + 1) * 256], wr=[bvt])
                pot, bpo = pod[d], bpod[d]
                for ec in range(2):
                    S.op("pe", lambda e, pot=pot, vt=vt, a_=a_, ec=ec: e.matmul(
                        pot[ec][:, 0:P], vt[:, ec * P:(ec + 1) * P], a_[:], start=True, stop=False),
                        rd=[bvt, ba_], wr=[bpo])
                order = (0, 1) if d == 0 else (1, 0)
                for oi, hh in enumerate(order):
                    c = 2 * n + hh
                    csl = slice(c * 64, (c + 1) * 64)
                    for ec in range(2):
                        S.op("pe", lambda e, pot=pot, d=d, ec=ec, hh=hh, csl=csl, oi=oi: e.matmul(
                            pot[ec][:, hh * 64:(hh + 1) * 64], Sbf[d][:, ec * P:(ec + 1) * P], qt[d][:, csl],
                            start=False, stop=(oi == 1)), rd=[bSb[d], bq[d]], wr=[bpo])
                    pst, bps = pS.next()
                    S.op("pe", lambda e, pst=pst, d=d, n=n, hh=hh, vt=vt: e.matmul(
                        pst[:, 0:256], khtm[d][hh * 64:(hh + 1) * 64, n, :], vt[hh * 64:(hh + 1) * 64, :],
                        start=True, stop=True), rd=[bkh[d], bvt], wr=[bps])
                    S.op("dve", lambda e, pst=pst, d=d, c=c: e.scalar_tensor_tensor(
                        out=Sst[d][:], in0=Sst[d][:], scalar=dec[d][:, c:c + 1], in1=pst[:, 0:256],
                        op0=ALU.mult, op1=ALU.add), rd=[bps, bdec[d]], wr=[bS[d]])
                    S.op("dve", lambda e, d=d: e.tensor_copy(Sbf[d][:], Sst[d][:]), rd=[bS[d]], wr=[bSb[d]])
                for ec in range(2):
                    S.op("dve", lambda e, pot=pot, tsl=tsl, ec=ec: e.tensor_tensor(
                        acc[:, ec, tsl], pot[ec][:, 0:P], acc[:, ec, tsl], ALU.add), rd=[bpo], wr=[bacc[n]])
        S.op("dve", lambda e: e.engine_nop(), rd=bacc, wr=[ball])
        def otile(t0, tl):
            gt, bgt = grr.next()
            S.dma("sp", gt[:, :, 0:tl], grv[:, 2 * h:2 * h + 2, t0:t0 + tl], wr=[bgt])
            nq, bnq = nqr.next()
            rms_fm(S, acc[:, :, t0:t0 + tl], ball, 2, ones, bones, 11, sv, bsv, _RingView(sqr, tl, 3),
                   _RingView(pz, tl, 2), _RingView(rvr, tl, 2), nq[:, :, 0:tl], bnq)
            S.op("pool", lambda e: e.tensor_tensor(nq[:, :, 0:tl], nq[:, :, 0:tl], gt[:, :, 0:tl], ALU.mult),
                 rd=[bnq, bgt], wr=[bnq])
            S.dma("sp", yav[:, 2 * h:2 * h + 2, t0:t0 + tl], nq[:, :, 0:tl], rd=[bnq])

        for t0 in range(0, T, 512):
            otile(t0, min(512, T - t0))

    for h in range(4):
        head(h)


def host_gla_consts():
    sm = np.ones((P, GSEG), np.float32)
    sm[:, ::64] = 0.0
    bm = np.zeros((2, P, P), np.float32)
    for j in range(P):
        for i in range(P):
            if j // 64 == i // 64:
                bm[0, j, i] = 1.0 if j <= i else 0.0
                bm[1, j, i] = 1.0 if j > i else 0.0
    return dict(scanmask=sm, blockmask=bm)


def st_attn(S, dqT, dkT, dv, qnT, qrT, knT, krT, mv, diff_lam_l, smallv_l, lam_init, ybT, ycT, T, LT, ctx_q):
    NT = T // P
    sv = S.sbuf("sv", [P, 16], F32)
    bsv = Buf()
    S.dma("sp", sv[:], smallv_l, wr=[bsv])
    ones = S.sbuf("ones", [P, P], F32)
    onesb = S.sbuf("onesb", [P, P], BF16)
    bones = Buf()
    S.op("dve", lambda e: e.memset(ones[:], 1.0), wr=[bones])
    S.op("dve", lambda e: e.memset(onesb[:], 1.0), wr=[bones])
    dl = S.sbuf("dl", [P, 4, 64], F32)
    lm = S.sbuf("lm", [P, 8], F32)
    blm = Buf()
    S.dma("sp", dl[:].rearrange("p a b -> p (a b)"),
          diff_lam_l.rearrange("a b -> (a b)").rearrange("(o n) -> o n", o=1).partition_broadcast(P), wr=[blm])
    pr_ = S.sbuf("prd", [P, 2, 64], F32)
    S.op("dve", lambda e: e.tensor_tensor(pr_[:, 0, :], dl[:, 0, :], dl[:, 1, :], ALU.mult), rd=[blm], wr=[blm])
    S.op("dve", lambda e: e.tensor_tensor(pr_[:, 1, :], dl[:, 2, :], dl[:, 3, :], ALU.mult), rd=[blm], wr=[blm])
    S.op("dve", lambda e: e.reduce_sum(lm[:, 0:2], pr_[:], AX.X), rd=[blm], wr=[blm])
    S.op("act", lambda e: e.activation(out=lm[:, 2:4], in_=lm[:, 0:2], func=AF.Exp), rd=[blm], wr=[blm])
    S.op("dve", lambda e: e.tensor_tensor(lm[:, 4:5], lm[:, 3:4], lm[:, 2:3], ALU.subtract), rd=[blm], wr=[blm])
    S.op("dve", lambda e: e.tensor_scalar(lm[:, 4:5], lm[:, 4:5], -lam_init, None, ALU.add), rd=[blm], wr=[blm])
    S.op("dve", lambda e: e.tensor_scalar(sv[:, 14:15], sv[:, 13:14], 1.0 - lam_init, None, ALU.mult),
         rd=[bsv], wr=[bsv])

    kT = Ring(S, "kT", [P, T], BF16, 2)
    qT = Ring(S, "qT", [P, T], BF16, 2)
    k2 = Ring(S, "k2", [P, T], BF16, 1)
    q2 = Ring(S, "q2", [P, T], BF16, 2)
    tmpr = Ring(S, "ptsum", [P, 1024], BF16, 2)
    vv = Ring(S, "vv", [P, NT, P], BF16, 2)
    ptr = Ring(S, "pt", [P, 1024], BF16, 4)
    psr = Ring(S, "pss", [P, 1024], F32, 3, psum=True)
    accr = Ring(S, "acc", [P, 1024], F32, 2)
    pacc = {i: S.psum(f"pacc{i}", [P, 512], F32) for i in (0, 2)}
    bpacc = {i: Buf() for i in (0, 2)}
    rr = Ring(S, "rr", [P, 512], F32, 2)
    tt = Ring(S, "tt", [P, 1, 512], F32, 3)
    sqr = Ring(S, "sq", [P, 1, 512], F32, 1)
    rvr = Ring(S, "rv", [P, 512], F32, 1)
    outr = Ring(S, "ob", [P, 1, 512], BF16, 3)
    dvv = dv.rearrange("(n p) c -> p n c", p=P)
    mvv = mv.rearrange("(n p) c -> p n c", p=P)

    qtiles = [(t0, 512, 0, NT) for t0 in range(0, LT, 512)]
    if ctx_q:
        qtiles.append((LT, T - LT, LT // P, NT))

    k2t, bk2 = k2.next()
    S.dma("sp", k2t[0:64, :], krT, wr=[bk2])
    S.dma("sp", k2t[64:128, :], krT, wr=[bk2])

    def run_head(h, kind):
        kt_, bkt = kT.next()
        qt_, bqt = qT.next()
        vt_, bvt = vv.next()
        if kind == "diff":
            S.dma("sp", kt_[:], dkT[h * P:(h + 1) * P, :], wr=[bkt])
            S.dma("sp", qt_[:], dqT[h * P:(h + 1) * P, :], wr=[bqt])
            for n0 in range(0, NT, 8):
                n1 = min(NT, n0 + 8)
                S.dma("sp", vt_[:, n0:n1, :], dvv[:, n0:n1, h * P:(h + 1) * P], wr=[bvt])
            nm = 2
        else:
            S.dma("sp", kt_[:], knT[h * P:(h + 1) * P, :], wr=[bkt])
            S.dma("sp", qt_[:], qnT[h * P:(h + 1) * P, :], wr=[bqt])
            for n0 in range(0, NT, 8):
                n1 = min(NT, n0 + 8)
                S.dma("sp", vt_[:, n0:n1, :], mvv[:, n0:n1, h * P:(h + 1) * P], wr=[bvt])
            q2t, bq2 = q2.next()
            S.dma("sp", q2t[0:64, :], qrT[h * 64:(h + 1) * 64, :], wr=[bq2])
            S.dma("sp", q2t[64:128, :], qrT[h * 64:(h + 1) * 64, :], wr=[bq2])
            nm = 1
        def qtile(t0, tl, kb0, kb1):
            qs_ = slice(t0, t0 + tl)
            if kind == "diff":
                units = [((kb, 0), (kb, 1)) for kb in range(kb0, kb1)]
            else:
                units = [((kb, 0), (kb + 1, 0)) for kb in range(kb0, kb1, 2)]
            LA = 3
            pend = {}
            held = [None]
            ac, bac = accr.next()
            ac3 = ac[:].rearrange("p (a b) -> p a b", b=512)[:, :, 0:tl]

            def emit_score(u):
                ps, pb = psr.next()
                if kind == "diff":
                    for hf, (kb, m) in enumerate(units[u]):
                        ks_ = slice(kb * P, (kb + 1) * P)
                        o = ps[:, hf * 512:hf * 512 + tl]
                        S.op("pe", lambda e, o=o, m=m, ks_=ks_: e.matmul(
                            o, kt_[m * 64:(m + 1) * 64, ks_], qt_[m * 64:(m + 1) * 64, qs_],
                            start=True, stop=True), rd=[bkt, bqt], wr=[pb])
                else:
                    for hf, (kb, m) in enumerate(units[u]):
                        ks_ = slice(kb * P, (kb + 1) * P)
                        o = ps[:, hf * 512:hf * 512 + tl]
                        S.op("pe", lambda e, o=o, ks_=ks_: e.matmul(
                            o, kt_[:, ks_], qt_[:, qs_], start=True, stop=False), rd=[bkt, bqt], wr=[pb])
                    for hf, (kb, m) in enumerate(units[u]):
                        ks_ = slice(kb * P, (kb + 1) * P)
                        o = ps[:, hf * 512:hf * 512 + tl]
                        rs = slice(hf * 64, (hf + 1) * 64)
                        S.op("pe", lambda e, o=o, ks_=ks_, rs=rs: e.matmul(
                            o, k2t[rs, ks_], q2t[rs, qs_], start=False, stop=True), rd=[bk2, bq2], wr=[pb])
                pend[u] = (ps, pb)

            def emit_rest(u):
                ps, pb = pend.pop(u)
                pt, bpt = ptr.next()
                ps3 = ps[:].rearrange("p (a b) -> p a b", b=512)[:, :, 0:tl]
                pt3 = pt[:].rearrange("p (a b) -> p a b", b=512)[:, :, 0:tl]
                S.op("act", lambda e: e.activation(out=pt3, in_=ps3, func=AF.Exp), rd=[pb], wr=[bpt])
                nu = len(units)
                if u % 2 == 0 and u + 1 < nu:
                    held[0] = (pt3, bpt)
                else:
                    if u % 2 == 1:
                        p0, bp0 = held[0]
                        tm, btm = tmpr.next()
                        tm3 = tm[:].rearrange("p (a b) -> p a b", b=512)[:, :, 0:tl]
                        S.op("dve", lambda e: e.tensor_tensor(tm3, p0, pt3, ALU.add), rd=[bp0, bpt], wr=[btm])
                        src, bsrc = tm3, btm
                    else:
                        src, bsrc = pt3, bpt
                    if u <= 1:
                        S.op("dve", lambda e: e.tensor_copy(ac3, src), rd=[bsrc], wr=[bac])
                    else:
                        S.op("dve", lambda e: e.tensor_tensor(ac3, ac3, src, ALU.add), rd=[bsrc, bac], wr=[bac])
                for hf, (kb, m) in enumerate(units[u]):
                    a = 2 * m if kind == "diff" else 2 * hf
                    first, last = (u == 0), (u == nu - 1)
                    S.op("pe", lambda e, hf=hf, kb=kb, a=a, first=first, last=last: e.matmul(
                        pacc[a][:, 0:tl], vt_[:, kb, :], pt[:, hf * 512:hf * 512 + tl], start=first, stop=last),
                        rd=[bvt, bpt], wr=[bpacc[a]])

            for u in range(min(LA, len(units))):
                emit_score(u)
            for u in range(len(units)):
                emit_rest(u)
                if u + LA < len(units):
                    emit_score(u + LA)
            pz, bpz = psr.next()
            if kind == "diff":
                for m in range(2):
                    S.op("pe", lambda e, m=m: e.matmul(pz[:, m * 512:m * 512 + tl], ones[:], ac[:, m * 512:m * 512 + tl],
                                                       start=True, stop=True), rd=[bones, bac], wr=[bpz])
            else:
                for hf in range(2):
                    S.op("pe", lambda e, hf=hf: e.matmul(pz[:, 0:tl], ones[:], ac[:, hf * 512:hf * 512 + tl],
                                                         start=(hf == 0), stop=(hf == 1)), rd=[bones, bac], wr=[bpz])
            ob, bob = outr.next()
            r0, br0 = rr.next()
            S.op("dve", lambda e, r0=r0: e.reciprocal(r0[:, 0:tl], pz[:, 0:tl]), rd=[bpz], wr=[br0])
            if kind == "diff":
                r1, br1 = rr.next()
                S.op("dve", lambda e, r1=r1: e.reciprocal(r1[:, 0:tl], pz[:, 512:512 + tl]), rd=[bpz], wr=[br1])
                ta, bta = tt.next()
                S.op("dve", lambda e, ta=ta, r0=r0: e.tensor_tensor(ta[:, 0, 0:tl], pacc[0][:, 0:tl], r0[:, 0:tl], ALU.mult),
                     rd=[bpacc[0], br0], wr=[bta])
                tb_, btb = tt.next()
                S.op("dve", lambda e, tb_=tb_, r1=r1: e.tensor_tensor(tb_[:, 0, 0:tl], pacc[2][:, 0:tl], r1[:, 0:tl], ALU.mult),
                     rd=[bpacc[2], br1], wr=[btb])
                S.op("dve", lambda e, ta=ta, tb_=tb_: e.scalar_tensor_tensor(
                    out=ta[:, 0, 0:tl], in0=tb_[:, 0, 0:tl], scalar=lm[:, 4:5], in1=ta[:, 0, 0:tl],
                    op0=ALU.mult, op1=ALU.add), rd=[btb, blm], wr=[bta])
                rms_fm(S, ta[:, :, 0:tl], bta, 1, ones, bones, 14, sv, bsv, sqr_v(sqr, tl), psr_v(psr, tl), rvr_v(rvr, tl),
                       ob[:, :, 0:tl], bob)
                S.dma("sp", ybT[h * P:(h + 1) * P, qs_], ob[:, 0, 0:tl], rd=[bob])
            else:
                ta, bta = tt.next()
                S.op("dve", lambda e, ta=ta, r0=r0: e.tensor_tensor(ta[:, 0, 0:tl], pacc[0][:, 0:tl], r0[:, 0:tl], ALU.mult),
                     rd=[bpacc[0], br0], wr=[bta])
                tb_, btb = tt.next()
                S.op("dve", lambda e, tb_=tb_, r0=r0: e.tensor_tensor(tb_[:, 0, 0:tl], pacc[2][:, 0:tl], r0[:, 0:tl], ALU.mult),
                     rd=[bpacc[2], br0], wr=[btb])
                S.op("pool", lambda e, ob=ob, ta=ta, tb_=tb_: e.tensor_tensor(ob[:, 0, 0:tl], ta[:, 0, 0:tl], tb_[:, 0, 0:tl], ALU.add),
                     rd=[bta, btb], wr=[bob])
                S.dma("sp", ycT[h * P:(h + 1) * P, qs_], ob[:, 0, 0:tl], rd=[bob])

        for qt4 in qtiles:
            qtile(*qt4)

    for h in range(8):
        run_head(h, "diff")
    for h in range(8):
        run_head(h, "mla")


class _RingView:
    def __init__(self, ring, tl, nd):
        self.ring, self.tl, self.nd = ring, tl, nd

    def next(self):
        t, b = self.ring.next()
        if self.nd == 3:
            return t[:, :, 0:self.tl], b
        return t[:, 0:self.tl], b


def sqr_v(r, tl):
    return _RingView(r, tl, 3)


def psr_v(r, tl):
    return _RingView(r, tl, 2)


def rvr_v(r, tl):
    return _RingView(r, tl, 2)


def resid_ln(S, ps2, bps2, xs_rows, out_rows, gbc, bgbc, lng, blng, lnb, blnb, R):
    xt, bx = R["x"].next()
    S.dma("sp", xt[:], xs_rows, wr=[bx])
    u, bu = R["u"].next()
    for hf in range(2):
        S.op("dve", lambda e, hf=hf: e.tensor_tensor(
            u[:, hf * 512:(hf + 1) * 512], ps2[hf][:], gbc[:, hf * 512:(hf + 1) * 512], ALU.mult),
            rd=[bps2[hf], bgbc], wr=[bu])
    S.op("dve", lambda e: e.scalar_tensor_tensor(out=u[:], in0=xt[:], scalar=ALPHA, in1=u[:],
                                                  op0=ALU.mult, op1=ALU.add), rd=[bx, bu], wr=[bu])
    st, bst = R["st"].next()
    ln_stats(S, u, bu, st, bst)
    xn, bn = R["xn"].next()
    S.op("act", lambda e: e.activation(out=xn[:], in_=u[:], func=AF.Identity, scale=st[:, 14:15], bias=st[:, 15:16]),
         rd=[bu, bst], wr=[bn])
    S.op("dve", lambda e: e.tensor_tensor(xn[:], xn[:], lng[:], ALU.mult), rd=[bn, blng], wr=[bn])
    S.op("pool", lambda e: e.tensor_tensor(xn[:], xn[:], lnb[:], ALU.add), rd=[bn, blnb], wr=[bn])
    S.dma("sp", out_rows, xn[:], rd=[bn])


def resid_rings(S):
    return dict(x=Ring(S, "rx", [P, D], F32, 1), u=Ring(S, "ru", [P, D], F32, 1),
                st=Ring(S, "rst", [P, 16], F32, 2), xn=Ring(S, "rxn", [P, D], F32, 1))


def st_merge(S, yT3, gatesT, w_branch, w_out, modv_l, ln_g, ln_b, xs, xo, T, LT, Tproc):
    wb = S.sbuf("wb", [P, 3, 8, D], BF16)
    wo = S.sbuf("wo", [P, 8, D], BF16)
    bw = Buf()
    cring = cast_ring(S)
    for i in range(3):
        wv = w_branch[i].rearrange("(kc p) n -> p kc n", p=P)
        for kc in range(8):
            load_cast(S, cring, wb[:, i, kc, :], wv[:, kc, :], P, D, bw)
    wv = w_out.rearrange("(kc p) n -> p kc n", p=P)
    for kc in range(8):
        load_cast(S, cring, wo[:, kc, :], wv[:, kc, :], P, D, bw)
    gb = [load_bc(S, f"g1{j}", modv_l[j:j + 1, 2 * D:3 * D]) for j in range(2)]
    lng, blng = load_bc(S, "lng", ln_g)
    lnb, blnb = load_bc(S, "lnb", ln_b)
    yr = Ring(S, "yt", [P, 8, 512], BF16, 2)
    gr_ = Ring(S, "gt", [P, 24, 512], BF16, 1)
    yacc = S.sbuf("yacc", [P, 8, 512], F32)
    byacc = Buf()
    ybf = S.sbuf("ybf", [P, 8, 512], BF16)
    bybf = Buf()
    tmp = Ring(S, "tmp", [P, 512], F32, 2)
    pr = Ring(S, "ps", [P, 512], F32, 4, psum=True)
    pr2 = Ring(S, "ps2", [P, 512], F32, 4, psum=True)
    RR = resid_rings(S)
    gv_ = gatesT.rearrange("(c p) t -> p c t", p=P)
    def tile(t0, tl):
        W = slice(0, tl)
        gt, bgt = gr_.next()
        S.dma("sp", gt[:, :, W], gv_[:, :, t0:t0 + tl], wr=[bgt])
        for i in range(3):
            yt, byt = yr.next()
            S.dma("sp", yt[:, :, W], yT3[i].rearrange("(c p) t -> p c t", p=P)[:, :, t0:t0 + tl], wr=[byt])
            for n in range(8):
                ps, pb = pr.next()
                for kc in range(8):
                    S.op("pe", lambda e, ps=ps, i=i, kc=kc, n=n, yt=yt: e.matmul(
                        ps[:, W], wb[:, i, kc, n * P:(n + 1) * P], yt[:, kc, W], start=(kc == 0), stop=(kc == 7)),
                        rd=[bw, byt], wr=[pb])
                if i == 0:
                    S.op("dve", lambda e, ps=ps, n=n: e.tensor_tensor(
                        yacc[:, n, W], ps[:, W], gt[:, n, W], ALU.mult), rd=[pb, bgt], wr=[byacc])
                else:
                    tm, btm = tmp.next()
                    S.op("dve", lambda e, ps=ps, n=n, tm=tm, i=i: e.tensor_tensor(
                        tm[:, W], ps[:, W], gt[:, i * 8 + n, W], ALU.mult), rd=[pb, bgt], wr=[btm])
                    if i == 1:
                        S.op("pool", lambda e, n=n, tm=tm: e.tensor_tensor(
                            yacc[:, n, W], yacc[:, n, W], tm[:, W], ALU.add), rd=[btm, byacc], wr=[byacc])
                    else:
                        S.op("pool", lambda e, n=n, tm=tm: e.tensor_tensor(
                            ybf[:, n, W], yacc[:, n, W], tm[:, W], ALU.add), rd=[btm, byacc], wr=[bybf])
        for s in range(tl // P):
            r0 = t0 + s * P
            ps2 = []
            bps2 = []
            for hf in range(2):
                ps, pb = pr2.next()
                for kc in range(8):
                    S.op("pe", lambda e, ps=ps, kc=kc, s=s, hf=hf: e.matmul(
                        ps[:], ybf[:, kc, s * P:(s + 1) * P], wo[:, kc, hf * 512:(hf + 1) * 512],
                        start=(kc == 0), stop=(kc == 7)), rd=[bw, bybf], wr=[pb])
                ps2.append(ps)
                bps2.append(pb)
            j = 0 if r0 < LT else 1
            resid_ln(S, ps2, bps2, xs[r0:r0 + P, :], xo[r0:r0 + P, :], gb[j][0], gb[j][1], lng, blng, lnb, blnb, RR)

    for t0 in range(0, Tproc, 512):
        tile(t0, min(512, Tproc - t0))


def st_ffn(S, h2T, w1d, w2d, modv_l, ln_g, ln_b, xs, xo, T, LT, Tproc):
    NJ = FH // P
    w1 = S.sbuf("w1", [P, 8, 2 * FH], BF16)
    w2 = S.sbuf("w2", [P, NJ, D], BF16)
    bw = Buf()
    cring = Ring(S, "cst", [P, CAST_W], F32, 1)
    wv = w1d.rearrange("(kc p) n -> p kc n", p=P)
    for kc in range(8):
        load_cast(S, cring, w1[:, kc, :], wv[:, kc, :], P, 2 * FH, bw)
    wv = w2d.rearrange("(j p) n -> p j n", p=P)
    for j in range(NJ):
        load_cast(S, cring, w2[:, j, :], wv[:, j, :], P, D, bw)
    gb = [load_bc(S, f"g2{j}", modv_l[j:j + 1, 5 * D:6 * D]) for j in range(2)]
    lng, blng = load_bc(S, "lng", ln_g)
    lnb, blnb = load_bc(S, "lnb", ln_b)
    hr = Ring(S, "h2", [P, 8, 512], BF16, 1)
    hmid = S.sbuf("hmid", [P, NJ, 512], BF16)
    bhm = Buf()
    ar = Ring(S, "ar", [P, 512], BF16, 2)
    pr = Ring(S, "ps", [P, 512], F32, 4, psum=True)
    pr2 = Ring(S, "ps2", [P, 512], F32, 4, psum=True)
    RR = resid_rings(S)
    hv = h2T.rearrange("(kc p) t -> p kc t", p=P)
    def tile(t0, tl):
        W = slice(0, tl)
        ht, bht = hr.next()
        S.dma("sp", ht[:, :, W], hv[:, :, t0:t0 + tl], wr=[bht])
        for j in range(NJ):
            pg, bpg = pr.next()
            pu, bpu = pr.next()
            for kc in range(8):
                S.op("pe", lambda e, pg=pg, kc=kc, j=j: e.matmul(
                    pg[:, W], w1[:, kc, j * P:(j + 1) * P], ht[:, kc, W], start=(kc == 0), stop=(kc == 7)),
                    rd=[bw, bht], wr=[bpg])
            for kc in range(8):
                S.op("pe", lambda e, pu=pu, kc=kc, j=j: e.matmul(
                    pu[:, W], w1[:, kc, FH + j * P:FH + (j + 1) * P], ht[:, kc, W], start=(kc == 0), stop=(kc == 7)),
                    rd=[bw, bht], wr=[bpu])
            a_, ba_ = ar.next()
            S.op("act", lambda e, a_=a_, pg=pg: e.activation(out=a_[:, W], in_=pg[:, W], func=AF.Silu),
                 rd=[bpg], wr=[ba_])
            S.op("dve", lambda e, a_=a_, pu=pu, j=j: e.tensor_tensor(hmid[:, j, W], pu[:, W], a_[:, W], ALU.mult),
                 rd=[bpu, ba_], wr=[bhm])
        for s in range(tl // P):
            r0 = t0 + s * P
            ps2 = []
            bps2 = []
            for hf in range(2):
                ps, pb = pr2.next()
                for j in range(NJ):
                    S.op("pe", lambda e, ps=ps, j=j, s=s, hf=hf: e.matmul(
                        ps[:], hmid[:, j, s * P:(s + 1) * P], w2[:, j, hf * 512:(hf + 1) * 512],
                        start=(j == 0), stop=(j == NJ - 1)), rd=[bw, bhm], wr=[pb])
                ps2.append(ps)
                bps2.append(pb)
            jj = 0 if r0 < LT else 1
            dst = xo[r0:r0 + P, :]
            resid_ln(S, ps2, bps2, xs[r0:r0 + P, :], dst, gb[jj][0], gb[jj][1], lng, blng, lnb, blnb, RR)

    for t0 in range(0, Tproc, 512):
        tile(t0, min(512, Tproc - t0))


def tensor_specs(LT):
    T = LT + CT
    sp = dict(
        xin=([T, D], F32), cin=([P, 16], F32), w_mod=([DEPTH, D, 6 * D], F32), b_mod=([DEPTH, 6 * D], F32),
        w_in=([DEPTH, D, NIN], F32), gla_w_a2=([DEPTH, 2, 16, 512], F32), diff_lam=([DEPTH, 4, 64], F32),
        mla_w_uq=([DEPTH, 256, 1536], F32), mla_w_ukv=([DEPTH, 128, 2048], F32),
        w_branch=([DEPTH, 3, D, D], F32), w_out=([DEPTH, D, D], F32), ln1_g=([DEPTH, D], F32),
        ln1_b=([DEPTH, D], F32), ffn_w_in=([DEPTH, D, 2 * FH], F32), ffn_w_out=([DEPTH, FH, D], F32),
        ln2_g=([DEPTH, D], F32), ln2_b=([DEPTH, D], F32), smallv=([DEPTH, P, 16], F32),
        cosT=([P, T], F32), sinT=([P, T], F32), perm=([P, P], BF16), ident=([P, P], BF16),
        scanmask=([P, GSEG], F32), blockmask=([2, P, P], F32),
        modv=([DEPTH, 2, 6 * D], F32), hT=([D, T], BF16), h2T=([D, T], BF16),
        gqT=([512, T], F32), gkT=([512, T], F32), gv=([T, D], BF16), grT=([D, T], BF16), gaT=([32, T], F32),
        gatesT=([3 * D, T], BF16), dqT=([D, T], BF16), dkT=([D, T], BF16), dv=([T, D], BF16),
        mqT=([256, T], F32), mkvT=([128, T], F32), krT=([64, T], BF16),
        qnT=([D, T], BF16), qrT=([512, T], BF16), knT=([D, T], BF16), mv=([T, D], BF16),
        yaT=([D, T], BF16), ybT=([D, T], BF16), ycT=([D, T], BF16),
        xs1=([T, D], F32), xs2=([T, D], F32), out=([LT, D], F32),
    )
    return sp


HOST_INPUTS = ("xin", "cin", "w_mod", "b_mod", "w_in", "gla_w_a2", "diff_lam", "mla_w_uq", "mla_w_ukv", "w_branch",
               "w_out", "ln1_g", "ln1_b", "ffn_w_in", "ffn_w_out", "ln2_g", "ln2_b", "smallv", "cosT", "sinT",
               "perm", "ident", "scanmask", "blockmask")
FEATS = ("gqT", "gkT", "gv", "grT", "gaT", "gatesT", "dqT", "dkT", "dv", "mqT", "mkvT", "krT")


def stage_plan(LT):
    T = LT + CT
    plan = [dict(name="mod", r=["cin", "w_mod", "b_mod"], w=["modv"],
                 fn=lambda S, t: st_mod(S, t["cin"], t["w_mod"], t["b_mod"], t["modv"]))]
    for l in range(DEPTH):
        last = l == DEPTH - 1
        lam_init = 0.8 - 0.6 * math.exp(-0.3 * l)
        Tproc = LT if last else T
        xsrc = "xin" if l == 0 else "xs2"
        xdst = "out" if last else "xs2"

        def add(name, r, w, fn):
            plan.append(dict(name=f"L{l}.{name}", r=r, w=w, fn=fn))

        add("modulate1", [xsrc, "modv", "ident"], ["hT"],
            lambda S, t, l=l, xsrc=xsrc: st_modulate(S, t[xsrc], t["modv"][l], D, 0, t["hT"], t["ident"], T, LT))
        for pi in range(2):
            outs = [j for j in (FEATS[0:6] if pi == 0 else FEATS[6:12])]
            add(f"proj{pi}", ["hT", "w_in", "cosT", "sinT", "perm"], outs,
                lambda S, t, l=l, pi=pi: st_proj(S, t["hT"], 8, T, proj_jobs(t["w_in"][l], t)[pi],
                                                 t["cosT"], t["sinT"], t["perm"]))
        add("mla_up", ["mqT", "mkvT", "mla_w_uq", "mla_w_ukv", "smallv", "cosT", "sinT", "perm"],
            ["qnT", "qrT", "knT", "mv"],
            lambda S, t, l=l: st_mla_up(S, t["mqT"], t["mkvT"], t["mla_w_uq"][l], t["mla_w_ukv"][l], t["smallv"][l],
                                        t["qnT"], t["qrT"], t["knT"], t["mv"], t["cosT"], t["sinT"], t["perm"], T))
        add("gla", ["gqT", "gkT", "gv", "gaT", "grT", "gla_w_a2", "smallv", "ident", "scanmask", "blockmask"], ["yaT"],
            lambda S, t, l=l: st_gla(S, t["gqT"], t["gkT"], t["gv"], t["gaT"], t["grT"], t["gla_w_a2"][l],
                                     t["smallv"][l], t["yaT"], t["ident"], t["scanmask"], t["blockmask"], T, LT))
        add("attn", ["dqT", "dkT", "dv", "qnT", "qrT", "knT", "krT", "mv", "diff_lam", "smallv"], ["ybT", "ycT"],
            lambda S, t, l=l, lam_init=lam_init, last=last: st_attn(
                S, t["dqT"], t["dkT"], t["dv"], t["qnT"], t["qrT"], t["knT"], t["krT"], t["mv"], t["diff_lam"][l],
                t["smallv"][l], lam_init, t["ybT"], t["ycT"], T, LT, not last))
        add("merge", ["yaT", "ybT", "ycT", "gatesT", "w_branch", "w_out", "modv", "ln1_g", "ln1_b", xsrc], ["xs1"],
            lambda S, t, l=l, xsrc=xsrc, Tproc=Tproc: st_merge(
                S, [t["yaT"], t["ybT"], t["ycT"]], t["gatesT"], t["w_branch"][l], t["w_out"][l], t["modv"][l],
                t["ln1_g"][l:l + 1, :], t["ln1_b"][l:l + 1, :], t[xsrc], t["xs1"], T, LT, Tproc))
        add("modulate2", ["xs1", "modv", "ident"], ["h2T"],
            lambda S, t, l=l, Tproc=Tproc: st_modulate(S, t["xs1"], t["modv"][l], 4 * D, 3 * D, t["h2T"], t["ident"],
                                                       Tproc, LT))
        add("ffn", ["h2T", "ffn_w_in", "ffn_w_out", "modv", "ln2_g", "ln2_b", "xs1"], [xdst],
            lambda S, t, l=l, xdst=xdst, Tproc=Tproc: st_ffn(
                S, t["h2T"], t["ffn_w_in"][l], t["ffn_w_out"][l], t["modv"][l], t["ln2_g"][l:l + 1, :],
                t["ln2_b"][l:l + 1, :], t["xs1"], t[xdst], T, LT, Tproc))
    return plan


def default_groups(nstages):
    g = os.environ.get("K_GROUPS")
    if g:
        out = []
        for part in g.split(","):
            a, b = part.split("-") if "-" in part else (part, part)
            out.append(list(range(int(a), int(b) + 1)))
        return out
    return GROUPS(nstages)


def GROUPS(nstages):
    return [list(range(nstages))]


def build_launches(LT):
    specs = tensor_specs(LT)
    plan = stage_plan(LT)
    groups = default_groups(len(plan))
    launches = []
    for gi, g in enumerate(groups):
        later_reads = set()
        for g2 in groups[gi + 1:]:
            for si in g2:
                later_reads.update(plan[si]["r"])
        written, ext_in = set(), []
        for si in g:
            for n in plan[si]["r"]:
                if n not in written and n not in ext_in:
                    ext_in.append(n)
            written.update(plan[si]["w"])
        ext_out = [n for n in sorted(written) if n in later_reads or n == "out"]
        assert not (set(ext_in) & set(ext_out)), (ext_in, ext_out)
        nc = bass.Bass("TRN2", target_bir_lowering=False)
        t = {}
        names = list(ext_in) + [n for n in sorted(written) if n not in ext_in]
        for n in names:
            shape, dt = specs[n]
            kind = "ExternalInput" if n in ext_in else ("ExternalOutput" if n in ext_out else "Internal")
            t[n] = nc.dram_tensor(n, list(shape), dt, kind=kind).ap()
        for si in g:
            stage(nc, plan[si]["fn"], t)
        launches.append((nc, ext_in, ext_out))
    return launches


def host_inputs(inp, b, LT):
    T = LT + CT
    c = np.asarray(inp["c"][b], np.float32)
    cc = np.asarray(inp["c_ctx"], np.float32)
    cin = np.stack([c.reshape(8, P).T, cc.reshape(8, P).T], axis=-1).reshape(P, 16).astype(np.float32)
    xin = np.concatenate([np.asarray(inp["x"][b, :LT], np.float32), np.asarray(inp["ctx"][b], np.float32)], 0)
    return dict(xin=np.ascontiguousarray(xin), cin=cin)


_PROG = {}


def kernel(**inputs):
    LT = inputs["x"].shape[1]
    nb = inputs["x"].shape[0]
    T = LT + CT
    if LT not in _PROG:
        _PROG[LT] = build_launches(LT)
    launches = _PROG[LT]
    hc = host_consts(LT, T)
    gc = host_gla_consts()
    shared = dict(smallv=host_smallv(inputs), cosT=hc["cosT"], sinT=hc["sinT"], perm=hc["perm"], ident=hc["ident"],
                  scanmask=gc["scanmask"], blockmask=gc["blockmask"])
    for k in ("w_mod", "b_mod", "w_in", "gla_w_a2", "diff_lam", "mla_w_uq", "mla_w_ukv", "w_branch", "w_out",
              "ln1_g", "ln1_b", "ffn_w_in", "ffn_w_out", "ln2_g", "ln2_b"):
        shared[k] = np.ascontiguousarray(np.asarray(inputs[k], np.float32))
    percore = [host_inputs(inputs, b, LT) for b in range(nb)]
    for li, (nc, ext_in, ext_out) in enumerate(launches):
        if os.environ.get("K_VERBOSE"):
            print(f"[kernel] launch {li}: in={ext_in} out={ext_out}", flush=True)
        in_maps = [{n: (percore[b][n] if n in percore[b] else shared[n]) for n in ext_in} for b in range(nb)]
        res = run_bass_kernel_spmd(nc, in_maps, core_ids=list(range(nb)))
        for b in range(nb):
            for n in ext_out:
                percore[b][n] = res.results[b][n]
    return np.stack([np.asarray(percore[b]["out"], np.float32) for b in range(nb)], 0)
```
